# Optimizing a Trainium2 kernel written in Bass

```python
import math
import jax, jax.numpy as jnp
from jax import lax
import numpy as np

D_MODEL = 1024
BATCH = 32
SEQ = 2048
DEPTH = 4
DEC_BATCH = 16
DEC_SEQ = 2048
PAST_LEN = 128

GRID_W = 64
N_EVEN = (DEPTH + 1) // 2
N_ODD = DEPTH // 2
NA_HEADS = 8
NA_HEAD_DIM = 64
NA_WIDTH = NA_HEADS * NA_HEAD_DIM
NA_WIN_R = 8
NA_WIN_C = 16
SGU_WIDTH = 512
SGU_GROUPS = 4
SGU_GROUP_DIM = SGU_WIDTH // SGU_GROUPS
SGU_CHUNK = 128
MLA_HEADS = 8
MLA_Q_LORA = 256
MLA_KV_LORA = 128
MLA_NOPE = 64
MLA_ROPE = 32
MLA_V = 64
MLA_WIDTH = MLA_HEADS * MLA_V
MLA_Q_BLOCK = 128
ROPE_THETA = 10000.0
CONV_WIDTH = 512
CONV_K = 31
D_FF = 2816
FFN_CONV_K = 3
LN_EPS = 1e-5
RMS_EPS = 1e-6
ALPHA = (2 * DEPTH) ** 0.25
BETA = (8 * DEPTH) ** -0.25
EVEN_IN = 3 * NA_WIDTH + 2 * SGU_WIDTH
EVEN_MIX = NA_WIDTH + SGU_WIDTH
ODD_IN = MLA_Q_LORA + MLA_KV_LORA + MLA_ROPE + 2 * CONV_WIDTH
ODD_MIX = MLA_WIDTH + CONV_WIDTH

kernel_name = 'hybrid_natten_sgu_mla_conformer_encoder'


def layernorm(x, g, b):
    xf = x.astype(jnp.float32)
    mu = jnp.mean(xf, axis=-1, keepdims=True)
    var = jnp.mean(jnp.square(xf - mu), axis=-1, keepdims=True)
    y = (xf - mu) * lax.rsqrt(var + LN_EPS)
    return (y * g.astype(jnp.float32) + b.astype(jnp.float32)).astype(x.dtype)


def rmsnorm(x, g):
    xf = x.astype(jnp.float32)
    y = xf * lax.rsqrt(jnp.mean(jnp.square(xf), axis=-1, keepdims=True) + RMS_EPS)
    return (y * g.astype(jnp.float32)).astype(x.dtype)


def dwconv(x, w, b):
    K, C = w.shape
    y = lax.conv_general_dilated(x, w[:, None, :].astype(x.dtype), window_strides=(1,),
                                 padding=[(K // 2, K // 2)],
                                 dimension_numbers=('NWC', 'WIO', 'NWC'),
                                 feature_group_count=C)
    return y + b.astype(x.dtype)


def rope_tables(S, dtype):
    pos = jnp.arange(S, dtype=jnp.float32)
    inv = ROPE_THETA ** (-jnp.arange(0, MLA_ROPE, 2, dtype=jnp.float32) / MLA_ROPE)
    ang = pos[:, None] * inv[None, :]
    return jnp.cos(ang).astype(dtype), jnp.sin(ang).astype(dtype)


def apply_rope(x, cos, sin):
    x1, x2 = jnp.split(x, 2, axis=-1)
    return jnp.concatenate([x1 * cos - x2 * sin, x2 * cos + x1 * sin], axis=-1)


def neighbourhood_attention(q, k, v, rpb):
    B, S, H, dh = q.shape
    rows = S // GRID_W
    wr = min(NA_WIN_R, rows)
    r = jnp.arange(rows)
    r0 = jnp.clip(r - wr // 2, 0, rows - wr)
    dr = r0[:, None] + jnp.arange(wr)[None, :] - r[:, None]
    c = jnp.arange(GRID_W)
    c0 = jnp.clip(c - NA_WIN_C // 2, 0, GRID_W - NA_WIN_C)
    col_mask = (c[None, :] >= c0[:, None]) & (c[None, :] < c0[:, None] + NA_WIN_C)
    dc_idx = jnp.clip(c[None, :] - c[:, None], -(NA_WIN_C - 1), NA_WIN_C - 1) + NA_WIN_C - 1
    qg = q.reshape(B, rows, GRID_W, H, dh).transpose(1, 0, 2, 3, 4)
    kg = k.reshape(B, rows, GRID_W, H, dh)
    vg = v.reshape(B, rows, GRID_W, H, dh)
    scale = dh ** -0.5

    def row_block(args):
        q_r, r0_r, dr_r = args
        k_r = lax.dynamic_slice_in_dim(kg, r0_r, wr, axis=1)
        v_r = lax.dynamic_slice_in_dim(vg, r0_r, wr, axis=1)
        s = jnp.einsum('bqhd,bikhd->bhqik', q_r, k_r).astype(jnp.float32) * scale
        bias = rpb[:, (dr_r + NA_WIN_R - 1)[None, :, None], dc_idx[:, None, :]]
        s = jnp.where(col_mask[:, None, :], s + bias.astype(jnp.float32)[None], -jnp.inf)
        p = jax.nn.softmax(s.reshape(B, H, GRID_W, wr * GRID_W), axis=-1)
        p = p.reshape(s.shape).astype(v.dtype)
        return jnp.einsum('bhqik,bikhd->bqhd', p, v_r)

    out = lax.map(row_block, (qg, r0, dr))
    return out.transpose(1, 0, 2, 3, 4).reshape(B, S, H * dh)


def spatial_gating(u, v, ln_g, ln_b, w_s, b_s):
    B, S, _ = u.shape
    u = jax.nn.gelu(u)
    v = layernorm(jax.nn.gelu(v), ln_g, ln_b)
    vc = v.reshape(B, S // SGU_CHUNK, SGU_CHUNK, SGU_GROUPS, SGU_GROUP_DIM)
    mixed = jnp.einsum('gpq,bnqgc->bnpgc', w_s, vc) + b_s.T[None, None, :, :, None]
    return u * mixed.reshape(B, S, SGU_WIDTH)


def latent_attention(c_q, c_kv, k_rope_in, q_norm_g, w_uq, kv_norm_g, w_ukv, cos, sin):
    B, S, _ = c_q.shape
    q = (rmsnorm(c_q, q_norm_g) @ w_uq).reshape(B, S, MLA_HEADS, MLA_NOPE + MLA_ROPE)
    q_nope, q_rope = jnp.split(q, [MLA_NOPE], axis=-1)
    q_rope = apply_rope(q_rope, cos[:, None, :], sin[:, None, :])
    kv = (rmsnorm(c_kv, kv_norm_g) @ w_ukv).reshape(B, S, MLA_HEADS, MLA_NOPE + MLA_V)
    k_nope, v = jnp.split(kv, [MLA_NOPE], axis=-1)
    k_rope = apply_rope(k_rope_in, cos, sin)
    nb = S // MLA_Q_BLOCK
    qn_b = q_nope.reshape(B, nb, MLA_Q_BLOCK, MLA_HEADS, MLA_NOPE).transpose(1, 0, 2, 3, 4)
    qr_b = q_rope.reshape(B, nb, MLA_Q_BLOCK, MLA_HEADS, MLA_ROPE).transpose(1, 0, 2, 3, 4)
    scale = (MLA_NOPE + MLA_ROPE) ** -0.5

    def q_block(args):
        qn_i, qr_i = args
        s = (jnp.einsum('bqhd,bkhd->bhqk', qn_i, k_nope)
             + jnp.einsum('bqhr,bkr->bhqk', qr_i, k_rope)).astype(jnp.float32) * scale
        p = jax.nn.softmax(s, axis=-1).astype(v.dtype)
        return jnp.einsum('bhqk,bkhd->bqhd', p, v)

    o = lax.map(q_block, (qn_b, qr_b))
    return o.transpose(1, 0, 2, 3, 4).reshape(B, S, MLA_WIDTH)


def even_mixer(x, w_in, rpb, sgu_ln_g, sgu_ln_b, sgu_w, sgu_b, w_out):
    B, S, _ = x.shape
    h = x @ w_in
    q, k, v, u, g = jnp.split(h, [NA_WIDTH, 2 * NA_WIDTH, 3 * NA_WIDTH, 3 * NA_WIDTH + SGU_WIDTH], axis=-1)
    shp = (B, S, NA_HEADS, NA_HEAD_DIM)
    a = neighbourhood_attention(q.reshape(shp), k.reshape(shp), v.reshape(shp), rpb)
    b = spatial_gating(u, g, sgu_ln_g, sgu_ln_b, sgu_w, sgu_b)
    return jnp.concatenate([a, b], axis=-1) @ w_out


def odd_mixer(x, cos, sin, w_in, q_norm_g, w_uq, kv_norm_g, w_ukv,
              conv_w, conv_b, conv_ln_g, conv_ln_b, w_out):
    h = x @ w_in
    o1 = MLA_Q_LORA
    o2 = o1 + MLA_KV_LORA
    o3 = o2 + MLA_ROPE
    o4 = o3 + CONV_WIDTH
    c_q, c_kv, k_r, glu_a, glu_g = jnp.split(h, [o1, o2, o3, o4], axis=-1)
    c = latent_attention(c_q, c_kv, k_r, q_norm_g, w_uq, kv_norm_g, w_ukv, cos, sin)
    d = glu_a * jax.nn.sigmoid(glu_g)
    d = jax.nn.silu(layernorm(dwconv(d, conv_w, conv_b), conv_ln_g, conv_ln_b))
    return jnp.concatenate([c, d], axis=-1) @ w_out


def conv_ffn(x, w_up, conv_w, conv_b, w_down):
    h = dwconv(x @ w_up, conv_w, conv_b)
    gate, val = jnp.split(h, 2, axis=-1)
    return (jax.nn.silu(gate) * val) @ w_down


def trunk(x, w_in_even, rpb, sgu_ln_g, sgu_ln_b, sgu_w, sgu_b, w_out_even,
          w_in_odd, q_norm_g, w_uq, kv_norm_g, w_ukv, conv_w, conv_b, conv_ln_g, conv_ln_b,
          w_out_odd, ffn_w_up, ffn_conv_w, ffn_conv_b, ffn_w_down, ln1_g, ln1_b, ln2_g, ln2_b):
    cos, sin = rope_tables(x.shape[1], x.dtype)
    for l in range(DEPTH):
        i = l // 2
        if l % 2 == 0:
            m = even_mixer(x, w_in_even[i], rpb[i], sgu_ln_g[i], sgu_ln_b[i], sgu_w[i], sgu_b[i],
                           w_out_even[i])
        else:
            m = odd_mixer(x, cos, sin, w_in_odd[i], q_norm_g[i], w_uq[i], kv_norm_g[i], w_ukv[i],
                          conv_w[i], conv_b[i], conv_ln_g[i], conv_ln_b[i], w_out_odd[i])
        x = layernorm(ALPHA * x + m, ln1_g[l], ln1_b[l])
        f = conv_ffn(x, ffn_w_up[l], ffn_conv_w[l], ffn_conv_b[l], ffn_w_down[l])
        x = layernorm(ALPHA * x + f, ln2_g[l], ln2_b[l])
    return x


def setup_inputs(seed: int = 0) -> dict:
    key = jax.random.key(seed)
    ks = jax.random.split(key, 32)
    f32 = jnp.float32

    def nrm(k, shape, scale):
        return jax.random.normal(k, shape, f32) * scale

    def gain(k, shape):
        return 1.0 + 0.01 * jax.random.normal(k, shape, f32)

    return {
        'x_prompt': nrm(ks[0], (BATCH, SEQ, D_MODEL), 1.0),
        'x_sample': nrm(ks[1], (DEC_BATCH, DEC_SEQ, D_MODEL), 1.0),
        'w_in_even': nrm(ks[2], (N_EVEN, D_MODEL, EVEN_IN), D_MODEL ** -0.5),
        'rpb': nrm(ks[3], (N_EVEN, NA_HEADS, 2 * NA_WIN_R - 1, 2 * NA_WIN_C - 1), 0.05),
        'sgu_ln_g': gain(ks[4], (N_EVEN, SGU_WIDTH)),
        'sgu_ln_b': nrm(ks[5], (N_EVEN, SGU_WIDTH), 0.01),
        'sgu_w': nrm(ks[6], (N_EVEN, SGU_GROUPS, SGU_CHUNK, SGU_CHUNK), SGU_CHUNK ** -0.5),
        'sgu_b': gain(ks[7], (N_EVEN, SGU_GROUPS, SGU_CHUNK)),
        'w_out_even': nrm(ks[8], (N_EVEN, EVEN_MIX, D_MODEL), EVEN_MIX ** -0.5 * BETA),
        'w_in_odd': nrm(ks[9], (N_ODD, D_MODEL, ODD_IN), D_MODEL ** -0.5),
        'q_norm_g': gain(ks[10], (N_ODD, MLA_Q_LORA)),
        'w_uq': nrm(ks[11], (N_ODD, MLA_Q_LORA, MLA_HEADS * (MLA_NOPE + MLA_ROPE)), MLA_Q_LORA ** -0.5),
        'kv_norm_g': gain(ks[12], (N_ODD, MLA_KV_LORA)),
        'w_ukv': nrm(ks[13], (N_ODD, MLA_KV_LORA, MLA_HEADS * (MLA_NOPE + MLA_V)), MLA_KV_LORA ** -0.5),
        'conv_w': nrm(ks[14], (N_ODD, CONV_K, CONV_WIDTH), CONV_K ** -0.5),
        'conv_b': nrm(ks[15], (N_ODD, CONV_WIDTH), 0.01),
        'conv_ln_g': gain(ks[16], (N_ODD, CONV_WIDTH)),
        'conv_ln_b': nrm(ks[17], (N_ODD, CONV_WIDTH), 0.01),
        'w_out_odd': nrm(ks[18], (N_ODD, ODD_MIX, D_MODEL), ODD_MIX ** -0.5 * BETA),
        'ffn_w_up': nrm(ks[19], (DEPTH, D_MODEL, 2 * D_FF), D_MODEL ** -0.5),
        'ffn_conv_w': nrm(ks[20], (DEPTH, FFN_CONV_K, 2 * D_FF), FFN_CONV_K ** -0.5),
        'ffn_conv_b': nrm(ks[21], (DEPTH, 2 * D_FF), 0.01),
        'ffn_w_down': nrm(ks[22], (DEPTH, D_FF, D_MODEL), D_FF ** -0.5 * BETA),
        'ln1_g': gain(ks[23], (DEPTH, D_MODEL)),
        'ln1_b': nrm(ks[24], (DEPTH, D_MODEL), 0.01),
        'ln2_g': gain(ks[25], (DEPTH, D_MODEL)),
        'ln2_b': nrm(ks[26], (DEPTH, D_MODEL), 0.01),
    }


def reference(x_prompt, x_sample, w_in_even, rpb, sgu_ln_g, sgu_ln_b, sgu_w, sgu_b, w_out_even,
              w_in_odd, q_norm_g, w_uq, kv_norm_g, w_ukv, conv_w, conv_b, conv_ln_g, conv_ln_b,
              w_out_odd, ffn_w_up, ffn_conv_w, ffn_conv_b, ffn_w_down, ln1_g, ln1_b, ln2_g, ln2_b):
    y_prompt = trunk(x_prompt, w_in_even, rpb, sgu_ln_g, sgu_ln_b, sgu_w, sgu_b, w_out_even,
                     w_in_odd, q_norm_g, w_uq, kv_norm_g, w_ukv, conv_w, conv_b, conv_ln_g, conv_ln_b,
                     w_out_odd, ffn_w_up, ffn_conv_w, ffn_conv_b, ffn_w_down, ln1_g, ln1_b, ln2_g, ln2_b)
    y_sample = trunk(x_sample, w_in_even, rpb, sgu_ln_g, sgu_ln_b, sgu_w, sgu_b, w_out_even,
                     w_in_odd, q_norm_g, w_uq, kv_norm_g, w_ukv, conv_w, conv_b, conv_ln_g, conv_ln_b,
                     w_out_odd, ffn_w_up, ffn_conv_w, ffn_conv_b, ffn_w_down, ln1_g, ln1_b, ln2_g, ln2_b)
    return (y_prompt, y_sample)
```

```python
import contextlib
import numpy as np
import concourse.bass as bass
import concourse.mybir as mybir
from concourse.bass_utils import run_bass_kernel_spmd

F32 = mybir.dt.float32
BF16 = mybir.dt.bfloat16
U8 = mybir.dt.uint8
AF = mybir.ActivationFunctionType
ALU = mybir.AluOpType

S_LEN = 2048
DM = 1024
DEPTH = 4
DFF = 2816
ALPHA = float((2 * DEPTH) ** 0.25)
LN_EPS = 1e-5
RMS_EPS = 1e-6
N_CORES = 8


class _Op:
    __slots__ = ("eng", "fn", "deps", "signal", "count", "is_dma", "lane", "lane_k", "epoch")

    def __init__(self, eng, fn, is_dma):
        self.eng = eng
        self.fn = fn
        self.deps = []
        self.signal = False
        self.count = 0
        self.is_dma = is_dma
        self.lane = 0
        self.lane_k = 0
        self.epoch = 0


class Sched:
    ENG_ATTR = {"pe": "tensor", "act": "scalar", "dve": "vector", "pool": "gpsimd", "sp": "sync"}

    def __init__(self, nc, n_lanes=8, same_eng_sync=True):
        self.nc = nc
        self.ops = {k: [] for k in self.ENG_ATTR}
        self.res_w = {}
        self.res_r = {}
        self.n_lanes = n_lanes
        self.dma_cnt = {k: 0 for k in self.ENG_ATTR}
        self.same_eng_sync = same_eng_sync
        self.epoch = 0

    def _record(self, o, reads, writes):
        o.epoch = self.epoch
        deps = {}
        res_w, res_r = self.res_w, self.res_r
        for r in reads:
            w = res_w.get(r)
            if w is not None:
                deps[id(w)] = w
        for r in writes:
            w = res_w.get(r)
            if w is not None:
                deps[id(w)] = w
            rr = res_r.get(r)
            if rr:
                for x in rr.values():
                    deps[id(x)] = x
        eng = o.eng
        dl = []
        for d in deps.values():
            if d is o:
                continue
            if d.eng == eng and not d.is_dma and not o.is_dma:
                if eng == "pe" or not self.same_eng_sync:
                    continue
            dl.append(d)
        o.deps = dl
        for r in writes:
            res_w[r] = o
            res_r[r] = {}
        for r in reads:
            rr = res_r.get(r)
            if rr is None:
                rr = res_r[r] = {}
            rr[eng if not o.is_dma else (eng, o.lane)] = o
        self.ops[eng].append(o)
        return o

    def op(self, eng, fn, reads=(), writes=()):
        return self._record(_Op(eng, fn, False), reads, writes)

    def dma(self, eng, fn, reads=(), writes=()):
        o = _Op(eng, fn, True)
        i = self.dma_cnt[eng]
        self.dma_cnt[eng] = i + 1
        o.lane = i % self.n_lanes
        o.lane_k = i // self.n_lanes + 1
        return self._record(o, reads, writes)

    def emit(self):
        import os
        TRACE = bool(os.environ.get("KTRACE"))
        nc = self.nc
        for eng, lst in self.ops.items():
            for o in lst:
                for d in o.deps:
                    d.signal = True
        with contextlib.ExitStack() as st:
            sems = {}
            for eng in self.ops:
                for ep in sorted(set(o.epoch for o in self.ops[eng])):
                    sems[(eng, ep)] = st.enter_context(nc.semaphore("s_%s_%d" % (eng, ep)))
            lanes = {}
            for eng in self.ops:
                if self.dma_cnt[eng] > 0:
                    lanes[eng] = [st.enter_context(nc.semaphore("l_%s%d" % (eng, i)))
                                  for i in range(min(self.n_lanes, self.dma_cnt[eng]))]
            for eng, lst in self.ops.items():
                cc = {}
                for o in lst:
                    if not o.is_dma and o.signal:
                        c = cc.get(o.epoch, 0) + 1
                        cc[o.epoch] = c
                        o.count = c
            block = st.enter_context(nc.Block())

            def siginfo(d):
                if d.is_dma:
                    return lanes[d.eng][d.lane], 16 * d.lane_k
                return sems[(d.eng, d.epoch)], d.count

            def body_for(eng):
                lst = self.ops[eng]

                def body(e):
                    waited = {}
                    last_dma = {}
                    for oi, o in enumerate(lst):
                        for d in o.deps:
                            s, v = siginfo(d)
                            k = id(s)
                            if waited.get(k, 0) < v:
                                e.wait_ge(s, v)
                                waited[k] = v
                                if TRACE:
                                    print("  %s op%d waits %s(ep%d lane%d) >= %d" % (eng, oi, d.eng, d.epoch, d.lane if d.is_dma else -1, v))
                        if TRACE:
                            print("%s op%d %s signal=%s count=%d ep=%d lane=%d k=%d" % (eng, oi, "DMA" if o.is_dma else "OP", o.signal, o.count, o.epoch, o.lane, o.lane_k))
                        if o.is_dma:
                            s = lanes[eng][o.lane]
                            if o.lane_k > 1:
                                k = id(s)
                                v = 16 * (o.lane_k - 1)
                                if waited.get(k, 0) < v:
                                    e.wait_ge(s, v)
                                    waited[k] = v
                            ins = o.fn(e)
                            ins.then_inc(s, 16)
                            last_dma[o.lane] = o
                        else:
                            ins = o.fn(e)
                            if o.signal:
                                ins.then_inc(sems[(eng, o.epoch)], 1)
                    for lane, o in last_dma.items():
                        s = lanes[eng][lane]
                        v = 16 * o.lane_k
                        if waited.get(id(s), 0) < v:
                            e.wait_ge(s, v)
                return body

            for eng, attr in self.ENG_ATTR.items():
                if self.ops[eng]:
                    getattr(block, attr)(body_for(eng))


def _carve(t, off, dt, dims):
    n = int(np.prod(dims))
    sz = 2 if dt == BF16 else 4
    ap = t[:, off:off + n * sz].bitcast(dt)
    if len(dims) == 2:
        ap = ap.rearrange("p (a b) -> p a b", a=dims[0])
    elif len(dims) == 3:
        ap = ap.rearrange("p (a b c) -> p a b c", a=dims[0], b=dims[1])
    return ap


def _param_layout():
    off = {}
    n = 0
    for l in range(DEPTH):
        for nm, w in (("ln1g", 8), ("ln1b", 8), ("ln2g", 8), ("ln2b", 8), ("fcw", 132), ("fcb", 44)):
            off[(nm, l)] = n
            n += w
    for i in range(2):
        for nm, w in (("qg", 2), ("kvg", 1), ("cw", 124), ("cb", 4), ("clg", 4), ("clb", 4)):
            off[(nm, i)] = n
            n += w
    return off, n


POFF, NPAR = _param_layout()


def _pack_params(inp):
    P = np.zeros((128, NPAR), np.float32)

    def put(key, arr):
        a = np.ascontiguousarray(arr, dtype=np.float32).reshape(128, -1)
        P[:, POFF[key]:POFF[key] + a.shape[1]] = a

    for l in range(DEPTH):
        put(("ln1g", l), inp["ln1_g"][l].reshape(8, 128).T)
        put(("ln1b", l), inp["ln1_b"][l].reshape(8, 128).T)
        put(("ln2g", l), inp["ln2_g"][l].reshape(8, 128).T)
        put(("ln2b", l), inp["ln2_b"][l].reshape(8, 128).T)
        put(("fcw", l), inp["ffn_conv_w"][l].reshape(3, 44, 128).transpose(2, 1, 0))
        put(("fcb", l), inp["ffn_conv_b"][l].reshape(44, 128).T)
    for i in range(2):
        put(("qg", i), inp["q_norm_g"][i].reshape(2, 128).T)
        put(("kvg", i), inp["kv_norm_g"][i].reshape(1, 128).T)
        put(("cw", i), inp["conv_w"][i].reshape(31, 4, 128).transpose(2, 1, 0))
        put(("cb", i), inp["conv_b"][i].reshape(4, 128).T)
        put(("clg", i), inp["conv_ln_g"][i].reshape(4, 128).T)
        put(("clb", i), inp["conv_ln_b"][i].reshape(4, 128).T)
    return P


def _consts():
    c = {}
    c["ident"] = np.eye(128, dtype=np.float32)
    cq = np.arange(64)
    c0 = np.clip(cq - 8, 0, 48)
    ck = np.arange(64)
    m = ((ck[:, None] >= c0[None, :]) & (ck[:, None] < c0[None, :] + 16)).astype(np.float32)
    c["namask"] = np.concatenate([m, m], axis=0)
    pos = np.arange(S_LEN, dtype=np.float32)
    inv = (10000.0 ** (-np.arange(0, 32, 2, dtype=np.float32) / 32)).astype(np.float32)
    ang = (pos[:, None] * inv[None, :]).astype(np.float32)
    cos = np.cos(ang).astype(np.float32).T
    sin = np.sin(ang).astype(np.float32).T
    ct = np.zeros((128, S_LEN), np.float32)
    sn = np.zeros((128, S_LEN), np.float32)
    ct[64:80] = cos
    ct[80:96] = cos
    sn[64:80] = -sin
    sn[80:96] = sin
    c["cost"] = ct
    c["sint"] = sn
    return c


WSPECS = [
    ("w_in_even", [2, 1024, 2560]), ("w_out_even", [2, 1024, 1024]), ("w_in_odd", [2, 1024, 1440]),
    ("w_uq", [2, 256, 768]), ("w_ukv", [2, 128, 1024]), ("w_out_odd", [2, 1024, 1024]),
    ("ffn_w_up", [4, 1024, 5632]), ("ffn_w_down", [4, 2816, 1024]), ("sgu_wT", [2, 4, 128, 128]),
]


def build_nc(nseq, nlayers=DEPTH, do_ffn=True):
    nc = bass.Bass("TRN2", target_bir_lowering=False)
    din = lambda n, s, d=F32: nc.dram_tensor(n, list(s), d, kind="ExternalInput").ap()
    x_d = din("x", [nseq, S_LEN, DM])
    y_d = nc.dram_tensor("y", [nseq, S_LEN, DM], F32, kind="ExternalOutput").ap()
    wsrc = {n: din(n, s) for n, s in WSPECS}
    wbf = {n: nc.dram_tensor(n + "_bf", list(s), BF16, kind="Internal").ap() for n, s in WSPECS}
    rpbpad_d = din("rpb_pad", [2, 8, 15, 127])
    sgu_lng_d = din("sgu_ln_g", [2, 512])
    sgu_lnb_d = din("sgu_ln_b", [2, 512])
    sgu_b_d = din("sgu_b", [2, 512])
    params_d = din("params", [128, NPAR])
    ident_d = din("ident", [128, 128])
    namask_d = din("namask", [128, 64])
    cost_d = din("cost", [128, S_LEN])
    sint_d = din("sint", [128, S_LEN])
    ebtab_d = nc.dram_tensor("ebtab", [2, 8, 128, 960], F32, kind="Internal").ap()
    dummy_d = nc.dram_tensor("dummy_scr", [1, 64], F32, kind="Internal").ap()

    import os
    DBG = os.environ.get("KDBG", "")
    if DBG:
        dbg_d = nc.dram_tensor("dbg", [128, 16384], BF16, kind="ExternalOutput").ap()
    S = Sched(nc, n_lanes=int(os.environ.get('KLANES', '8')))
    NSLOT = 6
    SLOTW = 1408
    SCRB = 40960

    with contextlib.ExitStack() as st:
        sb = lambda n, s, d: st.enter_context(nc.sbuf_tensor(n, s, d))
        XT = sb("XT", [128, 8, S_LEN], F32)
        XBr = sb("XBr", [128, 32768], U8)
        ABr = sb("ABr", [128, 32768], U8)
        wsl = sb("wsl", [128, NSLOT, SLOTW], BF16)
        scr = sb("scr", [128, SCRB], U8)
        par = sb("par", [128, NPAR], F32)
        identf = sb("identf", [128, 128], F32)
        identb = sb("identb", [128, 128], BF16)
        onesb = sb("onesb", [128, 128], BF16)
        onesf = sb("onesf", [1, 128], F32)
        dumt = sb("dumt", [128, 2], F32)
        ps = st.enter_context(nc.psum_tensor("ps", [128, 4096], F32))

        XB = _carve(XBr, 0, BF16, [8, S_LEN])
        ABT = _carve(ABr, 0, BF16, [8, S_LEN])
        bank = lambda b: ps[:, 512 * b:512 * (b + 1)]
        PB = lambda b: ("ps", b)
        SC = "SCR"

        def pc(key, w=1, j=0):
            o = POFF[key] + j
            return par[:, o:o + w]

        A = lambda fn, r=(), w=(): S.op("act", fn, list(r) + [SC], w)
        V = lambda fn, r=(), w=(): S.op("dve", fn, list(r) + [SC], w)
        G = lambda fn, r=(), w=(): S.op("pool", fn, list(r) + [SC], w)
        T = lambda fn, r=(), w=(): S.op("pe", fn, list(r) + [SC], w)
        D = lambda fn, r=(), w=(), nosc=False: S.dma("sp", fn, list(r) + ([] if nosc else [SC]), w)

        def fence():
            S.op("pool", lambda e: e.memset(dumt[:, 0:1], 0.0), [], [SC])

        def mm(out, pairs, r, w):
            def f(e):
                n = len(pairs)
                for i, (l, rh) in enumerate(pairs):
                    ins = e.matmul(out, l, rh, start=(i == 0), stop=(i == n - 1))
                return ins
            T(f, r, w)

        wctr = [0]

        def load_w(src_ap, nk, ncol=128, extra=None):
            slot = wctr[0] % NSLOT
            wctr[0] += 1
            dst = wsl[:, slot, 0:nk * ncol].rearrange("p (k j) -> p k j", j=ncol)
            key = ("w", slot)
            D(lambda e: e.dma_start(out=dst, in_=src_ap), ["dw"], [key], nosc=True)
            if extra:
                for (c0, c1, sap) in extra:
                    D(lambda e, c0=c0, c1=c1, sap=sap: e.dma_start(out=dst[:, :, c0:c1], in_=sap), ["dw"], [key], nosc=True)
            return dst, key

        D(lambda e: e.dma_start(out=par[:], in_=params_d), [], ["par"])
        D(lambda e: e.dma_start(out=identf[:], in_=ident_d), [], ["identf"])
        V(lambda e: e.tensor_copy(identb[:], identf[:]), ["identf"], ["identb"])
        G(lambda e: e.memset(onesb[:], 1.0), [], ["onesb"])
        G(lambda e: e.memset(onesf[:], 1.0), [], ["onesf"])

        st32 = [XT[:, 2 * i:2 * i + 2, :].rearrange("p a b -> p (a b)") for i in range(4)]
        st16 = [_carve(XBr, 8192 * i, BF16, [4096]) for i in range(4)]
        pst = []
        ci = 0
        import os
        SKIP = os.environ.get("KSKIP", "").split(",")
        for name, shp in WSPECS:
            if "conv" in SKIP:
                break
            n = int(np.prod(shp))
            per = n // 128
            letters = " ".join("abcd"[:len(shp)])
            sv = wsrc[name].rearrange("%s -> (%s)" % (letters, letters)).rearrange("(p f) -> p f", p=128)
            dv = wbf[name].rearrange("%s -> (%s)" % (letters, letters)).rearrange("(p f) -> p f", p=128)
            for c0 in range(0, per, 4096):
                w = min(4096, per - c0)
                sl = ci % 4
                D(lambda e, sl=sl, w=w, c0=c0, sv=sv: e.dma_start(out=st32[sl][:, 0:w], in_=sv[:, c0:c0 + w]),
                  [], [("st32", sl)])
                eng = ("act", "dve")[ci % 2]
                if eng == "act":
                    A(lambda e, sl=sl, w=w: e.activation(st16[sl][:, 0:w], st32[sl][:, 0:w], AF.Copy),
                      [("st32", sl)], [("st16", sl)])
                else:
                    S.op(eng, lambda e, sl=sl, w=w: e.tensor_copy(st16[sl][:, 0:w], st32[sl][:, 0:w]),
                         [("st32", sl)], [("st16", sl)])
                k = ("pst", ci)
                D(lambda e, sl=sl, w=w, c0=c0, dv=dv: e.dma_start(out=dv[:, c0:c0 + w], in_=st16[sl][:, 0:w]),
                  [("st16", sl)], [k])
                pst.append(k)
                ci += 1
        hk = _carve(scr, 0, F32, [15, 64])
        eb = _carve(scr, 4096, F32, [15, 64])
        mk = _carve(scr, 8192, F32, [64])
        D(lambda e: e.dma_start(out=mk, in_=namask_d), [], ["mk"])
        for i in range(2):
            if "eb" in SKIP:
                break
            for h in range(8):
                base = ((i * 8 + h) * 15) * 127
                src = bass.AP(tensor=rpbpad_d.tensor, offset=base, ap=[[1, 64], [127, 15], [1, 64]])
                D(lambda e, src=src: e.dma_start(out=hk[0:64], in_=src), [], ["hk0"])
                D(lambda e, src=src: e.dma_start(out=hk[64:128], in_=src), [], ["hk1"])
                A(lambda e: e.activation(hk, hk, AF.Exp), ["hk0", "hk1"], ["hk0", "hk1"])
                V(lambda e: e.tensor_tensor(eb, hk[:, :, ::-1], mk.unsqueeze(1).broadcast_to([128, 15, 64]), ALU.mult),
                  ["hk0", "hk1", "mk"], ["eb"])
                k = ("pst", ci)
                ci += 1
                D(lambda e, i=i, h=h: e.dma_start(out=ebtab_d[i, h].rearrange("p (a b) -> p a b", a=15), in_=eb),
                  ["eb"], [k])
                pst.append(k)
        D(lambda e: e.dma_start(out=dummy_d, in_=ident_d[0:1, 0:64]), pst, ["dw"])
        stage_keys = [("st32", i) for i in range(4)] + [("st16", i) for i in range(4)]

        XTk = lambda cs, tts: [("XT", c, t) for c in cs for t in tts]
        XBk = lambda cs, tts: [("XB", c, t) for c in cs for t in tts]
        ABk = lambda cs, tts: [("AB", c, t) for c in cs for t in tts]
        R8 = range(8)
        R4 = range(4)

        def load_x(s, first):
            fence()
            for tc in range(16):
                sl = tc % 2
                xtok = _carve(scr, 4096 * sl, F32, [1024])
                tt = tc // 4
                D(lambda e, xtok=xtok, tc=tc: e.dma_start(out=xtok, in_=x_d[s, tc * 128:(tc + 1) * 128, :]),
                  [], [("tok", sl, 0), ("tok", sl, 1)] + (["hk0", "hk1", "eb", "mk"] if first else []))
                for half in range(2):
                    b = (tc * 2 + half) % 8

                    def f(e, xtok=xtok, half=half, b=b):
                        for j in range(4):
                            ins = e.transpose(ps[:, 512 * b + 128 * j:512 * b + 128 * (j + 1)],
                                              xtok[:, (4 * half + j) * 128:(4 * half + j + 1) * 128], identf[:])
                        return ins
                    T(f, [("tok", sl, 0), ("tok", sl, 1), "identf"], [PB(b)])
                    cs = range(4 * half, 4 * half + 4)
                    src = bank(b).rearrange("p (a b) -> p a b", a=4)
                    extra = stage_keys if first else []
                    A(lambda e, half=half, tc=tc, src=src: e.activation(
                        XT[:, 4 * half:4 * half + 4, tc * 128:(tc + 1) * 128], src, AF.Copy),
                      [PB(b)], XTk(cs, [tt]) + extra)
                    V(lambda e, half=half, tc=tc: e.tensor_copy(
                        XB[:, 4 * half:4 * half + 4, tc * 128:(tc + 1) * 128],
                        XT[:, 4 * half:4 * half + 4, tc * 128:(tc + 1) * 128]),
                      XTk(cs, [tt]), XBk(cs, [tt]) + extra)

        def store_y(s):
            fence()
            for tc in range(16):
                sl = tc % 2
                ytok = _carve(scr, 4096 * sl, F32, [1024])
                tt = tc // 4
                for half in range(2):
                    b = (tc * 2 + half) % 8

                    def f(e, half=half, b=b, tc=tc):
                        for j in range(4):
                            ins = e.transpose(ps[:, 512 * b + 128 * j:512 * b + 128 * (j + 1)],
                                              XT[:, 4 * half + j, tc * 128:(tc + 1) * 128], identf[:])
                        return ins
                    T(f, XTk(range(4 * half, 4 * half + 4), [tt]) + ["identf"], [PB(b)])
                    if half == 0:
                        A(lambda e, ytok=ytok, b=b: e.activation(ytok[:, 0:512], bank(b), AF.Copy),
                          [PB(b), SC], [("tok", sl, 0)])
                    else:
                        V(lambda e, ytok=ytok, b=b: e.tensor_copy(ytok[:, 512:1024], bank(b)),
                          [PB(b), SC], [("tok", sl, 1)])
                D(lambda e, ytok=ytok, tc=tc: e.dma_start(out=y_d[s, tc * 128:(tc + 1) * 128, :], in_=ytok),
                  [("tok", sl, 0), ("tok", sl, 1)], [("tok", sl, 0), ("tok", sl, 1)])

        def ln_stats(tt, src_fn, nch, scale_n, eps, bA, bB, sq, x16, m_t, v_t, l_t, rkeys, x16keys):
            pass

        def ln_phase(gkey, bkey, l):
            fence()
            if "ln" in SKIP:
                return
            sqs = [_carve(scr, 8192 * j, BF16, [8, 512]) for j in range(2)]
            m_ts = [_carve(scr, 16384 + 2048 * j, F32, [512]) for j in range(2)]
            v_ts = [_carve(scr, 20480 + 2048 * j, F32, [512]) for j in range(2)]
            l_ts = [_carve(scr, 24576 + 2048 * j, F32, [512]) for j in range(2)]

            def s1(tt):
                ts = slice(tt * 512, (tt + 1) * 512)
                bA, bB = 2 * tt, 2 * tt + 1
                sq = sqs[tt % 2]
                sqk = ("sq", tt % 2)
                xk = XTk(R8, [tt])
                A(lambda e: e.activation(sq, XT[:, :, ts], AF.Square), xk, [sqk])
                V(lambda e: e.tensor_copy(XB[:, :, ts], XT[:, :, ts]), xk, XBk(R8, [tt]))
                mm(bank(bA), [(onesb[:], XB[:, c, ts]) for c in R8], XBk(R8, [tt]) + ["onesb"], [PB(bA)])
                mm(bank(bB), [(onesb[:], sq[:, c, :]) for c in R8], [sqk, "onesb"], [PB(bB)])

            def s2(tt):
                bA, bB = 2 * tt, 2 * tt + 1
                m_t, v_t, l_t = m_ts[tt % 2], v_ts[tt % 2], l_ts[tt % 2]
                mk_, vk_, lk_ = ("m_t", tt % 2), ("v_t", tt % 2), ("l_t", tt % 2)
                A(lambda e: e.activation(m_t, bank(bA), AF.Identity, scale=1.0 / DM), [PB(bA)], [mk_])
                A(lambda e: e.activation(v_t, bank(bA), AF.Square, scale=1.0 / DM), [PB(bA)], [vk_])
                V(lambda e: e.scalar_tensor_tensor(v_t, bank(bB), 1.0 / DM, v_t, ALU.mult, ALU.subtract),
                  [PB(bB), vk_], [vk_])
                A(lambda e: e.activation(l_t, v_t, AF.Ln, bias=float(LN_EPS)), [vk_], [lk_])
                A(lambda e: e.activation(bank(bA), l_t, AF.Exp, scale=-0.5), [lk_, mk_], [PB(bA)])
                V(lambda e: e.scalar_tensor_tensor(bank(bB), m_t, -1.0, bank(bA), ALU.mult, ALU.mult),
                  [mk_, PB(bA), vk_], [PB(bB)])

            def s3(tt):
                ts = slice(tt * 512, (tt + 1) * 512)
                bA, bB = 2 * tt, 2 * tt + 1
                xk = XTk(R8, [tt])
                V(lambda e: e.tensor_tensor(
                    XT[:, :, ts], XT[:, :, ts], bank(bA).unsqueeze(1).broadcast_to([128, 8, 512]), ALU.mult),
                  xk + [PB(bA)], xk)
                V(lambda e: e.tensor_tensor(
                    XT[:, :, ts], XT[:, :, ts], bank(bB).unsqueeze(1).broadcast_to([128, 8, 512]), ALU.add),
                  xk + [PB(bB)], xk)

            def s4(tt):
                ts = slice(tt * 512, (tt + 1) * 512)
                for c in R8:
                    gs = pc((gkey, l), 1, c)
                    bs = pc((bkey, l), 1, c)
                    if c < 6:
                        A(lambda e, c=c, gs=gs, bs=bs: e.activation(XB[:, c, ts], XT[:, c, ts], AF.Identity, bias=bs, scale=gs),
                          [("XT", c, tt), "par"], [("XB", c, tt)])
                        A(lambda e, c=c, gs=gs, bs=bs: e.activation(XT[:, c, ts], XT[:, c, ts], AF.Identity, bias=bs, scale=gs),
                          [("XT", c, tt), "par"], [("XT", c, tt)])
                    else:
                        V(lambda e, c=c, gs=gs, bs=bs: e.tensor_scalar(XB[:, c, ts], XT[:, c, ts], gs, bs, ALU.mult, ALU.add),
                          [("XT", c, tt), "par"], [("XB", c, tt)])
                        V(lambda e, c=c, gs=gs, bs=bs: e.tensor_scalar(XT[:, c, ts], XT[:, c, ts], gs, bs, ALU.mult, ALU.add),
                          [("XT", c, tt), "par"], [("XT", c, tt)])

            for st_fn, tt_ in ((s1, 0), (s1, 1), (s2, 0), (s1, 2), (s2, 1), (s3, 0), (s1, 3), (s2, 2), (s3, 1), (s4, 0),
                               (s2, 3), (s3, 2), (s4, 1), (s3, 3), (s4, 2), (s4, 3)):
                st_fn(tt_)

        def w_out_phase(wname, i):
            fence()
            if DBG:
                D(lambda e: e.dma_start(out=dbg_d, in_=_carve(ABr, 0, BF16, [16384])), ABk(R8, R4), [])
            if "wout" in SKIP:
                return
            bctr = 0
            for oc in R8:
                wt, wk = load_w(wbf[wname][i, :, oc * 128:(oc + 1) * 128].rearrange("(k p) j -> p k j", p=128), 8)
                for tt in R4:
                    ts = slice(tt * 512, (tt + 1) * 512)
                    b = bctr % 8
                    bctr += 1
                    mm(bank(b), [(wt[:, k, :], ABT[:, k, ts]) for k in R8], [wk] + ABk(R8, [tt]), [PB(b)])
                    V(lambda e, oc=oc, ts=ts, b=b: e.scalar_tensor_tensor(
                        XT[:, oc, ts], XT[:, oc, ts], ALPHA, bank(b), ALU.mult, ALU.add),
                      [PB(b), ("XT", oc, tt)], [("XT", oc, tt)])

        def ffn_phase(l):
            fence()
            if "ffn" in SKIP:
                return
            groups = [(0, 6), (6, 12), (12, 17), (17, 22)]
            Abuf = ABT
            for gi, (c0, c1) in enumerate(groups):
                for ci_ in range(c0, c1):
                    a = ci_ - c0
                    for half_gv in range(2):
                        col0 = ci_ * 128 + (DFF if half_gv else 0)
                        ch = ci_ + (22 if half_gv else 0)
                        wt, wk = load_w(wbf["ffn_w_up"][l, :, col0:col0 + 128].rearrange("(k p) j -> p k j", p=128), 8)
                        b0 = 4 * half_gv
                        hb = [PB(b0 + j) for j in R4]
                        for tt in R4:
                            ts = slice(tt * 512, (tt + 1) * 512)
                            mm(bank(b0 + tt), [(wt[:, k, :], XB[:, k, ts]) for k in R8],
                               [wk] + XBk(R8, [tt]), [PB(b0 + tt)])
                        H = ps[:, 2048 * half_gv:2048 * (half_gv + 1)]
                        w0 = pc(("fcw", l), 1, ch * 3 + 0)
                        w1 = pc(("fcw", l), 1, ch * 3 + 1)
                        w2 = pc(("fcw", l), 1, ch * 3 + 2)
                        bb = pc(("fcb", l), 1, ch)
                        bufs = [_carve(scr, (8192 if half_gv else 0) + 4096 * hf, F32, [1024]) for hf in range(2)]
                        bks = [("ffb", half_gv, hf) for hf in range(2)]
                        for hf in range(2):
                            lo = 1024 * hf
                            A(lambda e, buf=bufs[hf], lo=lo, H=H, w1=w1, bb=bb: e.activation(
                                buf, H[:, lo:lo + 1024], AF.Identity, bias=bb, scale=w1), hb + ["par", SC], [bks[hf]])
                        hb01 = [PB(b0), PB(b0 + 1)]
                        for hf in range(2):
                            lo = 1024 * hf
                            buf = bufs[hf]
                            bk = bks[hf]
                            if hf == 0:
                                V(lambda e, buf=buf, H=H, w0=w0: e.scalar_tensor_tensor(
                                    buf[:, 1:1024], H[:, 0:1023], w0, buf[:, 1:1024], ALU.mult, ALU.add),
                                  hb01 + [bks[0], "par"], [bk])
                                V(lambda e, buf=buf, H=H, w2=w2: e.scalar_tensor_tensor(
                                    buf[:, 0:1023], H[:, 1:1024], w2, buf[:, 0:1023], ALU.mult, ALU.add),
                                  hb01 + [bk, "par"], [bk])
                                V(lambda e, buf=buf, H=H, w2=w2: e.scalar_tensor_tensor(
                                    buf[:, 1023:1024], H[:, 1024:1025], w2, buf[:, 1023:1024], ALU.mult, ALU.add),
                                  hb + [bk, bks[1], "par"], [bk])
                            else:
                                V(lambda e, buf=buf, H=H, w0=w0: e.scalar_tensor_tensor(
                                    buf[:, 0:1024], H[:, 1023:2047], w0, buf[:, 0:1024], ALU.mult, ALU.add),
                                  hb + [bks[0], bks[1], "par"], [bk])
                                V(lambda e, buf=buf, H=H, w2=w2: e.scalar_tensor_tensor(
                                    buf[:, 0:1023], H[:, 1025:2048], w2, buf[:, 0:1023], ALU.mult, ALU.add),
                                  hb + [bk, "par"], [bk])
                            if half_gv == 0:
                                A(lambda e, buf=buf: e.activation(buf, buf, AF.Silu), [bk], [bk])
                            else:
                                gbuf = _carve(scr, 4096 * hf, F32, [1024])
                                G(lambda e, buf=buf, gbuf=gbuf, a=a, lo=lo: e.tensor_tensor(
                                    Abuf[:, a, lo:lo + 1024], gbuf, buf, ALU.mult),
                                  [bk, ("ffb", 0, hf)], ABk([a], [2 * hf, 2 * hf + 1]))
                nk = c1 - c0
                bctr = 0
                wts = {}
                for kpass in range(2):
                    ks = list(range(nk - 1)) if kpass == 0 else [nk - 1]
                    for oc in R8:
                        if kpass == 0 or oc >= 4:
                            wt, wk = load_w(wbf["ffn_w_down"][l, c0 * 128:c1 * 128, oc * 128:(oc + 1) * 128]
                                            .rearrange("(k p) j -> p k j", p=128), nk)
                            wts[oc] = (wt, wk)
                        else:
                            wt, wk = load_w(wbf["ffn_w_down"][l, (c1 - 1) * 128:c1 * 128, oc * 128:(oc + 1) * 128]
                                            .rearrange("(k p) j -> p k j", p=128), 1)
                            wt = wt
                            wts[oc] = (None, None)
                        for tt in R4:
                            ts = slice(tt * 512, (tt + 1) * 512)
                            b = bctr % 8
                            bctr += 1
                            if kpass == 1 and oc < 4:
                                pairs = [(wt[:, 0, :], Abuf[:, nk - 1, ts])]
                            elif kpass == 1:
                                pairs = [(wt[:, nk - 1, :], Abuf[:, nk - 1, ts])]
                            else:
                                pairs = [(wt[:, k, :], Abuf[:, k, ts]) for k in ks]
                            mm(bank(b), pairs, [wk] + ABk(ks, [tt]), [PB(b)])
                            sc_ = ALPHA if (gi == 0 and kpass == 0) else 1.0
                            V(lambda e, oc=oc, ts=ts, b=b, sc_=sc_: e.scalar_tensor_tensor(
                                XT[:, oc, ts], XT[:, oc, ts], sc_, bank(b), ALU.mult, ALU.add),
                              [PB(b), ("XT", oc, tt)], [("XT", oc, tt)])

        def even_mixer(i):
            win = wbf["w_in_even"]
            wcol = lambda c0: win[i, :, c0:c0 + 128].rearrange("(k p) j -> p k j", p=128)
            fence()
            if "sgu" in SKIP:
                G(lambda e: e.memset(ABT[:, 4:8, :], 0.0), [], ABk(range(4, 8), R4))
                return na_part(i, wcol)
            vln = _carve(scr, 0, BF16, [16, 512])
            U = _carve(scr, 16384, F32, [2048])
            tmpA = [_carve(scr, 24576 + 2048 * j, F32, [512]) for j in range(2)]
            gbc = _carve(scr, 28672, F32, [512])
            bbc = _carve(scr, 30720, F32, [512])
            sguW = _carve(scr, 32768, BF16, [4, 128])
            stt = _carve(scr, 33792, F32, [16])
            sgub = _carve(scr, 34048, F32, [512])
            D(lambda e: e.dma_start(out=gbc, in_=sgu_lng_d[i, :].partition_broadcast(128)), [SC], ["gbc"])
            D(lambda e: e.dma_start(out=bbc, in_=sgu_lnb_d[i, :].partition_broadcast(128)), [SC], ["bbc"])
            D(lambda e: e.dma_start(out=sguW, in_=wbf["sgu_wT"][i].rearrange("g q p -> q g p")), [SC, "dw"], ["sguW"])
            D(lambda e: e.dma_start(out=sgub[0:1, :], in_=sgu_b_d[i:i + 1, :]), [SC], ["sgub"])
            gw = [load_w(wcol(2048 + 128 * j), 8) for j in R4]
            for tc in range(16):
                b = tc % 4
                tt = tc // 4
                tcs = slice(tc * 128, (tc + 1) * 128)

                def f(e, b=b, tcs=tcs):
                    for j in R4:
                        for k in R8:
                            ins = e.matmul(ps[:, 512 * b + 128 * j:512 * b + 128 * (j + 1)], XB[:, k, tcs],
                                           gw[j][0][:, k, :], start=(k == 0), stop=(k == 7))
                    return ins
                T(f, [g_[1] for g_ in gw] + XBk(R8, [tt]), [PB(b)])
                ta = tmpA[tc % 2]
                tk = ("tmpA", tc % 2)
                A(lambda e, ta=ta, b=b: e.activation(ta, bank(b), AF.Gelu_apprx_tanh), [PB(b), SC], [tk])
                V(lambda e, ta=ta: e.bn_stats(stt[:, 0:6], ta), [tk, SC], ["stt"])
                V(lambda e: e.bn_aggr(stt[:, 6:8], stt[:, 0:6]), ["stt"], ["stt"])
                A(lambda e: e.activation(stt[:, 8:9], stt[:, 7:8], AF.Ln, bias=float(LN_EPS)), ["stt"], ["stt"])
                A(lambda e: e.activation(stt[:, 9:10], stt[:, 8:9], AF.Exp, scale=-0.5), ["stt"], ["stt"])
                V(lambda e: e.scalar_tensor_tensor(stt[:, 10:11], stt[:, 6:7], -1.0, stt[:, 9:10], ALU.mult, ALU.mult),
                  ["stt"], ["stt"])
                A(lambda e, ta=ta: e.activation(ta, ta, AF.Identity, bias=stt[:, 10:11], scale=stt[:, 9:10]),
                  ["stt", tk], [tk])
                V(lambda e, ta=ta: e.tensor_tensor(ta, ta, gbc, ALU.mult), [tk, "gbc"], [tk])
                V(lambda e, ta=ta, tc=tc: e.tensor_tensor(vln[:, tc, :], ta, bbc, ALU.add), [tk, "bbc"], [("vln", tc)])
            for g in R4:
                uw, uk = load_w(wcol(1536 + 128 * g), 8)
                for tt in R4:
                    ts = slice(tt * 512, (tt + 1) * 512)
                    mm(bank(4 + tt), [(uw[:, k, :], XB[:, k, ts]) for k in R8], [uk] + XBk(R8, [tt]), [PB(4 + tt)])
                    A(lambda e, tt=tt, ts=ts: e.activation(U[:, ts], bank(4 + tt), AF.Gelu_apprx_tanh),
                      [PB(4 + tt), SC], [("U", tt)])
                for tt in R4:
                    ts = slice(tt * 512, (tt + 1) * 512)

                    def f(e, g=g, tt=tt):
                        for j in R4:
                            tc = 4 * tt + j
                            o = ps[:, 512 * tt + 128 * j:512 * tt + 128 * (j + 1)]
                            e.matmul(o, vln[:, tc, g * 128:(g + 1) * 128], sguW[:, g, :], start=True, stop=False)
                            ins = e.matmul(o, onesf[0:1, :], sgub[0:1, g * 128:(g + 1) * 128], start=False, stop=True)
                        return ins
                    T(f, [("vln", 4 * tt + j) for j in R4] + ["sguW", "sgub", "onesf"], [PB(tt)])
                    V(lambda e, g=g, tt=tt, ts=ts: e.tensor_tensor(ABT[:, 4 + g, ts], bank(tt), U[:, ts], ALU.mult),
                      [PB(tt), ("U", tt)], [("AB", 4 + g, tt)])
            return na_part(i, wcol)

        def na_part(i, wcol):
            fence()
            if "na" in SKIP:
                G(lambda e: e.memset(ABT[:, 0:4, :], 0.0), [], ABk(R4, R4))
                return w_out_phase("w_out_even", i)
            QT = _carve(scr, 0, BF16, [2048])
            KT = _carve(scr, 4096, BF16, [2048])
            Vaug = _carve(scr, 8192, BF16, [16, 256])
            EBs = [_carve(scr, 16384 + 3840 * j, F32, [15, 64]) for j in range(2)]
            etm = [_carve(scr, 24064 + 2048 * j, F32, [512]) for j in range(2)]
            NPT = 5
            SKEW = int(os.environ.get("KSKEW", "2"))
            pend = []
            PTs = [_carve(scr, 28160 + 1024 * j, BF16, [512]) for j in range(NPT)]
            rcs = [_carve(scr, 33280 + 2048 * j, F32, [512]) for j in range(2)]
            if "nomemset" not in SKIP:
                G(lambda e: e.memset(Vaug[:, :, 64:192], 1.0), [SC], ["Vones"])
            r0 = lambda r: min(max(r - 4, 0), 24)
            VON = [] if "novon" in SKIP else ["Vones"]
            it = 0
            sbctr = 0
            scale = 64 ** -0.5
            for c in ([1, 0, 3, 2] if "corder" in SKIP else R4):
                qw, qk = load_w(wcol(128 * c), 8)
                kw, kk = load_w(wcol(512 + 128 * c), 8)
                vw, vk = load_w(wcol(1024 + 128 * c), 8)
                for tt in R4:
                    ts = slice(tt * 512, (tt + 1) * 512)
                    mm(bank(2 + tt), [(qw[:, k, :], XB[:, k, ts]) for k in R8], [qk] + XBk(R8, [tt]), [PB(2 + tt)])
                    A(lambda e, tt=tt, ts=ts: e.activation(QT[:, ts], bank(2 + tt), AF.Copy), [PB(2 + tt), SC], [("QT", tt)])
                for tt in R4:
                    ts = slice(tt * 512, (tt + 1) * 512)
                    mm(bank(2 + tt), [(kw[:, k, :], XB[:, k, ts]) for k in R8], [kk] + XBk(R8, [tt]), [PB(2 + tt)])
                    V(lambda e, tt=tt, ts=ts: e.tensor_copy(KT[:, ts], bank(2 + tt)), [PB(2 + tt), SC], [("KT", tt)])
                for t4 in R4:
                    if "nov" in SKIP:
                        break
                    b = 2 + t4

                    def f(e, b=b, t4=t4, vw=vw):
                        for j in R4:
                            tc = 4 * t4 + j
                            for k in R8:
                                ins = e.matmul(ps[:, 512 * b + 128 * j:512 * b + 128 * (j + 1)],
                                               XB[:, k, tc * 128:(tc + 1) * 128], vw[:, k, :], start=(k == 0), stop=(k == 7))
                        return ins
                    T(f, [vk] + XBk(R8, [t4]), [PB(b)])
                    src = bank(b).rearrange("p (a b) -> p a b", a=4)
                    A(lambda e, t4=t4, src=src: e.activation(Vaug[:, 4 * t4:4 * t4 + 4, 0:64], src[:, :, 0:64], AF.Copy),
                      [PB(b), "Vones"], [("Va", t4, 0)])
                    A(lambda e, t4=t4, src=src: e.activation(Vaug[:, 4 * t4:4 * t4 + 4, 192:256], src[:, :, 64:128], AF.Copy),
                      [PB(b), "Vones"], [("Va", t4, 1)])
                if "na1" in SKIP:
                    G(lambda e, c=c: e.memset(ABT[:, c, :], 0.0), [], ABk([c], R4))
                    continue
                for hh in range(2):
                    h = 2 * c + hh
                    p0 = 64 * hh
                    EB = EBs[h % 2]
                    ek = ("EB", h % 2)
                    D(lambda e, EB=EB, h=h: e.dma_start(out=EB, in_=ebtab_d[i, h].rearrange("p (a b) -> p a b", a=15)),
                      ["dw", SC], [ek])
                    for qb in R4:
                        ob = qb % 2
                        OB = bank(ob)
                        krs = list(range(r0(8 * qb), r0(8 * qb + 7) + 8))
                        for ki, kr in enumerate(krs):
                            rows = [r for r in range(8 * qb, 8 * qb + 8) if r0(r) <= kr <= r0(r) + 7]
                            ra, rb = rows[0], rows[-1] + 1
                            n = (rb - ra) * 64
                            pk = 64 * (kr % 2)
                            sbk = 2 + sbctr % 6
                            sbctr += 1
                            SBt = ps[pk:pk + 64, 512 * sbk:512 * sbk + n]
                            mm(SBt, [(KT[p0:p0 + 64, kr * 64:(kr + 1) * 64], QT[p0:p0 + 64, ra * 64:rb * 64])],
                               [("KT", kr // 8), ("QT", qb)], [PB(sbk)])
                            et = etm[it % 2]
                            etk = ("etm", it % 2)
                            pt = PTs[it % NPT]
                            ptk = ("PT", it % NPT)
                            it += 1
                            A(lambda e, et=et, SBt=SBt, pk=pk, n=n: e.activation(et[pk:pk + 64, 0:n], SBt, AF.Exp, scale=scale),
                              [PB(sbk), SC], [etk])
                            t_hi = kr - ra + 7
                            nr = rb - ra
                            ebv = EB[pk:pk + 64, t_hi - nr + 1:t_hi + 1, :][:, ::-1, :]
                            V(lambda e, pt=pt, et=et, pk=pk, n=n, ebv=ebv: e.tensor_tensor(
                                pt[pk:pk + 64, 0:n].rearrange("p (a b) -> p a b", b=64),
                                et[pk:pk + 64, 0:n].rearrange("p (a b) -> p a b", b=64), ebv, ALU.mult),
                              [etk, ek], [ptk])
                            tcv = kr // 2

                            def st2(ob=ob, ra=ra, rb=rb, qb=qb, pk=pk, tcv=tcv, hh=hh, pt=pt, n=n, ki=ki, nkr=len(krs), ptk=ptk):
                                T(lambda e: e.matmul(
                                    ps[:, 512 * ob + (ra - 8 * qb) * 64:512 * ob + (rb - 8 * qb) * 64],
                                    Vaug[pk:pk + 64, tcv, 128 * hh:128 * hh + 128], pt[pk:pk + 64, 0:n],
                                    start=(ki == 0), stop=(ki == nkr - 1)),
                                  [ptk, ("Va", tcv // 4, hh), "Vones"], [PB(ob)])
                            pend.append(st2)
                            while len(pend) > SKEW:
                                pend.pop(0)()

                        def st3(ob=ob, OB=OB, p0=p0, c=c, qb=qb):
                            rc = rcs[ob]
                            rk = ("rc", ob)
                            dn = 64 - p0
                            A(lambda e: e.activation(rc[p0:p0 + 64, :], OB[dn:dn + 64, :], AF.Ln), [PB(ob), SC], [rk])
                            A(lambda e: e.activation(rc[p0:p0 + 64, :], rc[p0:p0 + 64, :], AF.Exp, scale=-1.0), [rk], [rk])
                            V(lambda e: e.tensor_tensor(
                                ABT[p0:p0 + 64, c, qb * 512:(qb + 1) * 512], OB[p0:p0 + 64, :], rc[p0:p0 + 64, :], ALU.mult),
                              [PB(ob), rk], [("AB", c, qb)])
                        pend.append(st3)
                while pend:
                    pend.pop(0)()
            w_out_phase("w_out_even", i)

        def odd_mixer(i):
            win = wbf["w_in_odd"]
            wcol = lambda c0: win[i, :, c0:c0 + 128].rearrange("(k p) j -> p k j", p=128)
            fence()
            cqn = _carve(scr, 0, BF16, [2, 2048])
            ckvn = _carve(scr, 8192, BF16, [2048])
            kr_t = _carve(scr, 12288, BF16, [2048])
            sq3 = _carve(scr, 16384, BF16, [3, 512])
            r1 = _carve(scr, 19456, F32, [512])
            r2 = _carve(scr, 21504, F32, [512])
            sgs = [_carve(scr, 23552 + 2048 * j, F32, [512]) for j in range(2)]
            cos_s = _carve(scr, 27648, F32, [512])
            sin_s = _carve(scr, 29696, F32, [512])
            t1 = _carve(scr, 31744, F32, [512])
            t2 = _carve(scr, 33792, F32, [512])
            Dpad = _carve(ABr, 0, BF16, [4, 2078])
            DPK = [("AB", c, t) for c in range(5) for t in R4]
            wq0, k0 = load_w(wcol(0), 8)
            wq1, k1 = load_w(wcol(128), 8)
            wkv, k2 = load_w(wcol(256), 8)
            for tt in R4:
                ts = slice(tt * 512, (tt + 1) * 512)
                xk = XBk(R8, [tt])
                for j, (wt, wk) in enumerate(((wq0, k0), (wq1, k1), (wkv, k2))):
                    mm(bank(j), [(wt[:, k, :], XB[:, k, ts]) for k in R8], [wk] + xk, [PB(j)])
                A(lambda e: e.activation(sq3, ps[:, 0:1536].rearrange("p (a b) -> p a b", a=3), AF.Square),
                  [PB(0), PB(1), PB(2), SC], ["sq3"])
                mm(bank(3), [(onesb[:], sq3[:, 0, :]), (onesb[:], sq3[:, 1, :])], ["sq3", "onesb"], [PB(3)])
                mm(bank(4), [(onesb[:], sq3[:, 2, :])], ["sq3", "onesb"], [PB(4)])
                A(lambda e: e.activation(r1, bank(3), AF.Ln, bias=float(RMS_EPS), scale=1.0 / 256), [PB(3), SC], ["r1"])
                A(lambda e: e.activation(r1, r1, AF.Exp, scale=-0.5), ["r1"], ["r1"])
                A(lambda e: e.activation(r2, bank(4), AF.Ln, bias=float(RMS_EPS), scale=1.0 / 128), [PB(4), SC], ["r2"])
                A(lambda e: e.activation(r2, r2, AF.Exp, scale=-0.5), ["r2"], ["r2"])
                for j in range(2):
                    V(lambda e, j=j, ts=ts: e.scalar_tensor_tensor(cqn[:, j, ts], bank(j), pc(("qg", i), 1, j), r1,
                                                                   ALU.mult, ALU.mult),
                      [PB(j), "r1", "par"], [("cqn", tt)])
                V(lambda e, ts=ts: e.scalar_tensor_tensor(ckvn[:, ts], bank(2), pc(("kvg", i), 1, 0), r2, ALU.mult, ALU.mult),
                  [PB(2), "r2", "par"], [("ckvn", tt)])
            wA, kA = load_w(wcol(320), 8)
            wB, kB = load_w(wcol(320), 8, extra=[
                (64, 80, win[i, :, 400:416].rearrange("(k p) j -> p k j", p=128)),
                (80, 96, win[i, :, 384:400].rearrange("(k p) j -> p k j", p=128))])
            for tt in R4:
                ts = slice(tt * 512, (tt + 1) * 512)
                xk = XBk(R8, [tt])
                D(lambda e, ts=ts: e.dma_start(out=cos_s[64:96, :], in_=cost_d[64:96, ts]), [SC], ["cos_s"])
                D(lambda e, ts=ts: e.dma_start(out=sin_s[64:96, :], in_=sint_d[64:96, ts]), [SC], ["sin_s"])
                mm(bank(5), [(wA[:, k, :], XB[:, k, ts]) for k in R8], [kA] + xk, [PB(5)])
                mm(bank(6), [(wB[:, k, :], XB[:, k, ts]) for k in R8], [kB] + xk, [PB(6)])
                V(lambda e: e.tensor_tensor(t1[64:96, :], ps[64:96, 512 * 5:512 * 6], cos_s[64:96, :], ALU.mult),
                  [PB(5), "cos_s", SC], ["t1"])
                V(lambda e: e.tensor_tensor(t2[64:96, :], ps[64:96, 512 * 6:512 * 7], sin_s[64:96, :], ALU.mult),
                  [PB(6), "sin_s", SC], ["t2"])
                G(lambda e, ts=ts: e.tensor_tensor(kr_t[64:96, ts], t1[64:96, :], t2[64:96, :], ALU.add),
                  ["t1", "t2"], [("kr", tt)])
            G(lambda e: e.memset(Dpad[:, :, 0:15], 0.0), DPK, DPK)
            G(lambda e: e.memset(Dpad[:, :, 2063:2078], 0.0), DPK, DPK)
            it = 0
            for cc in R4:
                wa, ka = load_w(wcol(416 + 128 * cc), 8)
                wg, kg = load_w(wcol(928 + 128 * cc), 8)
                for tt in R4:
                    ts = slice(tt * 512, (tt + 1) * 512)
                    xk = XBk(R8, [tt])
                    ba, bg = (it % 2) * 2, (it % 2) * 2 + 1
                    sg = sgs[it % 2]
                    sk = ("sg", it % 2)
                    it += 1
                    mm(bank(ba), [(wa[:, k, :], XB[:, k, ts]) for k in R8], [ka] + xk, [PB(ba)])
                    mm(bank(bg), [(wg[:, k, :], XB[:, k, ts]) for k in R8], [kg] + xk, [PB(bg)])
                    A(lambda e, sg=sg, bg=bg: e.activation(sg, bank(bg), AF.Sigmoid), [PB(bg), SC], [sk])
                    V(lambda e, sg=sg, ba=ba, cc=cc, tt=tt: e.tensor_tensor(
                        Dpad[:, cc, 15 + tt * 512:15 + (tt + 1) * 512], bank(ba), sg, ALU.mult),
                      [PB(ba), sk] + DPK, DPK)
            fence()
            diag = _carve(scr, 16384, BF16, [31, 128])
            sq4 = _carve(scr, 24320, BF16, [4, 512])
            m_t = _carve(scr, 28416, F32, [512])
            v_t = _carve(scr, 30464, F32, [512])
            l_t = _carve(scr, 32512, F32, [512])
            c16 = _carve(scr, 34560, BF16, [4, 512])
            Cv = _carve(XBr, 0, F32, [4, 2048])
            CVK = lambda cc, tt: [("XB", 2 * cc, tt), ("XB", 2 * cc + 1, tt)]
            CVA = [k for cc in R4 for tt in R4 for k in CVK(cc, tt)]
            bctr = 0
            diag2 = _carve(ABr, 20480, BF16, [31, 128])
            D2K = [("AB", c_, t_) for c_ in (5, 6) for t_ in R4]
            for cc in R4:
                cwv = pc(("cw", i), 31, cc * 31)
                dg = diag if cc % 2 == 0 else diag2
                dgk = ["diag"] if cc % 2 == 0 else D2K
                G(lambda e, cwv=cwv, dg=dg: e.tensor_tensor(
                    dg, identb[:].unsqueeze(1).broadcast_to([128, 31, 128]),
                    cwv.unsqueeze(2).broadcast_to([128, 31, 128]), ALU.mult),
                  ["identb", "par", SC], dgk)
                for tt in R4:
                    b = bctr % 4
                    bctr += 1
                    mm(bank(b), [(dg[:, j, :], Dpad[:, cc, tt * 512 + j:tt * 512 + j + 512]) for j in range(31)],
                       dgk + DPK, [PB(b)])
                    A(lambda e, cc=cc, tt=tt, b=b: e.activation(Cv[:, cc, tt * 512:(tt + 1) * 512], bank(b), AF.Identity,
                                                                bias=pc(("cb", i), 1, cc)),
                      [PB(b), "par"], CVA if (cc == 0 and tt == 0) else CVK(cc, tt))
            for tt in R4:
                ts = slice(tt * 512, (tt + 1) * 512)
                bA, bB = 4 + 2 * (tt % 2), 5 + 2 * (tt % 2)
                ck = [k for cc in R4 for k in CVK(cc, tt)]
                A(lambda e, ts=ts: e.activation(sq4, Cv[:, :, ts], AF.Square), ck + [SC], ["sq4"])
                G(lambda e, ts=ts: e.tensor_copy(c16, Cv[:, :, ts]), ck + [SC], ["c16"])
                mm(bank(bA), [(onesb[:], c16[:, c, :]) for c in R4], ["c16", "onesb"], [PB(bA)])
                mm(bank(bB), [(onesb[:], sq4[:, c, :]) for c in R4], ["sq4", "onesb"], [PB(bB)])
                A(lambda e, bA=bA: e.activation(m_t, bank(bA), AF.Identity, scale=1.0 / 512), [PB(bA), SC], ["m_t"])
                V(lambda e: e.tensor_tensor(v_t, m_t, m_t, ALU.mult), ["m_t", SC], ["v_t"])
                V(lambda e, bB=bB: e.scalar_tensor_tensor(v_t, bank(bB), 1.0 / 512, v_t, ALU.mult, ALU.subtract),
                  [PB(bB), "v_t"], ["v_t"])
                A(lambda e: e.activation(l_t, v_t, AF.Ln, bias=float(LN_EPS)), ["v_t", SC], ["l_t"])
                A(lambda e, bA=bA: e.activation(bank(bA), l_t, AF.Exp, scale=-0.5), ["l_t", "m_t"], [PB(bA)])
                V(lambda e, bA=bA, bB=bB: e.scalar_tensor_tensor(bank(bB), m_t, -1.0, bank(bA), ALU.mult, ALU.mult),
                  ["m_t", PB(bA), "v_t"], [PB(bB)])
                V(lambda e, ts=ts, bA=bA: e.tensor_tensor(
                    Cv[:, :, ts], Cv[:, :, ts], bank(bA).unsqueeze(1).broadcast_to([128, 4, 512]), ALU.mult),
                  ck + [PB(bA), "sq4", "c16"], ck)
                V(lambda e, ts=ts, bB=bB: e.tensor_tensor(
                    Cv[:, :, ts], Cv[:, :, ts], bank(bB).unsqueeze(1).broadcast_to([128, 4, 512]), ALU.add),
                  ck + [PB(bB)], ck)
                for cc in R4:
                    A(lambda e, cc=cc, ts=ts: e.activation(ABT[:, 4 + cc, ts], Cv[:, cc, ts], AF.Silu,
                                                           bias=pc(("clb", i), 1, cc), scale=pc(("clg", i), 1, cc)),
                      CVK(cc, tt) + ["par"] + (DPK if (cc == 0) else []), [("AB", 4 + cc, tt)])
            fence()
            QT = _carve(scr, 16384, BF16, [2048])
            KT = _carve(scr, 20480, BF16, [2048])
            VA = [_carve(scr, 24576 + 4096 * j, BF16, [16, 128]) for j in range(2)]
            PT2 = [_carve(scr, 32768 + 2048 * j, BF16, [1024]) for j in range(3)]
            pend = []
            cosF = _carve(XBr, 0, F32, [2048])
            sinF = _carve(XBr, 8192, F32, [2048])
            rcs = [_carve(XBr, 16384 + 2048 * j, F32, [512]) for j in range(2)]
            u1 = _carve(XBr, 20480, F32, [512])
            u2 = _carve(XBr, 22528, F32, [512])
            XBall = XBk(R8, R4)
            D(lambda e: e.dma_start(out=cosF[64:96, :], in_=cost_d[64:96, :]), XBall, XBall)
            D(lambda e: e.dma_start(out=sinF[64:96, :], in_=sint_d[64:96, :]), XBall, XBall)
            TBL = [("XB", 0, 0)]
            G(lambda e: e.memset(VA[0][:, :, 64:128], 1.0), [SC], ["VAones0"])
            G(lambda e: e.memset(VA[1][:, :, 0:64], 1.0), [SC], ["VAones1"])
            scale = 96 ** -0.5
            it = 0
            sbctr = 0
            for h in R8:
                hh = h % 2
                p0 = 64 * hh
                c = h // 2
                uq = wbf["w_uq"]
                wa, ka = load_w(uq[i, :, 96 * h:96 * h + 96].rearrange("(k p) j -> p k j", p=128), 2, ncol=96)
                wb_, kb = load_w(uq[i, :, 96 * h:96 * h + 96].rearrange("(k p) j -> p k j", p=128), 2, ncol=96, extra=[
                    (64, 80, uq[i, :, 96 * h + 80:96 * h + 96].rearrange("(k p) j -> p k j", p=128)),
                    (80, 96, uq[i, :, 96 * h + 64:96 * h + 80].rearrange("(k p) j -> p k j", p=128))])
                wkv_, kkv = load_w(wbf["w_ukv"][i, :, 128 * h:128 * h + 128].rearrange("(k p) j -> p k j", p=128), 1)
                for tt in R4:
                    ts = slice(tt * 512, (tt + 1) * 512)
                    mm(ps[0:96, 512 * 2:512 * 3], [(wa[:, k, :], cqn[:, k, ts]) for k in range(2)], [ka, ("cqn", tt)], [PB(2)])
                    mm(ps[0:96, 512 * 3:512 * 4], [(wb_[:, k, :], cqn[:, k, ts]) for k in range(2)], [kb, ("cqn", tt)], [PB(3)])
                    mm(ps[0:64, 512 * 4:512 * 5], [(wkv_[:, 0, 0:64], ckvn[:, ts])], [kkv, ("ckvn", tt)], [PB(4)])
                    V(lambda e, ts=ts: e.tensor_copy(QT[0:64, ts], ps[0:64, 1024:1536]), [PB(2), SC], [("QTm", tt, 0)])
                    V(lambda e, ts=ts: e.tensor_tensor(u1[64:96, :], ps[64:96, 1024:1536], cosF[64:96, ts], ALU.mult),
                      [PB(2)] + TBL, ["u1"])
                    V(lambda e, ts=ts: e.tensor_tensor(u2[64:96, :], ps[64:96, 1536:2048], sinF[64:96, ts], ALU.mult),
                      [PB(3)] + TBL, ["u2"])
                    G(lambda e, ts=ts: e.tensor_tensor(QT[64:96, ts], u1[64:96, :], u2[64:96, :], ALU.add),
                      ["u1", "u2", SC], [("QTm", tt, 1)])
                    A(lambda e, ts=ts: e.activation(KT[0:64, ts], ps[0:64, 2048:2560], AF.Copy), [PB(4), SC], [("KTm", tt, 0)])
                    G(lambda e, ts=ts: e.tensor_copy(KT[64:96, ts], kr_t[64:96, ts]), [("kr", tt), SC], [("KTm", tt, 1)])
                va = VA[hh]
                for t8 in range(2):
                    b = 5

                    def f(e, t8=t8, b=b, wkv_=wkv_):
                        for j in range(8):
                            tc = 8 * t8 + j
                            ins = e.matmul(ps[:, 512 * b + 64 * j:512 * b + 64 * (j + 1)], ckvn[:, tc * 128:(tc + 1) * 128],
                                           wkv_[:, 0, 64:128], start=True, stop=True)
                        return ins
                    T(f, [kkv, ("ckvn", 2 * t8), ("ckvn", 2 * t8 + 1)], [PB(b)])
                    src = bank(b).rearrange("p (a b) -> p a b", a=8)
                    V(lambda e, va=va, t8=t8, src=src, hh=hh: e.tensor_copy(
                        va[:, 8 * t8:8 * t8 + 8, 64 * hh:64 * hh + 64], src),
                      [PB(b), SC, "VAones%d" % hh], [("VA", hh, t8)])
                for tt in R4:
                    ts = slice(tt * 512, (tt + 1) * 512)
                    ob = tt % 2
                    OB = bank(ob)
                    for j2 in range(8):
                        sb0 = 2 + 2 * (sbctr % 3)
                        sbctr += 1
                        for u in range(2):
                            kc = 2 * j2 + u
                            mm(bank(sb0 + u), [(KT[0:96, kc * 128:(kc + 1) * 128], QT[0:96, ts])],
                               [("KTm", kc // 4, 0), ("KTm", kc // 4, 1), ("QTm", tt, 0), ("QTm", tt, 1)], [PB(sb0 + u)])
                        pt = PT2[it % 3]
                        ptk = ("PT", it % 3)
                        it += 1
                        A(lambda e, pt=pt, sb0=sb0: e.activation(pt, ps[:, 512 * sb0:512 * sb0 + 1024], AF.Exp, scale=scale),
                          [PB(sb0), PB(sb0 + 1), SC], [ptk])

                        def st2(OB=OB, va=va, j2=j2, pt=pt, ptk=ptk, hh=hh, ob=ob):
                            def f(e):
                                for u in range(2):
                                    kc = 2 * j2 + u
                                    ins = e.matmul(OB, va[:, kc, :], pt[:, 512 * u:512 * (u + 1)], start=(kc == 0), stop=(kc == 15))
                                return ins
                            T(f, [ptk, ("VA", hh, j2 // 4), "VAones%d" % hh], [PB(ob)])
                        pend.append(st2)
                        while len(pend) > 2:
                            pend.pop(0)()

                    def st3(ob=ob, OB=OB, p0=p0, c=c, ts=ts, tt=tt):
                        rc = rcs[ob]
                        rk = ("rc", ob)
                        dn = 64 - p0
                        A(lambda e: e.activation(rc[p0:p0 + 64, :], OB[dn:dn + 64, :], AF.Ln), [PB(ob)] + TBL, [rk])
                        A(lambda e: e.activation(rc[p0:p0 + 64, :], rc[p0:p0 + 64, :], AF.Exp, scale=-1.0), [rk], [rk])
                        V(lambda e: e.tensor_tensor(ABT[p0:p0 + 64, c, ts], OB[p0:p0 + 64, :], rc[p0:p0 + 64, :], ALU.mult),
                          [PB(ob), rk], [("AB", c, tt)])
                    pend.append(st3)
                while pend:
                    pend.pop(0)()
            G(lambda e: e.memset(dumt[:, 1:2], 0.0), ["u1", "u2", ("rc", 0), ("rc", 1)] + TBL, XBall)
            w_out_phase("w_out_odd", i)

        PLAN = os.environ.get("KPLAN", "")
        if PLAN:
            for j, tok in enumerate(PLAN.split(",")):
                S.epoch = 1 + j // 2
                if tok[0] == "L":
                    load_x(int(tok[1]), j == 0)
                else:
                    store_y(int(tok[1]))
            nseq = 0
        for s in range(nseq):
            S.epoch = (s + 1) if 'noepoch' not in SKIP else 0
            load_x(s, s == 0)
            for l in range(nlayers):
                if l % 2 == 0:
                    even_mixer(l // 2)
                else:
                    odd_mixer(l // 2)
                ln_phase("ln1g", "ln1b", l)
                if do_ffn:
                    ffn_phase(l)
                    ln_phase("ln2g", "ln2b", l)
            store_y(s)
        S.emit()
    return nc


_NC_CACHE = {}


def _get_nc(nseq, nlayers=DEPTH, do_ffn=True):
    k = (nseq, nlayers, do_ffn)
    if k not in _NC_CACHE:
        _NC_CACHE[k] = build_nc(nseq, nlayers, do_ffn)
    return _NC_CACHE[k]


def _common_inputs(inp):
    c = _consts()
    d = {n: np.ascontiguousarray(inp[n], dtype=np.float32) for n, _ in WSPECS if n != "sgu_wT"}
    d["sgu_wT"] = np.ascontiguousarray(np.transpose(inp["sgu_w"], (0, 1, 3, 2)), dtype=np.float32)
    rp = np.zeros((2, 8, 15, 127), np.float32)
    rp[..., 48:79] = inp["rpb"]
    d["rpb_pad"] = rp
    d["sgu_ln_g"] = np.ascontiguousarray(inp["sgu_ln_g"], dtype=np.float32)
    d["sgu_ln_b"] = np.ascontiguousarray(inp["sgu_ln_b"], dtype=np.float32)
    d["sgu_b"] = np.ascontiguousarray(np.asarray(inp["sgu_b"], dtype=np.float32).reshape(2, 512))
    d["params"] = _pack_params(inp)
    d.update(c)
    return d


def kernel(**inputs):
    inp = {k: np.asarray(v) for k, v in inputs.items()}
    xs = np.concatenate([inp["x_prompt"], inp["x_sample"]], axis=0).astype(np.float32, copy=False)
    nseq = xs.shape[0] // N_CORES
    nc = _get_nc(nseq)
    common = _common_inputs(inp)
    in_maps = []
    for c in range(N_CORES):
        m = dict(common)
        m["x"] = np.ascontiguousarray(xs[c * nseq:(c + 1) * nseq])
        in_maps.append(m)
    res = run_bass_kernel_spmd(nc, in_maps, core_ids=list(range(N_CORES)))
    ys = np.concatenate([r["y"] for r in res.results], axis=0)
    nb = inp["x_prompt"].shape[0]
    return (np.ascontiguousarray(ys[:nb]), np.ascontiguousarray(ys[nb:]))
```

```python
import contextlib
import numpy as np
import concourse.bass as bass
import concourse.mybir as mybir
from concourse.bass_utils import run_bass_kernel_spmd

F32 = mybir.dt.float32
BF16 = mybir.dt.bfloat16
U8 = mybir.dt.uint8
AF = mybir.ActivationFunctionType
ALU = mybir.AluOpType

S_LEN = 2048
DM = 1024
DEPTH = 4
DFF = 2816
ALPHA = float((2 * DEPTH) ** 0.25)
LN_EPS = 1e-5
RMS_EPS = 1e-6
N_CORES = 8


class _Op:
    __slots__ = ("eng", "fn", "deps", "signal", "count", "is_dma", "lane", "lane_k", "epoch")

    def __init__(self, eng, fn, is_dma):
        self.eng = eng
        self.fn = fn
        self.deps = []
        self.signal = False
        self.count = 0
        self.is_dma = is_dma
        self.lane = 0
        self.lane_k = 0
        self.epoch = 0


class Sched:
    ENG_ATTR = {"pe": "tensor", "act": "scalar", "dve": "vector", "pool": "gpsimd", "sp": "sync"}

    def __init__(self, nc, n_lanes=8, same_eng_sync=True):
        self.nc = nc
        self.ops = {k: [] for k in self.ENG_ATTR}
        self.res_w = {}
        self.res_r = {}
        self.n_lanes = n_lanes
        self.dma_cnt = {k: 0 for k in self.ENG_ATTR}
        self.same_eng_sync = same_eng_sync
        self.epoch = 0

    def _record(self, o, reads, writes):
        o.epoch = self.epoch
        deps = {}
        res_w, res_r = self.res_w, self.res_r
        for r in reads:
            w = res_w.get(r)
            if w is not None:
                deps[id(w)] = w
        for r in writes:
            w = res_w.get(r)
            if w is not None:
                deps[id(w)] = w
            rr = res_r.get(r)
            if rr:
                for x in rr.values():
                    deps[id(x)] = x
        eng = o.eng
        dl = []
        for d in deps.values():
            if d is o:
                continue
            if d.eng == eng and not d.is_dma and not o.is_dma:
                if eng == "pe" or not self.same_eng_sync:
                    continue
            dl.append(d)
        o.deps = dl
        for r in writes:
            res_w[r] = o
            res_r[r] = {}
        for r in reads:
            rr = res_r.get(r)
            if rr is None:
                rr = res_r[r] = {}
            rr[eng if not o.is_dma else (eng, o.lane)] = o
        self.ops[eng].append(o)
        return o

    def op(self, eng, fn, reads=(), writes=()):
        return self._record(_Op(eng, fn, False), reads, writes)

    def dma(self, eng, fn, reads=(), writes=()):
        o = _Op(eng, fn, True)
        i = self.dma_cnt[eng]
        self.dma_cnt[eng] = i + 1
        o.lane = i % self.n_lanes
        o.lane_k = i // self.n_lanes + 1
        return self._record(o, reads, writes)

    def emit(self):
        import os
        TRACE = bool(os.environ.get("KTRACE"))
        nc = self.nc
        for eng, lst in self.ops.items():
            for o in lst:
                for d in o.deps:
                    d.signal = True
        with contextlib.ExitStack() as st:
            sems = {}
            for eng in self.ops:
                for ep in sorted(set(o.epoch for o in self.ops[eng])):
                    sems[(eng, ep)] = st.enter_context(nc.semaphore("s_%s_%d" % (eng, ep)))
            lanes = {}
            for eng in self.ops:
                if self.dma_cnt[eng] > 0:
                    lanes[eng] = [st.enter_context(nc.semaphore("l_%s%d" % (eng, i)))
                                  for i in range(min(self.n_lanes, self.dma_cnt[eng]))]
            for eng, lst in self.ops.items():
                cc = {}
                for o in lst:
                    if not o.is_dma and o.signal:
                        c = cc.get(o.epoch, 0) + 1
                        cc[o.epoch] = c
                        o.count = c
            block = st.enter_context(nc.Block())

            def siginfo(d):
                if d.is_dma:
                    return lanes[d.eng][d.lane], 16 * d.lane_k
                return sems[(d.eng, d.epoch)], d.count

            def body_for(eng):
                lst = self.ops[eng]

                def body(e):
                    waited = {}
                    last_dma = {}
                    for oi, o in enumerate(lst):
                        for d in o.deps:
                            s, v = siginfo(d)
                            k = id(s)
                            if waited.get(k, 0) < v:
                                e.wait_ge(s, v)
                                waited[k] = v
                                if TRACE:
                                    print("  %s op%d waits %s(ep%d lane%d) >= %d" % (eng, oi, d.eng, d.epoch, d.lane if d.is_dma else -1, v))
                        if TRACE:
                            print("%s op%d %s signal=%s count=%d ep=%d lane=%d k=%d" % (eng, oi, "DMA" if o.is_dma else "OP", o.signal, o.count, o.epoch, o.lane, o.lane_k))
                        if o.is_dma:
                            s = lanes[eng][o.lane]
                            if o.lane_k > 1:
                                k = id(s)
                                v = 16 * (o.lane_k - 1)
                                if waited.get(k, 0) < v:
                                    e.wait_ge(s, v)
                                    waited[k] = v
                            ins = o.fn(e)
                            ins.then_inc(s, 16)
                            last_dma[o.lane] = o
                        else:
                            ins = o.fn(e)
                            if o.signal:
                                ins.then_inc(sems[(eng, o.epoch)], 1)
                    for lane, o in last_dma.items():
                        s = lanes[eng][lane]
                        v = 16 * o.lane_k
                        if waited.get(id(s), 0) < v:
                            e.wait_ge(s, v)
                return body

            for eng, attr in self.ENG_ATTR.items():
                if self.ops[eng]:
                    getattr(block, attr)(body_for(eng))


def _carve(t, off, dt, dims):
    n = int(np.prod(dims))
    sz = 2 if dt == BF16 else 4
    ap = t[:, off:off + n * sz].bitcast(dt)
    if len(dims) == 2:
        ap = ap.rearrange("p (a b) -> p a b", a=dims[0])
    elif len(dims) == 3:
        ap = ap.rearrange("p (a b c) -> p a b c", a=dims[0], b=dims[1])
    return ap


def _param_layout():
    off = {}
    n = 0
    for l in range(DEPTH):
        for nm, w in (("ln1g", 8), ("ln1b", 8), ("ln2g", 8), ("ln2b", 8), ("fcw", 132), ("fcb", 44)):
            off[(nm, l)] = n
            n += w
    for i in range(2):
        for nm, w in (("qg", 2), ("kvg", 1), ("cw", 124), ("cb", 4), ("clg", 4), ("clb", 4)):
            off[(nm, i)] = n
            n += w
    return off, n


POFF, NPAR = _param_layout()


def _pack_params(inp):
    P = np.zeros((128, NPAR), np.float32)

    def put(key, arr):
        a = np.ascontiguousarray(arr, dtype=np.float32).reshape(128, -1)
        P[:, POFF[key]:POFF[key] + a.shape[1]] = a

    for l in range(DEPTH):
        put(("ln1g", l), inp["ln1_g"][l].reshape(8, 128).T)
        put(("ln1b", l), inp["ln1_b"][l].reshape(8, 128).T)
        put(("ln2g", l), inp["ln2_g"][l].reshape(8, 128).T)
        put(("ln2b", l), inp["ln2_b"][l].reshape(8, 128).T)
        put(("fcw", l), inp["ffn_conv_w"][l].reshape(3, 44, 128).transpose(2, 1, 0))
        put(("fcb", l), inp["ffn_conv_b"][l].reshape(44, 128).T)
    for i in range(2):
        put(("qg", i), inp["q_norm_g"][i].reshape(2, 128).T)
        put(("kvg", i), inp["kv_norm_g"][i].reshape(1, 128).T)
        put(("cw", i), inp["conv_w"][i].reshape(31, 4, 128).transpose(2, 1, 0))
        put(("cb", i), inp["conv_b"][i].reshape(4, 128).T)
        put(("clg", i), inp["conv_ln_g"][i].reshape(4, 128).T)
        put(("clb", i), inp["conv_ln_b"][i].reshape(4, 128).T)
    return P


def _consts():
    c = {}
    c["ident"] = np.eye(128, dtype=np.float32)
    cq = np.arange(64)
    c0 = np.clip(cq - 8, 0, 48)
    ck = np.arange(64)
    m = ((ck[:, None] >= c0[None, :]) & (ck[:, None] < c0[None, :] + 16)).astype(np.float32)
    c["namask"] = np.concatenate([m, m], axis=0)
    pos = np.arange(S_LEN, dtype=np.float32)
    inv = (10000.0 ** (-np.arange(0, 32, 2, dtype=np.float32) / 32)).astype(np.float32)
    ang = (pos[:, None] * inv[None, :]).astype(np.float32)
    cos = np.cos(ang).astype(np.float32).T
    sin = np.sin(ang).astype(np.float32).T
    ct = np.zeros((128, S_LEN), np.float32)
    sn = np.zeros((128, S_LEN), np.float32)
    ct[64:80] = cos
    ct[80:96] = cos
    sn[64:80] = -sin
    sn[80:96] = sin
    c["cost"] = ct
    c["sint"] = sn
    return c


WSPECS = [
    ("w_in_even", [2, 1024, 2560]), ("w_out_even", [2, 1024, 1024]), ("w_in_odd", [2, 1024, 1440]),
    ("w_uq", [2, 256, 768]), ("w_ukv", [2, 128, 1024]), ("w_out_odd", [2, 1024, 1024]),
    ("ffn_w_up", [4, 1024, 5632]), ("ffn_w_down", [4, 2816, 1024]), ("sgu_wT", [2, 4, 128, 128]),
]


def build_nc(nseq, nlayers=DEPTH, do_ffn=True):
    nc = bass.Bass("TRN2", target_bir_lowering=False)
    din = lambda n, s, d=F32: nc.dram_tensor(n, list(s), d, kind="ExternalInput").ap()
    x_d = din("x", [nseq, S_LEN, DM])
    y_d = nc.dram_tensor("y", [nseq, S_LEN, DM], F32, kind="ExternalOutput").ap()
    wsrc = {n: din(n, s) for n, s in WSPECS}
    wbf = {n: nc.dram_tensor(n + "_bf", list(s), BF16, kind="Internal").ap() for n, s in WSPECS}
    rpbpad_d = din("rpb_pad", [2, 8, 15, 127])
    sgu_lng_d = din("sgu_ln_g", [2, 512])
    sgu_lnb_d = din("sgu_ln_b", [2, 512])
    sgu_b_d = din("sgu_b", [2, 512])
    params_d = din("params", [128, NPAR])
    ident_d = din("ident", [128, 128])
    namask_d = din("namask", [128, 64])
    cost_d = din("cost", [128, S_LEN])
    sint_d = din("sint", [128, S_LEN])
    ebtab_d = nc.dram_tensor("ebtab", [2, 8, 128, 960], F32, kind="Internal").ap()
    dummy_d = nc.dram_tensor("dummy_scr", [1, 64], F32, kind="Internal").ap()

    import os
    DBG = os.environ.get("KDBG", "")
    if DBG:
        dbg_d = nc.dram_tensor("dbg", [128, 16384], BF16, kind="ExternalOutput").ap()
    S = Sched(nc, n_lanes=int(os.environ.get('KLANES', '8')))
    NSLOT = 6
    SLOTW = 1408
    SCRB = 40960

    with contextlib.ExitStack() as st:
        sb = lambda n, s, d: st.enter_context(nc.sbuf_tensor(n, s, d))
        XT = sb("XT", [128, 8, S_LEN], F32)
        XBr = sb("XBr", [128, 32768], U8)
        ABr = sb("ABr", [128, 32768], U8)
        wsl = sb("wsl", [128, NSLOT, SLOTW], BF16)
        scr = sb("scr", [128, SCRB], U8)
        par = sb("par", [128, NPAR], F32)
        identf = sb("identf", [128, 128], F32)
        identb = sb("identb", [128, 128], BF16)
        onesb = sb("onesb", [128, 128], BF16)
        onesf = sb("onesf", [1, 128], F32)
        dumt = sb("dumt", [128, 2], F32)
        ps = st.enter_context(nc.psum_tensor("ps", [128, 4096], F32))

        XB = _carve(XBr, 0, BF16, [8, S_LEN])
        ABT = _carve(ABr, 0, BF16, [8, S_LEN])
        bank = lambda b: ps[:, 512 * b:512 * (b + 1)]
        PB = lambda b: ("ps", b)
        SC = "SCR"

        def pc(key, w=1, j=0):
            o = POFF[key] + j
            return par[:, o:o + w]

        A = lambda fn, r=(), w=(): S.op("act", fn, list(r) + [SC], w)
        V = lambda fn, r=(), w=(): S.op("dve", fn, list(r) + [SC], w)
        G = lambda fn, r=(), w=(): S.op("pool", fn, list(r) + [SC], w)
        T = lambda fn, r=(), w=(): S.op("pe", fn, list(r) + [SC], w)
        D = lambda fn, r=(), w=(), nosc=False: S.dma("sp", fn, list(r) + ([] if nosc else [SC]), w)

        def fence():
            S.op("pool", lambda e: e.memset(dumt[:, 0:1], 0.0), [], [SC])

        def mm(out, pairs, r, w):
            def f(e):
                n = len(pairs)
                for i, (l, rh) in enumerate(pairs):
                    ins = e.matmul(out, l, rh, start=(i == 0), stop=(i == n - 1))
                return ins
            T(f, r, w)

        wctr = [0]

        def load_w(src_ap, nk, ncol=128, extra=None):
            slot = wctr[0] % NSLOT
            wctr[0] += 1
            dst = wsl[:, slot, 0:nk * ncol].rearrange("p (k j) -> p k j", j=ncol)
            key = ("w", slot)
            D(lambda e: e.dma_start(out=dst, in_=src_ap), ["dw"], [key], nosc=True)
            if extra:
                for (c0, c1, sap) in extra:
                    D(lambda e, c0=c0, c1=c1, sap=sap: e.dma_start(out=dst[:, :, c0:c1], in_=sap), ["dw"], [key], nosc=True)
            return dst, key

        D(lambda e: e.dma_start(out=par[:], in_=params_d), [], ["par"])
        D(lambda e: e.dma_start(out=identf[:], in_=ident_d), [], ["identf"])
        V(lambda e: e.tensor_copy(identb[:], identf[:]), ["identf"], ["identb"])
        G(lambda e: e.memset(onesb[:], 1.0), [], ["onesb"])
        G(lambda e: e.memset(onesf[:], 1.0), [], ["onesf"])

        st32 = [XT[:, 2 * i:2 * i + 2, :].rearrange("p a b -> p (a b)") for i in range(4)]
        st16 = [_carve(XBr, 8192 * i, BF16, [4096]) for i in range(4)]
        pst = []
        ci = 0
        import os
        SKIP = os.environ.get("KSKIP", "").split(",")
        for name, shp in WSPECS:
            if "conv" in SKIP:
                break
            n = int(np.prod(shp))
            per = n // 128
            letters = " ".join("abcd"[:len(shp)])
            sv = wsrc[name].rearrange("%s -> (%s)" % (letters, letters)).rearrange("(p f) -> p f", p=128)
            dv = wbf[name].rearrange("%s -> (%s)" % (letters, letters)).rearrange("(p f) -> p f", p=128)
            for c0 in range(0, per, 4096):
                w = min(4096, per - c0)
                sl = ci % 4
                D(lambda e, sl=sl, w=w, c0=c0, sv=sv: e.dma_start(out=st32[sl][:, 0:w], in_=sv[:, c0:c0 + w]),
                  [], [("st32", sl)])
                eng = ("act", "dve")[ci % 2]
                if eng == "act":
                    A(lambda e, sl=sl, w=w: e.activation(st16[sl][:, 0:w], st32[sl][:, 0:w], AF.Copy),
                      [("st32", sl)], [("st16", sl)])
                else:
                    S.op(eng, lambda e, sl=sl, w=w: e.tensor_copy(st16[sl][:, 0:w], st32[sl][:, 0:w]),
                         [("st32", sl)], [("st16", sl)])
                k = ("pst", ci)
                D(lambda e, sl=sl, w=w, c0=c0, dv=dv: e.dma_start(out=dv[:, c0:c0 + w], in_=st16[sl][:, 0:w]),
                  [("st16", sl)], [k])
                pst.append(k)
                ci += 1
        hk = _carve(scr, 0, F32, [15, 64])
        eb = _carve(scr, 4096, F32, [15, 64])
        mk = _carve(scr, 8192, F32, [64])
        D(lambda e: e.dma_start(out=mk, in_=namask_d), [], ["mk"])
        for i in range(2):
            if "eb" in SKIP:
                break
            for h in range(8):
                base = ((i * 8 + h) * 15) * 127
                src = bass.AP(tensor=rpbpad_d.tensor, offset=base, ap=[[1, 64], [127, 15], [1, 64]])
                D(lambda e, src=src: e.dma_start(out=hk[0:64], in_=src), [], ["hk0"])
                D(lambda e, src=src: e.dma_start(out=hk[64:128], in_=src), [], ["hk1"])
                A(lambda e: e.activation(hk, hk, AF.Exp), ["hk0", "hk1"], ["hk0", "hk1"])
                V(lambda e: e.tensor_tensor(eb, hk[:, :, ::-1], mk.unsqueeze(1).broadcast_to([128, 15, 64]), ALU.mult),
                  ["hk0", "hk1", "mk"], ["eb"])
                k = ("pst", ci)
                ci += 1
                D(lambda e, i=i, h=h: e.dma_start(out=ebtab_d[i, h].rearrange("p (a b) -> p a b", a=15), in_=eb),
                  ["eb"], [k])
                pst.append(k)
        D(lambda e: e.dma_start(out=dummy_d, in_=ident_d[0:1, 0:64]), pst, ["dw"])
        stage_keys = [("st32", i) for i in range(4)] + [("st16", i) for i in range(4)]

        XTk = lambda cs, tts: [("XT", c, t) for c in cs for t in tts]
        XBk = lambda cs, tts: [("XB", c, t) for c in cs for t in tts]
        ABk = lambda cs, tts: [("AB", c, t) for c in cs for t in tts]
        R8 = range(8)
        R4 = range(4)

        def load_x(s, first):
            fence()
            for tc in range(16):
                sl = tc % 2
                xtok = _carve(scr, 4096 * sl, F32, [1024])
                tt = tc // 4
                D(lambda e, xtok=xtok, tc=tc: e.dma_start(out=xtok, in_=x_d[s, tc * 128:(tc + 1) * 128, :]),
                  [], [("tok", sl, 0), ("tok", sl, 1)] + (["hk0", "hk1", "eb", "mk"] if first else []))
                for half in range(2):
                    b = (tc * 2 + half) % 8

                    def f(e, xtok=xtok, half=half, b=b):
                        for j in range(4):
                            ins = e.transpose(ps[:, 512 * b + 128 * j:512 * b + 128 * (j + 1)],
                                              xtok[:, (4 * half + j) * 128:(4 * half + j + 1) * 128], identf[:])
                        return ins
                    T(f, [("tok", sl, 0), ("tok", sl, 1), "identf"], [PB(b)])
                    cs = range(4 * half, 4 * half + 4)
                    src = bank(b).rearrange("p (a b) -> p a b", a=4)
                    extra = stage_keys if first else []
                    A(lambda e, half=half, tc=tc, src=src: e.activation(
                        XT[:, 4 * half:4 * half + 4, tc * 128:(tc + 1) * 128], src, AF.Copy),
                      [PB(b)], XTk(cs, [tt]) + extra)
                    V(lambda e, half=half, tc=tc: e.tensor_copy(
                        XB[:, 4 * half:4 * half + 4, tc * 128:(tc + 1) * 128],
                        XT[:, 4 * half:4 * half + 4, tc * 128:(tc + 1) * 128]),
                      XTk(cs, [tt]), XBk(cs, [tt]) + extra)

        def store_y(s):
            fence()
            for tc in range(16):
                sl = tc % 2
                ytok = _carve(scr, 4096 * sl, F32, [1024])
                tt = tc // 4
                for half in range(2):
                    b = (tc * 2 + half) % 8

                    def f(e, half=half, b=b, tc=tc):
                        for j in range(4):
                            ins = e.transpose(ps[:, 512 * b + 128 * j:512 * b + 128 * (j + 1)],
                                              XT[:, 4 * half + j, tc * 128:(tc + 1) * 128], identf[:])
                        return ins
                    T(f, XTk(range(4 * half, 4 * half + 4), [tt]) + ["identf"], [PB(b)])
                    if half == 0:
                        A(lambda e, ytok=ytok, b=b: e.activation(ytok[:, 0:512], bank(b), AF.Copy),
                          [PB(b), SC], [("tok", sl, 0)])
                    else:
                        V(lambda e, ytok=ytok, b=b: e.tensor_copy(ytok[:, 512:1024], bank(b)),
                          [PB(b), SC], [("tok", sl, 1)])
                D(lambda e, ytok=ytok, tc=tc: e.dma_start(out=y_d[s, tc * 128:(tc + 1) * 128, :], in_=ytok),
                  [("tok", sl, 0), ("tok", sl, 1)], [("tok", sl, 0), ("tok", sl, 1)])

        def ln_stats(tt, src_fn, nch, scale_n, eps, bA, bB, sq, x16, m_t, v_t, l_t, rkeys, x16keys):
            pass

        def ln_phase(gkey, bkey, l):
            fence()
            if "ln" in SKIP:
                return
            sqs = [_carve(scr, 8192 * j, BF16, [8, 512]) for j in range(2)]
            m_ts = [_carve(scr, 16384 + 2048 * j, F32, [512]) for j in range(2)]
            v_ts = [_carve(scr, 20480 + 2048 * j, F32, [512]) for j in range(2)]
            l_ts = [_carve(scr, 24576 + 2048 * j, F32, [512]) for j in range(2)]

            def s1(tt):
                ts = slice(tt * 512, (tt + 1) * 512)
                bA, bB = 2 * tt, 2 * tt + 1
                sq = sqs[tt % 2]
                sqk = ("sq", tt % 2)
                xk = XTk(R8, [tt])
                A(lambda e: e.activation(sq, XT[:, :, ts], AF.Square), xk, [sqk])
                V(lambda e: e.tensor_copy(XB[:, :, ts], XT[:, :, ts]), xk, XBk(R8, [tt]))
                mm(bank(bA), [(onesb[:], XB[:, c, ts]) for c in R8], XBk(R8, [tt]) + ["onesb"], [PB(bA)])
                mm(bank(bB), [(onesb[:], sq[:, c, :]) for c in R8], [sqk, "onesb"], [PB(bB)])

            def s2(tt):
                bA, bB = 2 * tt, 2 * tt + 1
                m_t, v_t, l_t = m_ts[tt % 2], v_ts[tt % 2], l_ts[tt % 2]
                mk_, vk_, lk_ = ("m_t", tt % 2), ("v_t", tt % 2), ("l_t", tt % 2)
                A(lambda e: e.activation(m_t, bank(bA), AF.Identity, scale=1.0 / DM), [PB(bA)], [mk_])
                A(lambda e: e.activation(v_t, bank(bA), AF.Square, scale=1.0 / DM), [PB(bA)], [vk_])
                V(lambda e: e.scalar_tensor_tensor(v_t, bank(bB), 1.0 / DM, v_t, ALU.mult, ALU.subtract),
                  [PB(bB), vk_], [vk_])
                A(lambda e: e.activation(l_t, v_t, AF.Ln, bias=float(LN_EPS)), [vk_], [lk_])
                A(lambda e: e.activation(bank(bA), l_t, AF.Exp, scale=-0.5), [lk_, mk_], [PB(bA)])
                V(lambda e: e.scalar_tensor_tensor(bank(bB), m_t, -1.0, bank(bA), ALU.mult, ALU.mult),
                  [mk_, PB(bA), vk_], [PB(bB)])

            def s3(tt):
                ts = slice(tt * 512, (tt + 1) * 512)
                bA, bB = 2 * tt, 2 * tt + 1
                xk = XTk(R8, [tt])
                V(lambda e: e.tensor_tensor(
                    XT[:, :, ts], XT[:, :, ts], bank(bA).unsqueeze(1).broadcast_to([128, 8, 512]), ALU.mult),
                  xk + [PB(bA)], xk)
                V(lambda e: e.tensor_tensor(
                    XT[:, :, ts], XT[:, :, ts], bank(bB).unsqueeze(1).broadcast_to([128, 8, 512]), ALU.add),
                  xk + [PB(bB)], xk)

            def s4(tt):
                ts = slice(tt * 512, (tt + 1) * 512)
                for c in R8:
                    gs = pc((gkey, l), 1, c)
                    bs = pc((bkey, l), 1, c)
                    A(lambda e, c=c, gs=gs, bs=bs: e.activation(XT[:, c, ts], XT[:, c, ts], AF.Identity, bias=bs, scale=gs),
                      [("XT", c, tt), "par"], [("XT", c, tt)])
                V(lambda e: e.tensor_copy(XB[:, :, ts], XT[:, :, ts]), XTk(R8, [tt]), XBk(R8, [tt]))

            for st_fn, tt_ in ((s1, 0), (s1, 1), (s2, 0), (s1, 2), (s2, 1), (s3, 0), (s1, 3), (s2, 2), (s3, 1), (s4, 0),
                               (s2, 3), (s3, 2), (s4, 1), (s3, 3), (s4, 2), (s4, 3)):
                st_fn(tt_)

        def w_out_phase(wname, i):
            fence()
            if DBG:
                D(lambda e: e.dma_start(out=dbg_d, in_=_carve(ABr, 0, BF16, [16384])), ABk(R8, R4), [])
            if "wout" in SKIP:
                return
            bctr = 0
            for oc in R8:
                wt, wk = load_w(wbf[wname][i, :, oc * 128:(oc + 1) * 128].rearrange("(k p) j -> p k j", p=128), 8)
                for tt in R4:
                    ts = slice(tt * 512, (tt + 1) * 512)
                    b = bctr % 8
                    bctr += 1
                    mm(bank(b), [(wt[:, k, :], ABT[:, k, ts]) for k in R8], [wk] + ABk(R8, [tt]), [PB(b)])
                    V(lambda e, oc=oc, ts=ts, b=b: e.scalar_tensor_tensor(
                        XT[:, oc, ts], XT[:, oc, ts], ALPHA, bank(b), ALU.mult, ALU.add),
                      [PB(b), ("XT", oc, tt)], [("XT", oc, tt)])

        def ffn_phase(l):
            fence()
            if "ffn" in SKIP:
                return
            groups = [(0, 6), (6, 12), (12, 17), (17, 22)]
            Abuf = ABT
            for gi, (c0, c1) in enumerate(groups):
                for ci_ in range(c0, c1):
                    a = ci_ - c0
                    for half_gv in range(2):
                        col0 = ci_ * 128 + (DFF if half_gv else 0)
                        ch = ci_ + (22 if half_gv else 0)
                        wt, wk = load_w(wbf["ffn_w_up"][l, :, col0:col0 + 128].rearrange("(k p) j -> p k j", p=128), 8)
                        b0 = 4 * half_gv
                        hb = [PB(b0 + j) for j in R4]
                        for tt in R4:
                            ts = slice(tt * 512, (tt + 1) * 512)
                            mm(bank(b0 + tt), [(wt[:, k, :], XB[:, k, ts]) for k in R8],
                               [wk] + XBk(R8, [tt]), [PB(b0 + tt)])
                        H = ps[:, 2048 * half_gv:2048 * (half_gv + 1)]
                        w0 = pc(("fcw", l), 1, ch * 3 + 0)
                        w1 = pc(("fcw", l), 1, ch * 3 + 1)
                        w2 = pc(("fcw", l), 1, ch * 3 + 2)
                        bb = pc(("fcb", l), 1, ch)
                        bufs = [_carve(scr, (8192 if half_gv else 0) + 4096 * hf, F32, [1024]) for hf in range(2)]
                        bks = [("ffb", half_gv, hf) for hf in range(2)]
                        for hf in range(2):
                            lo = 1024 * hf
                            A(lambda e, buf=bufs[hf], lo=lo, H=H, w1=w1, bb=bb: e.activation(
                                buf, H[:, lo:lo + 1024], AF.Identity, bias=bb, scale=w1),
                              [PB(b0 + 2 * hf), PB(b0 + 2 * hf + 1), "par", SC], [bks[hf]])
                        hb01 = [PB(b0), PB(b0 + 1)]
                        for hf in range(2):
                            lo = 1024 * hf
                            buf = bufs[hf]
                            bk = bks[hf]
                            if hf == 0:
                                V(lambda e, buf=buf, H=H, w0=w0: e.scalar_tensor_tensor(
                                    buf[:, 1:1024], H[:, 0:1023], w0, buf[:, 1:1024], ALU.mult, ALU.add),
                                  hb01 + [bks[0], "par"], [bk])
                                V(lambda e, buf=buf, H=H, w2=w2: e.scalar_tensor_tensor(
                                    buf[:, 0:1023], H[:, 1:1024], w2, buf[:, 0:1023], ALU.mult, ALU.add),
                                  hb01 + [bk, "par"], [bk])
                                V(lambda e, buf=buf, H=H, w2=w2: e.scalar_tensor_tensor(
                                    buf[:, 1023:1024], H[:, 1024:1025], w2, buf[:, 1023:1024], ALU.mult, ALU.add),
                                  hb + [bk, bks[1], "par"], [bk])
                            else:
                                V(lambda e, buf=buf, H=H, w0=w0: e.scalar_tensor_tensor(
                                    buf[:, 0:1024], H[:, 1023:2047], w0, buf[:, 0:1024], ALU.mult, ALU.add),
                                  hb + [bks[0], bks[1], "par"], [bk])
                                V(lambda e, buf=buf, H=H, w2=w2: e.scalar_tensor_tensor(
                                    buf[:, 0:1023], H[:, 1025:2048], w2, buf[:, 0:1023], ALU.mult, ALU.add),
                                  hb + [bk, "par"], [bk])
                            if half_gv == 0:
                                A(lambda e, buf=buf: e.activation(buf, buf, AF.Silu), [bk], [bk])
                            else:
                                gbuf = _carve(scr, 4096 * hf, F32, [1024])
                                G(lambda e, buf=buf, gbuf=gbuf, a=a, lo=lo: e.tensor_tensor(
                                    Abuf[:, a, lo:lo + 1024], gbuf, buf, ALU.mult),
                                  [bk, ("ffb", 0, hf)], ABk([a], [2 * hf, 2 * hf + 1]))
                nk = c1 - c0
                bctr = 0
                wts = {}
                for kpass in range(2):
                    ks = list(range(nk - 1)) if kpass == 0 else [nk - 1]
                    for oc in R8:
                        if kpass == 0 or oc >= 4:
                            wt, wk = load_w(wbf["ffn_w_down"][l, c0 * 128:c1 * 128, oc * 128:(oc + 1) * 128]
                                            .rearrange("(k p) j -> p k j", p=128), nk)
                            wts[oc] = (wt, wk)
                        else:
                            wt, wk = load_w(wbf["ffn_w_down"][l, (c1 - 1) * 128:c1 * 128, oc * 128:(oc + 1) * 128]
                                            .rearrange("(k p) j -> p k j", p=128), 1)
                            wt = wt
                            wts[oc] = (None, None)
                        for tt in R4:
                            ts = slice(tt * 512, (tt + 1) * 512)
                            b = bctr % 8
                            bctr += 1
                            if kpass == 1 and oc < 4:
                                pairs = [(wt[:, 0, :], Abuf[:, nk - 1, ts])]
                            elif kpass == 1:
                                pairs = [(wt[:, nk - 1, :], Abuf[:, nk - 1, ts])]
                            else:
                                pairs = [(wt[:, k, :], Abuf[:, k, ts]) for k in ks]
                            mm(bank(b), pairs, [wk] + ABk(ks, [tt]), [PB(b)])
                            sc_ = ALPHA if (gi == 0 and kpass == 0) else 1.0
                            V(lambda e, oc=oc, ts=ts, b=b, sc_=sc_: e.scalar_tensor_tensor(
                                XT[:, oc, ts], XT[:, oc, ts], sc_, bank(b), ALU.mult, ALU.add),
                              [PB(b), ("XT", oc, tt)], [("XT", oc, tt)])

        def even_mixer(i):
            win = wbf["w_in_even"]
            wcol = lambda c0: win[i, :, c0:c0 + 128].rearrange("(k p) j -> p k j", p=128)
            fence()
            if "sgu" in SKIP:
                G(lambda e: e.memset(ABT[:, 4:8, :], 0.0), [], ABk(range(4, 8), R4))
                return na_part(i, wcol)
            vln = _carve(scr, 0, BF16, [16, 512])
            U = _carve(scr, 16384, F32, [2048])
            tmpA = [_carve(scr, 24576 + 2048 * j, F32, [512]) for j in range(2)]
            gbc = _carve(scr, 28672, F32, [512])
            bbc = _carve(scr, 30720, F32, [512])
            sguW = _carve(scr, 32768, BF16, [4, 128])
            stt = _carve(scr, 33792, F32, [16])
            sgub = _carve(scr, 34048, F32, [512])
            D(lambda e: e.dma_start(out=gbc, in_=sgu_lng_d[i, :].partition_broadcast(128)), [SC], ["gbc"])
            D(lambda e: e.dma_start(out=bbc, in_=sgu_lnb_d[i, :].partition_broadcast(128)), [SC], ["bbc"])
            D(lambda e: e.dma_start(out=sguW, in_=wbf["sgu_wT"][i].rearrange("g q p -> q g p")), [SC, "dw"], ["sguW"])
            D(lambda e: e.dma_start(out=sgub[0:1, :], in_=sgu_b_d[i:i + 1, :]), [SC], ["sgub"])
            gw = [load_w(wcol(2048 + 128 * j), 8) for j in R4]
            for tc in range(16):
                b = tc % 4
                tt = tc // 4
                tcs = slice(tc * 128, (tc + 1) * 128)

                def f(e, b=b, tcs=tcs):
                    for j in R4:
                        for k in R8:
                            ins = e.matmul(ps[:, 512 * b + 128 * j:512 * b + 128 * (j + 1)], XB[:, k, tcs],
                                           gw[j][0][:, k, :], start=(k == 0), stop=(k == 7))
                    return ins
                T(f, [g_[1] for g_ in gw] + XBk(R8, [tt]), [PB(b)])
                ta = tmpA[tc % 2]
                tk = ("tmpA", tc % 2)
                A(lambda e, ta=ta, b=b: e.activation(ta, bank(b), AF.Gelu_apprx_tanh), [PB(b), SC], [tk])
                V(lambda e, ta=ta: e.bn_stats(stt[:, 0:6], ta), [tk, SC], ["stt"])
                V(lambda e: e.bn_aggr(stt[:, 6:8], stt[:, 0:6]), ["stt"], ["stt"])
                A(lambda e: e.activation(stt[:, 8:9], stt[:, 7:8], AF.Ln, bias=float(LN_EPS)), ["stt"], ["stt"])
                A(lambda e: e.activation(stt[:, 9:10], stt[:, 8:9], AF.Exp, scale=-0.5), ["stt"], ["stt"])
                V(lambda e: e.scalar_tensor_tensor(stt[:, 10:11], stt[:, 6:7], -1.0, stt[:, 9:10], ALU.mult, ALU.mult),
                  ["stt"], ["stt"])
                A(lambda e, ta=ta: e.activation(ta, ta, AF.Identity, bias=stt[:, 10:11], scale=stt[:, 9:10]),
                  ["stt", tk], [tk])
                V(lambda e, ta=ta: e.tensor_tensor(ta, ta, gbc, ALU.mult), [tk, "gbc"], [tk])
                V(lambda e, ta=ta, tc=tc: e.tensor_tensor(vln[:, tc, :], ta, bbc, ALU.add), [tk, "bbc"], [("vln", tc)])
            for g in R4:
                uw, uk = load_w(wcol(1536 + 128 * g), 8)
                for tt in R4:
                    ts = slice(tt * 512, (tt + 1) * 512)
                    mm(bank(4 + tt), [(uw[:, k, :], XB[:, k, ts]) for k in R8], [uk] + XBk(R8, [tt]), [PB(4 + tt)])
                    A(lambda e, tt=tt, ts=ts: e.activation(U[:, ts], bank(4 + tt), AF.Gelu_apprx_tanh),
                      [PB(4 + tt), SC], [("U", tt)])
                for tt in R4:
                    ts = slice(tt * 512, (tt + 1) * 512)

                    def f(e, g=g, tt=tt):
                        for j in R4:
                            tc = 4 * tt + j
                            o = ps[:, 512 * tt + 128 * j:512 * tt + 128 * (j + 1)]
                            e.matmul(o, vln[:, tc, g * 128:(g + 1) * 128], sguW[:, g, :], start=True, stop=False)
                            ins = e.matmul(o, onesf[0:1, :], sgub[0:1, g * 128:(g + 1) * 128], start=False, stop=True)
                        return ins
                    T(f, [("vln", 4 * tt + j) for j in R4] + ["sguW", "sgub", "onesf"], [PB(tt)])
                    V(lambda e, g=g, tt=tt, ts=ts: e.tensor_tensor(ABT[:, 4 + g, ts], bank(tt), U[:, ts], ALU.mult),
                      [PB(tt), ("U", tt)], [("AB", 4 + g, tt)])
            return na_part(i, wcol)

        def na_part(i, wcol):
            fence()
            if "na" in SKIP:
                G(lambda e: e.memset(ABT[:, 0:4, :], 0.0), [], ABk(R4, R4))
                return w_out_phase("w_out_even", i)
            QT = _carve(scr, 0, BF16, [2048])
            KT = _carve(scr, 4096, BF16, [2048])
            Vaug = _carve(scr, 8192, BF16, [16, 256])
            EBs = [_carve(scr, 16384 + 3840 * j, F32, [15, 64]) for j in range(2)]
            etm = [_carve(scr, 24064 + 2048 * j, F32, [512]) for j in range(2)]
            NPT = 5
            SKEW = int(os.environ.get("KSKEW", "2"))
            pend = []
            PTs = [_carve(scr, 28160 + 1024 * j, BF16, [512]) for j in range(NPT)]
            rcs = [_carve(scr, 33280 + 2048 * j, F32, [512]) for j in range(2)]
            if "nomemset" not in SKIP:
                G(lambda e: e.memset(Vaug[:, :, 64:192], 1.0), [SC], ["Vones"])
            r0 = lambda r: min(max(r - 4, 0), 24)
            VON = [] if "novon" in SKIP else ["Vones"]
            it = 0
            sbctr = 0
            scale = 64 ** -0.5
            for c in ([1, 0, 3, 2] if "corder" in SKIP else R4):
                qw, qk = load_w(wcol(128 * c), 8)
                kw, kk = load_w(wcol(512 + 128 * c), 8)
                vw, vk = load_w(wcol(1024 + 128 * c), 8)
                for tt in R4:
                    ts = slice(tt * 512, (tt + 1) * 512)
                    mm(bank(2 + tt), [(qw[:, k, :], XB[:, k, ts]) for k in R8], [qk] + XBk(R8, [tt]), [PB(2 + tt)])
                    A(lambda e, tt=tt, ts=ts: e.activation(QT[:, ts], bank(2 + tt), AF.Copy), [PB(2 + tt), SC], [("QT", tt)])
                for tt in R4:
                    ts = slice(tt * 512, (tt + 1) * 512)
                    mm(bank(2 + tt), [(kw[:, k, :], XB[:, k, ts]) for k in R8], [kk] + XBk(R8, [tt]), [PB(2 + tt)])
                    V(lambda e, tt=tt, ts=ts: e.tensor_copy(KT[:, ts], bank(2 + tt)), [PB(2 + tt), SC], [("KT", tt)])
                for t4 in R4:
                    if "nov" in SKIP:
                        break
                    b = 2 + t4

                    def f(e, b=b, t4=t4, vw=vw):
                        for j in R4:
                            tc = 4 * t4 + j
                            for k in R8:
                                ins = e.matmul(ps[:, 512 * b + 128 * j:512 * b + 128 * (j + 1)],
                                               XB[:, k, tc * 128:(tc + 1) * 128], vw[:, k, :], start=(k == 0), stop=(k == 7))
                        return ins
                    T(f, [vk] + XBk(R8, [t4]), [PB(b)])
                    src = bank(b).rearrange("p (a b) -> p a b", a=4)
                    A(lambda e, t4=t4, src=src: e.activation(Vaug[:, 4 * t4:4 * t4 + 4, 0:64], src[:, :, 0:64], AF.Copy),
                      [PB(b), "Vones"], [("Va", t4, 0)])
                    A(lambda e, t4=t4, src=src: e.activation(Vaug[:, 4 * t4:4 * t4 + 4, 192:256], src[:, :, 64:128], AF.Copy),
                      [PB(b), "Vones"], [("Va", t4, 1)])
                if "na1" in SKIP:
                    G(lambda e, c=c: e.memset(ABT[:, c, :], 0.0), [], ABk([c], R4))
                    continue
                for hh in range(2):
                    h = 2 * c + hh
                    p0 = 64 * hh
                    EB = EBs[h % 2]
                    ek = ("EB", h % 2)
                    D(lambda e, EB=EB, h=h: e.dma_start(out=EB, in_=ebtab_d[i, h].rearrange("p (a b) -> p a b", a=15)),
                      ["dw", SC], [ek])
                    for qb in R4:
                        ob = qb % 2
                        OB = bank(ob)
                        krs = list(range(r0(8 * qb), r0(8 * qb + 7) + 8))
                        for ki, kr in enumerate(krs):
                            rows = [r for r in range(8 * qb, 8 * qb + 8) if r0(r) <= kr <= r0(r) + 7]
                            ra, rb = rows[0], rows[-1] + 1
                            n = (rb - ra) * 64
                            pk = 64 * (kr % 2)
                            sbk = 2 + sbctr % 6
                            sbctr += 1
                            SBt = ps[pk:pk + 64, 512 * sbk:512 * sbk + n]
                            mm(SBt, [(KT[p0:p0 + 64, kr * 64:(kr + 1) * 64], QT[p0:p0 + 64, ra * 64:rb * 64])],
                               [("KT", kr // 8), ("QT", qb)], [PB(sbk)])
                            et = etm[it % 2]
                            etk = ("etm", it % 2)
                            pt = PTs[it % NPT]
                            ptk = ("PT", it % NPT)
                            it += 1
                            A(lambda e, et=et, SBt=SBt, pk=pk, n=n: e.activation(et[pk:pk + 64, 0:n], SBt, AF.Exp, scale=scale),
                              [PB(sbk), SC], [etk])
                            t_hi = kr - ra + 7
                            nr = rb - ra
                            ebv = EB[pk:pk + 64, t_hi - nr + 1:t_hi + 1, :][:, ::-1, :]
                            V(lambda e, pt=pt, et=et, pk=pk, n=n, ebv=ebv: e.tensor_tensor(
                                pt[pk:pk + 64, 0:n].rearrange("p (a b) -> p a b", b=64),
                                et[pk:pk + 64, 0:n].rearrange("p (a b) -> p a b", b=64), ebv, ALU.mult),
                              [etk, ek], [ptk])
                            tcv = kr // 2

                            def st2(ob=ob, ra=ra, rb=rb, qb=qb, pk=pk, tcv=tcv, hh=hh, pt=pt, n=n, ki=ki, nkr=len(krs), ptk=ptk):
                                T(lambda e: e.matmul(
                                    ps[:, 512 * ob + (ra - 8 * qb) * 64:512 * ob + (rb - 8 * qb) * 64],
                                    Vaug[pk:pk + 64, tcv, 128 * hh:128 * hh + 128], pt[pk:pk + 64, 0:n],
                                    start=(ki == 0), stop=(ki == nkr - 1)),
                                  [ptk, ("Va", tcv // 4, hh), "Vones"], [PB(ob)])
                            pend.append(st2)
                            while len(pend) > SKEW:
                                pend.pop(0)()

                        def st3(ob=ob, OB=OB, p0=p0, c=c, qb=qb):
                            rc = rcs[ob]
                            rk = ("rc", ob)
                            dn = 64 - p0
                            A(lambda e: e.activation(rc[p0:p0 + 64, :], OB[dn:dn + 64, :], AF.Ln), [PB(ob), SC], [rk])
                            A(lambda e: e.activation(rc[p0:p0 + 64, :], rc[p0:p0 + 64, :], AF.Exp, scale=-1.0), [rk], [rk])
                            V(lambda e: e.tensor_tensor(
                                ABT[p0:p0 + 64, c, qb * 512:(qb + 1) * 512], OB[p0:p0 + 64, :], rc[p0:p0 + 64, :], ALU.mult),
                              [PB(ob), rk], [("AB", c, qb)])
                        pend.append(st3)
                while pend:
                    pend.pop(0)()
            w_out_phase("w_out_even", i)

        def odd_mixer(i):
            win = wbf["w_in_odd"]
            wcol = lambda c0: win[i, :, c0:c0 + 128].rearrange("(k p) j -> p k j", p=128)
            fence()
            cqn = _carve(scr, 0, BF16, [2, 2048])
            ckvn = _carve(scr, 8192, BF16, [2048])
            kr_t = _carve(scr, 12288, BF16, [2048])
            sq3 = _carve(scr, 16384, BF16, [3, 512])
            r1 = _carve(scr, 19456, F32, [512])
            r2 = _carve(scr, 21504, F32, [512])
            sgs = [_carve(scr, 23552 + 2048 * j, F32, [512]) for j in range(2)]
            cos_s = _carve(scr, 27648, F32, [512])
            sin_s = _carve(scr, 29696, F32, [512])
            t1 = _carve(scr, 31744, F32, [512])
            t2 = _carve(scr, 33792, F32, [512])
            Dpad = _carve(ABr, 0, BF16, [4, 2078])
            DPK = [("AB", c, t) for c in range(5) for t in R4]
            wq0, k0 = load_w(wcol(0), 8)
            wq1, k1 = load_w(wcol(128), 8)
            wkv, k2 = load_w(wcol(256), 8)
            for tt in R4:
                ts = slice(tt * 512, (tt + 1) * 512)
                xk = XBk(R8, [tt])
                for j, (wt, wk) in enumerate(((wq0, k0), (wq1, k1), (wkv, k2))):
                    mm(bank(j), [(wt[:, k, :], XB[:, k, ts]) for k in R8], [wk] + xk, [PB(j)])
                A(lambda e: e.activation(sq3, ps[:, 0:1536].rearrange("p (a b) -> p a b", a=3), AF.Square),
                  [PB(0), PB(1), PB(2), SC], ["sq3"])
                mm(bank(3), [(onesb[:], sq3[:, 0, :]), (onesb[:], sq3[:, 1, :])], ["sq3", "onesb"], [PB(3)])
                mm(bank(4), [(onesb[:], sq3[:, 2, :])], ["sq3", "onesb"], [PB(4)])
                A(lambda e: e.activation(r1, bank(3), AF.Ln, bias=float(RMS_EPS), scale=1.0 / 256), [PB(3), SC], ["r1"])
                A(lambda e: e.activation(r1, r1, AF.Exp, scale=-0.5), ["r1"], ["r1"])
                A(lambda e: e.activation(r2, bank(4), AF.Ln, bias=float(RMS_EPS), scale=1.0 / 128), [PB(4), SC], ["r2"])
                A(lambda e: e.activation(r2, r2, AF.Exp, scale=-0.5), ["r2"], ["r2"])
                for j in range(2):
                    V(lambda e, j=j, ts=ts: e.scalar_tensor_tensor(cqn[:, j, ts], bank(j), pc(("qg", i), 1, j), r1,
                                                                   ALU.mult, ALU.mult),
                      [PB(j), "r1", "par"], [("cqn", tt)])
                V(lambda e, ts=ts: e.scalar_tensor_tensor(ckvn[:, ts], bank(2), pc(("kvg", i), 1, 0), r2, ALU.mult, ALU.mult),
                  [PB(2), "r2", "par"], [("ckvn", tt)])
            wA, kA = load_w(wcol(320), 8)
            wB, kB = load_w(wcol(320), 8, extra=[
                (64, 80, win[i, :, 400:416].rearrange("(k p) j -> p k j", p=128)),
                (80, 96, win[i, :, 384:400].rearrange("(k p) j -> p k j", p=128))])
            for tt in R4:
                ts = slice(tt * 512, (tt + 1) * 512)
                xk = XBk(R8, [tt])
                D(lambda e, ts=ts: e.dma_start(out=cos_s[64:96, :], in_=cost_d[64:96, ts]), [SC], ["cos_s"])
                D(lambda e, ts=ts: e.dma_start(out=sin_s[64:96, :], in_=sint_d[64:96, ts]), [SC], ["sin_s"])
                mm(bank(5), [(wA[:, k, :], XB[:, k, ts]) for k in R8], [kA] + xk, [PB(5)])
                mm(bank(6), [(wB[:, k, :], XB[:, k, ts]) for k in R8], [kB] + xk, [PB(6)])
                V(lambda e: e.tensor_tensor(t1[64:96, :], ps[64:96, 512 * 5:512 * 6], cos_s[64:96, :], ALU.mult),
                  [PB(5), "cos_s", SC], ["t1"])
                V(lambda e: e.tensor_tensor(t2[64:96, :], ps[64:96, 512 * 6:512 * 7], sin_s[64:96, :], ALU.mult),
                  [PB(6), "sin_s", SC], ["t2"])
                G(lambda e, ts=ts: e.tensor_tensor(kr_t[64:96, ts], t1[64:96, :], t2[64:96, :], ALU.add),
                  ["t1", "t2"], [("kr", tt)])
            G(lambda e: e.memset(Dpad[:, :, 0:15], 0.0), DPK, DPK)
            G(lambda e: e.memset(Dpad[:, :, 2063:2078], 0.0), DPK, DPK)
            it = 0
            for cc in R4:
                wa, ka = load_w(wcol(416 + 128 * cc), 8)
                wg, kg = load_w(wcol(928 + 128 * cc), 8)
                for tt in R4:
                    ts = slice(tt * 512, (tt + 1) * 512)
                    xk = XBk(R8, [tt])
                    ba, bg = (it % 2) * 2, (it % 2) * 2 + 1
                    sg = sgs[it % 2]
                    sk = ("sg", it % 2)
                    it += 1
                    mm(bank(ba), [(wa[:, k, :], XB[:, k, ts]) for k in R8], [ka] + xk, [PB(ba)])
                    mm(bank(bg), [(wg[:, k, :], XB[:, k, ts]) for k in R8], [kg] + xk, [PB(bg)])
                    A(lambda e, sg=sg, bg=bg: e.activation(sg, bank(bg), AF.Sigmoid), [PB(bg), SC], [sk])
                    V(lambda e, sg=sg, ba=ba, cc=cc, tt=tt: e.tensor_tensor(
                        Dpad[:, cc, 15 + tt * 512:15 + (tt + 1) * 512], bank(ba), sg, ALU.mult),
                      [PB(ba), sk] + DPK, DPK)
            fence()
            diag = _carve(scr, 16384, BF16, [31, 128])
            sq4 = _carve(scr, 24320, BF16, [4, 512])
            m_t = _carve(scr, 28416, F32, [512])
            v_t = _carve(scr, 30464, F32, [512])
            l_t = _carve(scr, 32512, F32, [512])
            c16 = _carve(scr, 34560, BF16, [4, 512])
            Cv = _carve(XBr, 0, F32, [4, 2048])
            CVK = lambda cc, tt: [("XB", 2 * cc, tt), ("XB", 2 * cc + 1, tt)]
            CVA = [k for cc in R4 for tt in R4 for k in CVK(cc, tt)]
            bctr = 0
            diag2 = _carve(ABr, 20480, BF16, [31, 128])
            D2K = [("AB", c_, t_) for c_ in (5, 6) for t_ in R4]
            for cc in R4:
                cwv = pc(("cw", i), 31, cc * 31)
                dg = diag if cc % 2 == 0 else diag2
                dgk = ["diag"] if cc % 2 == 0 else D2K
                G(lambda e, cwv=cwv, dg=dg: e.tensor_tensor(
                    dg, identb[:].unsqueeze(1).broadcast_to([128, 31, 128]),
                    cwv.unsqueeze(2).broadcast_to([128, 31, 128]), ALU.mult),
                  ["identb", "par", SC], dgk)
                for tt in R4:
                    b = bctr % 4
                    bctr += 1
                    mm(bank(b), [(dg[:, j, :], Dpad[:, cc, tt * 512 + j:tt * 512 + j + 512]) for j in range(31)],
                       dgk + DPK, [PB(b)])
                    A(lambda e, cc=cc, tt=tt, b=b: e.activation(Cv[:, cc, tt * 512:(tt + 1) * 512], bank(b), AF.Identity,
                                                                bias=pc(("cb", i), 1, cc)),
                      [PB(b), "par"], CVA if (cc == 0 and tt == 0) else CVK(cc, tt))
            for tt in R4:
                ts = slice(tt * 512, (tt + 1) * 512)
                bA, bB = 4 + 2 * (tt % 2), 5 + 2 * (tt % 2)
                ck = [k for cc in R4 for k in CVK(cc, tt)]
                A(lambda e, ts=ts: e.activation(sq4, Cv[:, :, ts], AF.Square), ck + [SC], ["sq4"])
                G(lambda e, ts=ts: e.tensor_copy(c16, Cv[:, :, ts]), ck + [SC], ["c16"])
                mm(bank(bA), [(onesb[:], c16[:, c, :]) for c in R4], ["c16", "onesb"], [PB(bA)])
                mm(bank(bB), [(onesb[:], sq4[:, c, :]) for c in R4], ["sq4", "onesb"], [PB(bB)])
                A(lambda e, bA=bA: e.activation(m_t, bank(bA), AF.Identity, scale=1.0 / 512), [PB(bA), SC], ["m_t"])
                V(lambda e: e.tensor_tensor(v_t, m_t, m_t, ALU.mult), ["m_t", SC], ["v_t"])
                V(lambda e, bB=bB: e.scalar_tensor_tensor(v_t, bank(bB), 1.0 / 512, v_t, ALU.mult, ALU.subtract),
                  [PB(bB), "v_t"], ["v_t"])
                A(lambda e: e.activation(l_t, v_t, AF.Ln, bias=float(LN_EPS)), ["v_t", SC], ["l_t"])
                A(lambda e, bA=bA: e.activation(bank(bA), l_t, AF.Exp, scale=-0.5), ["l_t", "m_t"], [PB(bA)])
                V(lambda e, bA=bA, bB=bB: e.scalar_tensor_tensor(bank(bB), m_t, -1.0, bank(bA), ALU.mult, ALU.mult),
                  ["m_t", PB(bA), "v_t"], [PB(bB)])
                V(lambda e, ts=ts, bA=bA: e.tensor_tensor(
                    Cv[:, :, ts], Cv[:, :, ts], bank(bA).unsqueeze(1).broadcast_to([128, 4, 512]), ALU.mult),
                  ck + [PB(bA), "sq4", "c16"], ck)
                V(lambda e, ts=ts, bB=bB: e.tensor_tensor(
                    Cv[:, :, ts], Cv[:, :, ts], bank(bB).unsqueeze(1).broadcast_to([128, 4, 512]), ALU.add),
                  ck + [PB(bB)], ck)
                for cc in R4:
                    A(lambda e, cc=cc, ts=ts: e.activation(ABT[:, 4 + cc, ts], Cv[:, cc, ts], AF.Silu,
                                                           bias=pc(("clb", i), 1, cc), scale=pc(("clg", i), 1, cc)),
                      CVK(cc, tt) + ["par"] + (DPK if (cc == 0) else []), [("AB", 4 + cc, tt)])
            fence()
            QT = _carve(scr, 16384, BF16, [2048])
            KT = _carve(scr, 20480, BF16, [2048])
            VA = [_carve(scr, 24576 + 4096 * j, BF16, [16, 128]) for j in range(2)]
            PT2 = [_carve(scr, 32768 + 2048 * j, BF16, [1024]) for j in range(3)]
            pend = []
            cosF = _carve(XBr, 0, F32, [2048])
            sinF = _carve(XBr, 8192, F32, [2048])
            rcs = [_carve(XBr, 16384 + 2048 * j, F32, [512]) for j in range(2)]
            u1 = _carve(XBr, 20480, F32, [512])
            u2 = _carve(XBr, 22528, F32, [512])
            XBall = XBk(R8, R4)
            D(lambda e: e.dma_start(out=cosF[64:96, :], in_=cost_d[64:96, :]), XBall, XBall)
            D(lambda e: e.dma_start(out=sinF[64:96, :], in_=sint_d[64:96, :]), XBall, XBall)
            TBL = [("XB", 0, 0)]
            G(lambda e: e.memset(VA[0][:, :, 64:128], 1.0), [SC], ["VAones0"])
            G(lambda e: e.memset(VA[1][:, :, 0:64], 1.0), [SC], ["VAones1"])
            scale = 96 ** -0.5
            it = 0
            sbctr = 0
            QTs = [QT, _carve(XBr, 24576, BF16, [2048])]
            KTs = [KT, _carve(XBr, 28672, BF16, [2048])]
            wkv_of = {}

            def proj(h):
                QT, KT = QTs[h % 2], KTs[h % 2]
                hb_ = h % 2
                hh = h % 2
                p0 = 64 * hh
                c = h // 2
                uq = wbf["w_uq"]
                wa, ka = load_w(uq[i, :, 96 * h:96 * h + 96].rearrange("(k p) j -> p k j", p=128), 2, ncol=96)
                wb_, kb = load_w(uq[i, :, 96 * h:96 * h + 96].rearrange("(k p) j -> p k j", p=128), 2, ncol=96, extra=[
                    (64, 80, uq[i, :, 96 * h + 80:96 * h + 96].rearrange("(k p) j -> p k j", p=128)),
                    (80, 96, uq[i, :, 96 * h + 64:96 * h + 80].rearrange("(k p) j -> p k j", p=128))])
                wkv_, kkv = load_w(wbf["w_ukv"][i, :, 128 * h:128 * h + 128].rearrange("(k p) j -> p k j", p=128), 1)
                for tt in R4:
                    ts = slice(tt * 512, (tt + 1) * 512)
                    mm(ps[0:96, 512 * 2:512 * 3], [(wa[:, k, :], cqn[:, k, ts]) for k in range(2)], [ka, ("cqn", tt)], [PB(2)])
                    mm(ps[0:96, 512 * 3:512 * 4], [(wb_[:, k, :], cqn[:, k, ts]) for k in range(2)], [kb, ("cqn", tt)], [PB(3)])
                    mm(ps[0:64, 512 * 4:512 * 5], [(wkv_[:, 0, 0:64], ckvn[:, ts])], [kkv, ("ckvn", tt)], [PB(4)])
                    V(lambda e, ts=ts: e.tensor_copy(QT[0:64, ts], ps[0:64, 1024:1536]), [PB(2), SC] + TBL, [("QTm", hb_, tt, 0)])
                    V(lambda e, ts=ts: e.tensor_tensor(u1[64:96, :], ps[64:96, 1024:1536], cosF[64:96, ts], ALU.mult),
                      [PB(2)] + TBL, ["u1"])
                    V(lambda e, ts=ts: e.tensor_tensor(u2[64:96, :], ps[64:96, 1536:2048], sinF[64:96, ts], ALU.mult),
                      [PB(3)] + TBL, ["u2"])
                    G(lambda e, ts=ts: e.tensor_tensor(QT[64:96, ts], u1[64:96, :], u2[64:96, :], ALU.add),
                      ["u1", "u2", SC] + TBL, [("QTm", hb_, tt, 1)])
                    A(lambda e, ts=ts: e.activation(KT[0:64, ts], ps[0:64, 2048:2560], AF.Copy), [PB(4), SC] + TBL, [("KTm", hb_, tt, 0)])
                    G(lambda e, ts=ts: e.tensor_copy(KT[64:96, ts], kr_t[64:96, ts]), [("kr", tt), SC] + TBL, [("KTm", hb_, tt, 1)])
                va = VA[hh]
                for t8 in range(2):
                    b = 5

                    def f(e, t8=t8, b=b, wkv_=wkv_):
                        for j in range(8):
                            tc = 8 * t8 + j
                            ins = e.matmul(ps[:, 512 * b + 64 * j:512 * b + 64 * (j + 1)], ckvn[:, tc * 128:(tc + 1) * 128],
                                           wkv_[:, 0, 64:128], start=True, stop=True)
                        return ins
                    T(f, [kkv, ("ckvn", 2 * t8), ("ckvn", 2 * t8 + 1)], [PB(b)])
                    src = bank(b).rearrange("p (a b) -> p a b", a=8)
                    V(lambda e, va=va, t8=t8, src=src, hh=hh: e.tensor_copy(
                        va[:, 8 * t8:8 * t8 + 8, 64 * hh:64 * hh + 64], src),
                      [PB(b), SC, "VAones%d" % hh], [("VA", hh, t8)])

            def attn(h):
                nonlocal it, sbctr
                QT, KT = QTs[h % 2], KTs[h % 2]
                hb_ = h % 2
                hh = h % 2
                p0 = 64 * hh
                c = h // 2
                va = VA[hh]
                for tt in R4:
                    ts = slice(tt * 512, (tt + 1) * 512)
                    ob = tt % 2
                    OB = bank(ob)
                    for j2 in range(8):
                        sb0 = 2 + 2 * (sbctr % 3)
                        sbctr += 1
                        for u in range(2):
                            kc = 2 * j2 + u
                            mm(bank(sb0 + u), [(KT[0:96, kc * 128:(kc + 1) * 128], QT[0:96, ts])],
                               [("KTm", hb_, kc // 4, 0), ("KTm", hb_, kc // 4, 1), ("QTm", hb_, tt, 0), ("QTm", hb_, tt, 1)], [PB(sb0 + u)])
                        pt = PT2[it % 3]
                        ptk = ("PT", it % 3)
                        it += 1
                        A(lambda e, pt=pt, sb0=sb0: e.activation(pt, ps[:, 512 * sb0:512 * sb0 + 1024], AF.Exp, scale=scale),
                          [PB(sb0), PB(sb0 + 1), SC], [ptk])

                        def st2(OB=OB, va=va, j2=j2, pt=pt, ptk=ptk, hh=hh, ob=ob):
                            def f(e):
                                for u in range(2):
                                    kc = 2 * j2 + u
                                    ins = e.matmul(OB, va[:, kc, :], pt[:, 512 * u:512 * (u + 1)], start=(kc == 0), stop=(kc == 15))
                                return ins
                            T(f, [ptk, ("VA", hh, j2 // 4), "VAones%d" % hh], [PB(ob)])
                        pend.append(st2)
                        while len(pend) > 2:
                            pend.pop(0)()

                    def st3(ob=ob, OB=OB, p0=p0, c=c, ts=ts, tt=tt):
                        rc = rcs[ob]
                        rk = ("rc", ob)
                        dn = 64 - p0
                        A(lambda e: e.activation(rc[p0:p0 + 64, :], OB[dn:dn + 64, :], AF.Ln), [PB(ob)] + TBL, [rk])
                        A(lambda e: e.activation(rc[p0:p0 + 64, :], rc[p0:p0 + 64, :], AF.Exp, scale=-1.0), [rk], [rk])
                        V(lambda e: e.tensor_tensor(ABT[p0:p0 + 64, c, ts], OB[p0:p0 + 64, :], rc[p0:p0 + 64, :], ALU.mult),
                          [PB(ob), rk], [("AB", c, tt)])
                    pend.append(st3)
                while pend:
                    pend.pop(0)()

            proj(0)
            for h in R8:
                if h + 1 < 8:
                    proj(h + 1)
                attn(h)
            G(lambda e: e.memset(dumt[:, 1:2], 0.0), ["u1", "u2", ("rc", 0), ("rc", 1)] + TBL,
              XBall + [(nm, 1, t_, p_) for nm in ("QTm", "KTm") for t_ in R4 for p_ in range(2)])
            w_out_phase("w_out_odd", i)

        PLAN = os.environ.get("KPLAN", "")
        if PLAN:
            for j, tok in enumerate(PLAN.split(",")):
                S.epoch = 1 + j // 2
                if tok[0] == "L":
                    load_x(int(tok[1]), j == 0)
                else:
                    store_y(int(tok[1]))
            nseq = 0
        for s in range(nseq):
            S.epoch = (s + 1) if 'noepoch' not in SKIP else 0
            load_x(s, s == 0)
            for l in range(nlayers):
                if l % 2 == 0:
                    even_mixer(l // 2)
                else:
                    odd_mixer(l // 2)
                ln_phase("ln1g", "ln1b", l)
                if do_ffn:
                    ffn_phase(l)
                    ln_phase("ln2g", "ln2b", l)
            store_y(s)
        S.emit()
    return nc


_NC_CACHE = {}


def _get_nc(nseq, nlayers=DEPTH, do_ffn=True):
    k = (nseq, nlayers, do_ffn)
    if k not in _NC_CACHE:
        _NC_CACHE[k] = build_nc(nseq, nlayers, do_ffn)
    return _NC_CACHE[k]


def _common_inputs(inp):
    c = _consts()
    d = {n: np.ascontiguousarray(inp[n], dtype=np.float32) for n, _ in WSPECS if n != "sgu_wT"}
    d["sgu_wT"] = np.ascontiguousarray(np.transpose(inp["sgu_w"], (0, 1, 3, 2)), dtype=np.float32)
    rp = np.zeros((2, 8, 15, 127), np.float32)
    rp[..., 48:79] = inp["rpb"]
    d["rpb_pad"] = rp
    d["sgu_ln_g"] = np.ascontiguousarray(inp["sgu_ln_g"], dtype=np.float32)
    d["sgu_ln_b"] = np.ascontiguousarray(inp["sgu_ln_b"], dtype=np.float32)
    d["sgu_b"] = np.ascontiguousarray(np.asarray(inp["sgu_b"], dtype=np.float32).reshape(2, 512))
    d["params"] = _pack_params(inp)
    d.update(c)
    return d


def kernel(**inputs):
    inp = {k: np.asarray(v) for k, v in inputs.items()}
    xs = np.concatenate([inp["x_prompt"], inp["x_sample"]], axis=0).astype(np.float32, copy=False)
    nseq = xs.shape[0] // N_CORES
    nc = _get_nc(nseq)
    common = _common_inputs(inp)
    in_maps = []
    for c in range(N_CORES):
        m = dict(common)
        m["x"] = np.ascontiguousarray(xs[c * nseq:(c + 1) * nseq])
        in_maps.append(m)
    res = run_bass_kernel_spmd(nc, in_maps, core_ids=list(range(N_CORES)))
    ys = np.concatenate([r["y"] for r in res.results], axis=0)
    nb = inp["x_prompt"].shape[0]
    return (np.ascontiguousarray(ys[:nb]), np.ascontiguousarray(ys[nb:]))
```

```python
import contextlib
import numpy as np
import concourse.bass as bass
import concourse.mybir as mybir
from concourse.bass_utils import run_bass_kernel_spmd

F32 = mybir.dt.float32
BF16 = mybir.dt.bfloat16
U8 = mybir.dt.uint8
AF = mybir.ActivationFunctionType
ALU = mybir.AluOpType

S_LEN = 2048
DM = 1024
DEPTH = 4
DFF = 2816
ALPHA = float((2 * DEPTH) ** 0.25)
LN_EPS = 1e-5
RMS_EPS = 1e-6
N_CORES = 8


class _Op:
    __slots__ = ("eng", "fn", "deps", "signal", "count", "is_dma", "lane", "lane_k", "epoch")

    def __init__(self, eng, fn, is_dma):
        self.eng = eng
        self.fn = fn
        self.deps = []
        self.signal = False
        self.count = 0
        self.is_dma = is_dma
        self.lane = 0
        self.lane_k = 0
        self.epoch = 0


class Sched:
    ENG_ATTR = {"pe": "tensor", "act": "scalar", "dve": "vector", "pool": "gpsimd", "sp": "sync"}

    def __init__(self, nc, n_lanes=8, same_eng_sync=True):
        self.nc = nc
        self.ops = {k: [] for k in self.ENG_ATTR}
        self.res_w = {}
        self.res_r = {}
        self.n_lanes = n_lanes
        self.dma_cnt = {k: 0 for k in self.ENG_ATTR}
        self.same_eng_sync = same_eng_sync
        self.epoch = 0

    def _record(self, o, reads, writes):
        o.epoch = self.epoch
        deps = {}
        res_w, res_r = self.res_w, self.res_r
        for r in reads:
            w = res_w.get(r)
            if w is not None:
                deps[id(w)] = w
        for r in writes:
            w = res_w.get(r)
            if w is not None:
                deps[id(w)] = w
            rr = res_r.get(r)
            if rr:
                for x in rr.values():
                    deps[id(x)] = x
        eng = o.eng
        dl = []
        for d in deps.values():
            if d is o:
                continue
            if d.eng == eng and not d.is_dma and not o.is_dma:
                if eng == "pe" or not self.same_eng_sync:
                    continue
            dl.append(d)
        o.deps = dl
        for r in writes:
            res_w[r] = o
            res_r[r] = {}
        for r in reads:
            rr = res_r.get(r)
            if rr is None:
                rr = res_r[r] = {}
            rr[eng if not o.is_dma else (eng, o.lane)] = o
        self.ops[eng].append(o)
        return o

    def op(self, eng, fn, reads=(), writes=()):
        return self._record(_Op(eng, fn, False), reads, writes)

    def dma(self, eng, fn, reads=(), writes=()):
        o = _Op(eng, fn, True)
        i = self.dma_cnt[eng]
        self.dma_cnt[eng] = i + 1
        o.lane = i % self.n_lanes
        o.lane_k = i // self.n_lanes + 1
        return self._record(o, reads, writes)

    def emit(self):
        import os
        TRACE = bool(os.environ.get("KTRACE"))
        nc = self.nc
        for eng, lst in self.ops.items():
            for o in lst:
                for d in o.deps:
                    d.signal = True
        with contextlib.ExitStack() as st:
            sems = {}
            for eng in self.ops:
                for ep in sorted(set(o.epoch for o in self.ops[eng])):
                    sems[(eng, ep)] = st.enter_context(nc.semaphore("s_%s_%d" % (eng, ep)))
            lanes = {}
            for eng in self.ops:
                if self.dma_cnt[eng] > 0:
                    lanes[eng] = [st.enter_context(nc.semaphore("l_%s%d" % (eng, i)))
                                  for i in range(min(self.n_lanes, self.dma_cnt[eng]))]
            for eng, lst in self.ops.items():
                cc = {}
                for o in lst:
                    if not o.is_dma and o.signal:
                        c = cc.get(o.epoch, 0) + 1
                        cc[o.epoch] = c
                        o.count = c
            block = st.enter_context(nc.Block())

            def siginfo(d):
                if d.is_dma:
                    return lanes[d.eng][d.lane], 16 * d.lane_k
                return sems[(d.eng, d.epoch)], d.count

            def body_for(eng):
                lst = self.ops[eng]

                def body(e):
                    waited = {}
                    last_dma = {}
                    for oi, o in enumerate(lst):
                        for d in o.deps:
                            s, v = siginfo(d)
                            k = id(s)
                            if waited.get(k, 0) < v:
                                e.wait_ge(s, v)
                                waited[k] = v
                                if TRACE:
                                    print("  %s op%d waits %s(ep%d lane%d) >= %d" % (eng, oi, d.eng, d.epoch, d.lane if d.is_dma else -1, v))
                        if TRACE:
                            print("%s op%d %s signal=%s count=%d ep=%d lane=%d k=%d" % (eng, oi, "DMA" if o.is_dma else "OP", o.signal, o.count, o.epoch, o.lane, o.lane_k))
                        if o.is_dma:
                            s = lanes[eng][o.lane]
                            if o.lane_k > 1:
                                k = id(s)
                                v = 16 * (o.lane_k - 1)
                                if waited.get(k, 0) < v:
                                    e.wait_ge(s, v)
                                    waited[k] = v
                            ins = o.fn(e)
                            ins.then_inc(s, 16)
                            last_dma[o.lane] = o
                        else:
                            ins = o.fn(e)
                            if o.signal:
                                ins.then_inc(sems[(eng, o.epoch)], 1)
                    for lane, o in last_dma.items():
                        s = lanes[eng][lane]
                        v = 16 * o.lane_k
                        if waited.get(id(s), 0) < v:
                            e.wait_ge(s, v)
                return body

            for eng, attr in self.ENG_ATTR.items():
                if self.ops[eng]:
                    getattr(block, attr)(body_for(eng))


def _carve(t, off, dt, dims):
    n = int(np.prod(dims))
    sz = 2 if dt == BF16 else 4
    ap = t[:, off:off + n * sz].bitcast(dt)
    if len(dims) == 2:
        ap = ap.rearrange("p (a b) -> p a b", a=dims[0])
    elif len(dims) == 3:
        ap = ap.rearrange("p (a b c) -> p a b c", a=dims[0], b=dims[1])
    return ap


def _param_layout():
    off = {}
    n = 0
    for l in range(DEPTH):
        for nm, w in (("ln1g", 8), ("ln1b", 8), ("ln2g", 8), ("ln2b", 8), ("fcw", 132), ("fcb", 44)):
            off[(nm, l)] = n
            n += w
    for i in range(2):
        for nm, w in (("qg", 2), ("kvg", 1), ("cw", 124), ("cb", 4), ("clg", 4), ("clb", 4)):
            off[(nm, i)] = n
            n += w
    return off, n


POFF, NPAR = _param_layout()


def _pack_params(inp):
    P = np.zeros((128, NPAR), np.float32)

    def put(key, arr):
        a = np.ascontiguousarray(arr, dtype=np.float32).reshape(128, -1)
        P[:, POFF[key]:POFF[key] + a.shape[1]] = a

    for l in range(DEPTH):
        put(("ln1g", l), inp["ln1_g"][l].reshape(8, 128).T)
        put(("ln1b", l), inp["ln1_b"][l].reshape(8, 128).T)
        put(("ln2g", l), inp["ln2_g"][l].reshape(8, 128).T)
        put(("ln2b", l), inp["ln2_b"][l].reshape(8, 128).T)
        put(("fcw", l), inp["ffn_conv_w"][l].reshape(3, 44, 128).transpose(2, 1, 0))
        put(("fcb", l), inp["ffn_conv_b"][l].reshape(44, 128).T)
    for i in range(2):
        put(("qg", i), inp["q_norm_g"][i].reshape(2, 128).T)
        put(("kvg", i), inp["kv_norm_g"][i].reshape(1, 128).T)
        put(("cw", i), inp["conv_w"][i].reshape(31, 4, 128).transpose(2, 1, 0))
        put(("cb", i), inp["conv_b"][i].reshape(4, 128).T)
        put(("clg", i), inp["conv_ln_g"][i].reshape(4, 128).T)
        put(("clb", i), inp["conv_ln_b"][i].reshape(4, 128).T)
    return P


def _consts():
    c = {}
    c["ident"] = np.eye(128, dtype=np.float32)
    cq = np.arange(64)
    c0 = np.clip(cq - 8, 0, 48)
    ck = np.arange(64)
    m = ((ck[:, None] >= c0[None, :]) & (ck[:, None] < c0[None, :] + 16)).astype(np.float32)
    c["namask"] = np.concatenate([m, m], axis=0)
    pos = np.arange(S_LEN, dtype=np.float32)
    inv = (10000.0 ** (-np.arange(0, 32, 2, dtype=np.float32) / 32)).astype(np.float32)
    ang = (pos[:, None] * inv[None, :]).astype(np.float32)
    cos = np.cos(ang).astype(np.float32).T
    sin = np.sin(ang).astype(np.float32).T
    ct = np.zeros((128, S_LEN), np.float32)
    sn = np.zeros((128, S_LEN), np.float32)
    ct[64:80] = cos
    ct[80:96] = cos
    sn[64:80] = -sin
    sn[80:96] = sin
    c["cost"] = ct
    c["sint"] = sn
    return c


WSPECS = [
    ("w_in_even", [2, 1024, 2560]), ("w_out_even", [2, 1024, 1024]), ("w_in_odd", [2, 1024, 1440]),
    ("w_uq", [2, 256, 768]), ("w_ukv", [2, 128, 1024]), ("w_out_odd", [2, 1024, 1024]),
    ("ffn_w_up", [4, 1024, 5632]), ("ffn_w_down", [4, 2816, 1024]), ("sgu_wT", [2, 4, 128, 128]),
]


def build_nc(nseq, nlayers=DEPTH, do_ffn=True):
    nc = bass.Bass("TRN2", target_bir_lowering=False)
    din = lambda n, s, d=F32: nc.dram_tensor(n, list(s), d, kind="ExternalInput").ap()
    x_d = din("x", [nseq, S_LEN, DM])
    y_d = nc.dram_tensor("y", [nseq, S_LEN, DM], F32, kind="ExternalOutput").ap()
    wsrc = {n: din(n, s) for n, s in WSPECS}
    wbf = {n: nc.dram_tensor(n + "_bf", list(s), BF16, kind="Internal").ap() for n, s in WSPECS}
    rpbpad_d = din("rpb_pad", [2, 8, 15, 127])
    sgu_lng_d = din("sgu_ln_g", [2, 512])
    sgu_lnb_d = din("sgu_ln_b", [2, 512])
    sgu_b_d = din("sgu_b", [2, 512])
    params_d = din("params", [128, NPAR])
    ident_d = din("ident", [128, 128])
    namask_d = din("namask", [128, 64])
    cost_d = din("cost", [128, S_LEN])
    sint_d = din("sint", [128, S_LEN])
    ebtab_d = nc.dram_tensor("ebtab", [2, 8, 128, 960], F32, kind="Internal").ap()
    dummy_d = nc.dram_tensor("dummy_scr", [1, 64], F32, kind="Internal").ap()

    import os
    DBG = os.environ.get("KDBG", "")
    if DBG:
        dbg_d = nc.dram_tensor("dbg", [128, 16384], BF16, kind="ExternalOutput").ap()
    S = Sched(nc, n_lanes=int(os.environ.get('KLANES', '8')))
    NSLOT = 6
    SLOTW = 1408
    SCRB = 40960

    with contextlib.ExitStack() as st:
        sb = lambda n, s, d: st.enter_context(nc.sbuf_tensor(n, s, d))
        XT = sb("XT", [128, 8, S_LEN], F32)
        XBr = sb("XBr", [128, 32768], U8)
        ABr = sb("ABr", [128, 32768], U8)
        wsl = sb("wsl", [128, NSLOT, SLOTW], BF16)
        scr = sb("scr", [128, SCRB], U8)
        par = sb("par", [128, NPAR], F32)
        identf = sb("identf", [128, 128], F32)
        identb = sb("identb", [128, 128], BF16)
        onesb = sb("onesb", [128, 128], BF16)
        onesf = sb("onesf", [1, 128], F32)
        dumt = sb("dumt", [128, 2], F32)
        ps = st.enter_context(nc.psum_tensor("ps", [128, 4096], F32))

        XB = _carve(XBr, 0, BF16, [8, S_LEN])
        ABT = _carve(ABr, 0, BF16, [8, S_LEN])
        bank = lambda b: ps[:, 512 * b:512 * (b + 1)]
        PB = lambda b: ("ps", b)
        SC = "SCR"

        def pc(key, w=1, j=0):
            o = POFF[key] + j
            return par[:, o:o + w]

        A = lambda fn, r=(), w=(): S.op("act", fn, list(r) + [SC], w)
        V = lambda fn, r=(), w=(): S.op("dve", fn, list(r) + [SC], w)
        G = lambda fn, r=(), w=(): S.op("pool", fn, list(r) + [SC], w)
        T = lambda fn, r=(), w=(): S.op("pe", fn, list(r) + [SC], w)
        D = lambda fn, r=(), w=(), nosc=False: S.dma("sp", fn, list(r) + ([] if nosc else [SC]), w)

        def fence():
            S.op("pool", lambda e: e.memset(dumt[:, 0:1], 0.0), [], [SC])

        def mm(out, pairs, r, w):
            def f(e):
                n = len(pairs)
                for i, (l, rh) in enumerate(pairs):
                    ins = e.matmul(out, l, rh, start=(i == 0), stop=(i == n - 1))
                return ins
            T(f, r, w)

        wctr = [0]

        def load_w(src_ap, nk, ncol=128, extra=None):
            slot = wctr[0] % NSLOT
            wctr[0] += 1
            dst = wsl[:, slot, 0:nk * ncol].rearrange("p (k j) -> p k j", j=ncol)
            key = ("w", slot)
            D(lambda e: e.dma_start(out=dst, in_=src_ap), ["dw"], [key], nosc=True)
            if extra:
                for (c0, c1, sap) in extra:
                    D(lambda e, c0=c0, c1=c1, sap=sap: e.dma_start(out=dst[:, :, c0:c1], in_=sap), ["dw"], [key], nosc=True)
            return dst, key

        D(lambda e: e.dma_start(out=par[:], in_=params_d), [], ["par"])
        D(lambda e: e.dma_start(out=identf[:], in_=ident_d), [], ["identf"])
        V(lambda e: e.tensor_copy(identb[:], identf[:]), ["identf"], ["identb"])
        G(lambda e: e.memset(onesb[:], 1.0), [], ["onesb"])
        G(lambda e: e.memset(onesf[:], 1.0), [], ["onesf"])

        st32 = [XT[:, 2 * i:2 * i + 2, :].rearrange("p a b -> p (a b)") for i in range(4)]
        st16 = [_carve(XBr, 8192 * i, BF16, [4096]) for i in range(4)]
        pst = []
        ci = 0
        import os
        SKIP = os.environ.get("KSKIP", "").split(",")
        for name, shp in WSPECS:
            if "conv" in SKIP:
                break
            n = int(np.prod(shp))
            per = n // 128
            letters = " ".join("abcd"[:len(shp)])
            sv = wsrc[name].rearrange("%s -> (%s)" % (letters, letters)).rearrange("(p f) -> p f", p=128)
            dv = wbf[name].rearrange("%s -> (%s)" % (letters, letters)).rearrange("(p f) -> p f", p=128)
            for c0 in range(0, per, 4096):
                w = min(4096, per - c0)
                sl = ci % 4
                D(lambda e, sl=sl, w=w, c0=c0, sv=sv: e.dma_start(out=st32[sl][:, 0:w], in_=sv[:, c0:c0 + w]),
                  [], [("st32", sl)])
                eng = ("act", "dve")[ci % 2]
                if eng == "act":
                    A(lambda e, sl=sl, w=w: e.activation(st16[sl][:, 0:w], st32[sl][:, 0:w], AF.Copy),
                      [("st32", sl)], [("st16", sl)])
                else:
                    S.op(eng, lambda e, sl=sl, w=w: e.tensor_copy(st16[sl][:, 0:w], st32[sl][:, 0:w]),
                         [("st32", sl)], [("st16", sl)])
                k = ("pst", ci)
                D(lambda e, sl=sl, w=w, c0=c0, dv=dv: e.dma_start(out=dv[:, c0:c0 + w], in_=st16[sl][:, 0:w]),
                  [("st16", sl)], [k])
                pst.append(k)
                ci += 1
        hk = _carve(scr, 0, F32, [15, 64])
        eb = _carve(scr, 4096, F32, [15, 64])
        mk = _carve(scr, 8192, F32, [64])
        D(lambda e: e.dma_start(out=mk, in_=namask_d), [], ["mk"])
        for i in range(2):
            if "eb" in SKIP:
                break
            for h in range(8):
                base = ((i * 8 + h) * 15) * 127
                src = bass.AP(tensor=rpbpad_d.tensor, offset=base, ap=[[1, 64], [127, 15], [1, 64]])
                src2 = bass.AP(tensor=rpbpad_d.tensor, offset=base + 127, ap=[[1, 64], [127, 14], [1, 64]])
                D(lambda e, src=src: e.dma_start(out=hk[0:64], in_=src), [], ["hk0"])
                G(lambda e: e.memset(hk[64:128, 14:15, :], 0.0), [], ["hk1"])
                D(lambda e, src2=src2: e.dma_start(out=hk[64:128, 0:14, :], in_=src2), [], ["hk1"])
                A(lambda e: e.activation(hk, hk, AF.Exp), ["hk0", "hk1"], ["hk0", "hk1"])
                V(lambda e: e.tensor_tensor(eb, hk[:, :, ::-1], mk.unsqueeze(1).broadcast_to([128, 15, 64]), ALU.mult),
                  ["hk0", "hk1", "mk"], ["eb"])
                k = ("pst", ci)
                ci += 1
                D(lambda e, i=i, h=h: e.dma_start(out=ebtab_d[i, h].rearrange("p (a b) -> p a b", a=15), in_=eb),
                  ["eb"], [k])
                pst.append(k)
        D(lambda e: e.dma_start(out=dummy_d, in_=ident_d[0:1, 0:64]), pst, ["dw"])
        stage_keys = [("st32", i) for i in range(4)] + [("st16", i) for i in range(4)]

        XTk = lambda cs, tts: [("XT", c, t) for c in cs for t in tts]
        XBk = lambda cs, tts: [("XB", c, t) for c in cs for t in tts]
        ABk = lambda cs, tts: [("AB", c, t) for c in cs for t in tts]
        R8 = range(8)
        R4 = range(4)

        def load_x(s, first):
            fence()
            for tc in range(16):
                sl = tc % 2
                xtok = _carve(scr, 4096 * sl, F32, [1024])
                tt = tc // 4
                D(lambda e, xtok=xtok, tc=tc: e.dma_start(out=xtok, in_=x_d[s, tc * 128:(tc + 1) * 128, :]),
                  [], [("tok", sl, 0), ("tok", sl, 1)] + (["hk0", "hk1", "eb", "mk"] if first else []))
                for half in range(2):
                    b = (tc * 2 + half) % 8

                    def f(e, xtok=xtok, half=half, b=b):
                        for j in range(4):
                            ins = e.transpose(ps[:, 512 * b + 128 * j:512 * b + 128 * (j + 1)],
                                              xtok[:, (4 * half + j) * 128:(4 * half + j + 1) * 128], identf[:])
                        return ins
                    T(f, [("tok", sl, 0), ("tok", sl, 1), "identf"], [PB(b)])
                    cs = range(4 * half, 4 * half + 4)
                    src = bank(b).rearrange("p (a b) -> p a b", a=4)
                    extra = stage_keys if first else []
                    A(lambda e, half=half, tc=tc, src=src: e.activation(
                        XT[:, 4 * half:4 * half + 4, tc * 128:(tc + 1) * 128], src, AF.Copy),
                      [PB(b)], XTk(cs, [tt]) + extra)
                    V(lambda e, half=half, tc=tc: e.tensor_copy(
                        XB[:, 4 * half:4 * half + 4, tc * 128:(tc + 1) * 128],
                        XT[:, 4 * half:4 * half + 4, tc * 128:(tc + 1) * 128]),
                      XTk(cs, [tt]), XBk(cs, [tt]) + extra)

        def store_y(s):
            fence()
            for tc in range(16):
                sl = tc % 2
                ytok = _carve(scr, 4096 * sl, F32, [1024])
                tt = tc // 4
                for half in range(2):
                    b = (tc * 2 + half) % 8

                    def f(e, half=half, b=b, tc=tc):
                        for j in range(4):
                            ins = e.transpose(ps[:, 512 * b + 128 * j:512 * b + 128 * (j + 1)],
                                              XT[:, 4 * half + j, tc * 128:(tc + 1) * 128], identf[:])
                        return ins
                    T(f, XTk(range(4 * half, 4 * half + 4), [tt]) + ["identf"], [PB(b)])
                    if half == 0:
                        A(lambda e, ytok=ytok, b=b: e.activation(ytok[:, 0:512], bank(b), AF.Copy),
                          [PB(b), SC], [("tok", sl, 0)])
                    else:
                        V(lambda e, ytok=ytok, b=b: e.tensor_copy(ytok[:, 512:1024], bank(b)),
                          [PB(b), SC], [("tok", sl, 1)])
                D(lambda e, ytok=ytok, tc=tc: e.dma_start(out=y_d[s, tc * 128:(tc + 1) * 128, :], in_=ytok),
                  [("tok", sl, 0), ("tok", sl, 1)], [("tok", sl, 0), ("tok", sl, 1)])

        def ln_stats(tt, src_fn, nch, scale_n, eps, bA, bB, sq, x16, m_t, v_t, l_t, rkeys, x16keys):
            pass

        def ln_phase(gkey, bkey, l):
            fence()
            if "ln" in SKIP:
                return
            sqs = [_carve(scr, 8192 * j, BF16, [8, 512]) for j in range(2)]
            m_ts = [_carve(scr, 16384 + 2048 * j, F32, [512]) for j in range(2)]
            v_ts = [_carve(scr, 20480 + 2048 * j, F32, [512]) for j in range(2)]
            l_ts = [_carve(scr, 24576 + 2048 * j, F32, [512]) for j in range(2)]

            def s1(tt):
                ts = slice(tt * 512, (tt + 1) * 512)
                bA, bB = 2 * tt, 2 * tt + 1
                sq = sqs[tt % 2]
                sqk = ("sq", tt % 2)
                xk = XTk(R8, [tt])
                A(lambda e: e.activation(sq, XT[:, :, ts], AF.Square), xk, [sqk])
                V(lambda e: e.tensor_copy(XB[:, :, ts], XT[:, :, ts]), xk, XBk(R8, [tt]))
                mm(bank(bA), [(onesb[:], XB[:, c, ts]) for c in R8], XBk(R8, [tt]) + ["onesb"], [PB(bA)])
                mm(bank(bB), [(onesb[:], sq[:, c, :]) for c in R8], [sqk, "onesb"], [PB(bB)])

            def s2(tt):
                bA, bB = 2 * tt, 2 * tt + 1
                m_t, v_t, l_t = m_ts[tt % 2], v_ts[tt % 2], l_ts[tt % 2]
                mk_, vk_, lk_ = ("m_t", tt % 2), ("v_t", tt % 2), ("l_t", tt % 2)
                A(lambda e: e.activation(m_t, bank(bA), AF.Identity, scale=1.0 / DM), [PB(bA)], [mk_])
                A(lambda e: e.activation(v_t, bank(bA), AF.Square, scale=1.0 / DM), [PB(bA)], [vk_])
                V(lambda e: e.scalar_tensor_tensor(v_t, bank(bB), 1.0 / DM, v_t, ALU.mult, ALU.subtract),
                  [PB(bB), vk_], [vk_])
                A(lambda e: e.activation(l_t, v_t, AF.Ln, bias=float(LN_EPS)), [vk_], [lk_])
                A(lambda e: e.activation(bank(bA), l_t, AF.Exp, scale=-0.5), [lk_, mk_], [PB(bA)])
                V(lambda e: e.scalar_tensor_tensor(bank(bB), m_t, -1.0, bank(bA), ALU.mult, ALU.mult),
                  [mk_, PB(bA), vk_], [PB(bB)])

            def s3(tt):
                ts = slice(tt * 512, (tt + 1) * 512)
                bA, bB = 2 * tt, 2 * tt + 1
                xk = XTk(R8, [tt])
                V(lambda e: e.tensor_tensor(
                    XT[:, :, ts], XT[:, :, ts], bank(bA).unsqueeze(1).broadcast_to([128, 8, 512]), ALU.mult),
                  xk + [PB(bA)], xk)
                V(lambda e: e.tensor_tensor(
                    XT[:, :, ts], XT[:, :, ts], bank(bB).unsqueeze(1).broadcast_to([128, 8, 512]), ALU.add),
                  xk + [PB(bB)], xk)

            def s4(tt):
                ts = slice(tt * 512, (tt + 1) * 512)
                for c in R8:
                    gs = pc((gkey, l), 1, c)
                    bs = pc((bkey, l), 1, c)
                    A(lambda e, c=c, gs=gs, bs=bs: e.activation(XT[:, c, ts], XT[:, c, ts], AF.Identity, bias=bs, scale=gs),
                      [("XT", c, tt), "par"], [("XT", c, tt)])
                V(lambda e: e.tensor_copy(XB[:, :, ts], XT[:, :, ts]), XTk(R8, [tt]), XBk(R8, [tt]))

            for st_fn, tt_ in ((s1, 0), (s1, 1), (s2, 0), (s1, 2), (s2, 1), (s3, 0), (s1, 3), (s2, 2), (s3, 1), (s4, 0),
                               (s2, 3), (s3, 2), (s4, 1), (s3, 3), (s4, 2), (s4, 3)):
                st_fn(tt_)

        def w_out_phase(wname, i):
            fence()
            if DBG:
                D(lambda e: e.dma_start(out=dbg_d, in_=_carve(ABr, 0, BF16, [16384])), ABk(R8, R4), [])
            if "wout" in SKIP:
                return
            bctr = 0
            for oc in R8:
                wt, wk = load_w(wbf[wname][i, :, oc * 128:(oc + 1) * 128].rearrange("(k p) j -> p k j", p=128), 8)
                for tt in R4:
                    ts = slice(tt * 512, (tt + 1) * 512)
                    b = bctr % 8
                    bctr += 1
                    mm(bank(b), [(wt[:, k, :], ABT[:, k, ts]) for k in R8], [wk] + ABk(R8, [tt]), [PB(b)])
                    V(lambda e, oc=oc, ts=ts, b=b: e.scalar_tensor_tensor(
                        XT[:, oc, ts], XT[:, oc, ts], ALPHA, bank(b), ALU.mult, ALU.add),
                      [PB(b), ("XT", oc, tt)], [("XT", oc, tt)])

        def ffn_phase(l):
            fence()
            if "ffn" in SKIP:
                return
            groups = [(0, 6), (6, 12), (12, 17), (17, 22)]
            Abuf = ABT
            for gi, (c0, c1) in enumerate(groups):
                for ci_ in range(c0, c1):
                    a = ci_ - c0
                    for half_gv in range(2):
                        col0 = ci_ * 128 + (DFF if half_gv else 0)
                        ch = ci_ + (22 if half_gv else 0)
                        wt, wk = load_w(wbf["ffn_w_up"][l, :, col0:col0 + 128].rearrange("(k p) j -> p k j", p=128), 8)
                        b0 = 4 * half_gv
                        hb = [PB(b0 + j) for j in R4]
                        for tt in R4:
                            ts = slice(tt * 512, (tt + 1) * 512)
                            mm(bank(b0 + tt), [(wt[:, k, :], XB[:, k, ts]) for k in R8],
                               [wk] + XBk(R8, [tt]), [PB(b0 + tt)])
                        H = ps[:, 2048 * half_gv:2048 * (half_gv + 1)]
                        w0 = pc(("fcw", l), 1, ch * 3 + 0)
                        w1 = pc(("fcw", l), 1, ch * 3 + 1)
                        w2 = pc(("fcw", l), 1, ch * 3 + 2)
                        bb = pc(("fcb", l), 1, ch)
                        bufs = [_carve(scr, (8192 if half_gv else 0) + 4096 * hf, F32, [1024]) for hf in range(2)]
                        bks = [("ffb", half_gv, hf) for hf in range(2)]
                        for hf in range(2):
                            lo = 1024 * hf
                            A(lambda e, buf=bufs[hf], lo=lo, H=H, w1=w1, bb=bb: e.activation(
                                buf, H[:, lo:lo + 1024], AF.Identity, bias=bb, scale=w1),
                              [PB(b0 + 2 * hf), PB(b0 + 2 * hf + 1), "par", SC], [bks[hf]])
                        hb01 = [PB(b0), PB(b0 + 1)]
                        for hf in range(2):
                            lo = 1024 * hf
                            buf = bufs[hf]
                            bk = bks[hf]
                            if hf == 0:
                                V(lambda e, buf=buf, H=H, w0=w0: e.scalar_tensor_tensor(
                                    buf[:, 1:1024], H[:, 0:1023], w0, buf[:, 1:1024], ALU.mult, ALU.add),
                                  hb01 + [bks[0], "par"], [bk])
                                V(lambda e, buf=buf, H=H, w2=w2: e.scalar_tensor_tensor(
                                    buf[:, 0:1023], H[:, 1:1024], w2, buf[:, 0:1023], ALU.mult, ALU.add),
                                  hb01 + [bk, "par"], [bk])
                                V(lambda e, buf=buf, H=H, w2=w2: e.scalar_tensor_tensor(
                                    buf[:, 1023:1024], H[:, 1024:1025], w2, buf[:, 1023:1024], ALU.mult, ALU.add),
                                  hb + [bk, bks[1], "par"], [bk])
                            else:
                                V(lambda e, buf=buf, H=H, w0=w0: e.scalar_tensor_tensor(
                                    buf[:, 0:1024], H[:, 1023:2047], w0, buf[:, 0:1024], ALU.mult, ALU.add),
                                  hb + [bks[0], bks[1], "par"], [bk])
                                V(lambda e, buf=buf, H=H, w2=w2: e.scalar_tensor_tensor(
                                    buf[:, 0:1023], H[:, 1025:2048], w2, buf[:, 0:1023], ALU.mult, ALU.add),
                                  hb + [bk, "par"], [bk])
                            if half_gv == 0:
                                A(lambda e, buf=buf: e.activation(buf, buf, AF.Silu), [bk], [bk])
                            else:
                                gbuf = _carve(scr, 4096 * hf, F32, [1024])
                                G(lambda e, buf=buf, gbuf=gbuf, a=a, lo=lo: e.tensor_tensor(
                                    Abuf[:, a, lo:lo + 1024], gbuf, buf, ALU.mult),
                                  [bk, ("ffb", 0, hf)], ABk([a], [2 * hf, 2 * hf + 1]))
                nk = c1 - c0
                bctr = 0
                wts = {}
                for kpass in range(2):
                    ks = list(range(nk - 1)) if kpass == 0 else [nk - 1]
                    for oc in R8:
                        if kpass == 0 or oc >= 4:
                            wt, wk = load_w(wbf["ffn_w_down"][l, c0 * 128:c1 * 128, oc * 128:(oc + 1) * 128]
                                            .rearrange("(k p) j -> p k j", p=128), nk)
                            wts[oc] = (wt, wk)
                        else:
                            wt, wk = load_w(wbf["ffn_w_down"][l, (c1 - 1) * 128:c1 * 128, oc * 128:(oc + 1) * 128]
                                            .rearrange("(k p) j -> p k j", p=128), 1)
                            wt = wt
                            wts[oc] = (None, None)
                        for tt in R4:
                            ts = slice(tt * 512, (tt + 1) * 512)
                            b = bctr % 8
                            bctr += 1
                            if kpass == 1 and oc < 4:
                                pairs = [(wt[:, 0, :], Abuf[:, nk - 1, ts])]
                            elif kpass == 1:
                                pairs = [(wt[:, nk - 1, :], Abuf[:, nk - 1, ts])]
                            else:
                                pairs = [(wt[:, k, :], Abuf[:, k, ts]) for k in ks]
                            mm(bank(b), pairs, [wk] + ABk(ks, [tt]), [PB(b)])
                            sc_ = ALPHA if (gi == 0 and kpass == 0) else 1.0
                            V(lambda e, oc=oc, ts=ts, b=b, sc_=sc_: e.scalar_tensor_tensor(
                                XT[:, oc, ts], XT[:, oc, ts], sc_, bank(b), ALU.mult, ALU.add),
                              [PB(b), ("XT", oc, tt)], [("XT", oc, tt)])

        def even_mixer(i):
            win = wbf["w_in_even"]
            wcol = lambda c0: win[i, :, c0:c0 + 128].rearrange("(k p) j -> p k j", p=128)
            fence()
            if "sgu" in SKIP:
                G(lambda e: e.memset(ABT[:, 4:8, :], 0.0), [], ABk(range(4, 8), R4))
                return na_part(i, wcol)
            vln = _carve(scr, 0, BF16, [16, 512])
            U = _carve(scr, 16384, F32, [2048])
            tmpA = [_carve(scr, 24576 + 2048 * j, F32, [512]) for j in range(2)]
            gbc = _carve(scr, 28672, F32, [512])
            bbc = _carve(scr, 30720, F32, [512])
            sguW = _carve(scr, 32768, BF16, [4, 128])
            stt = _carve(scr, 33792, F32, [16])
            sgub = _carve(scr, 34048, F32, [512])
            D(lambda e: e.dma_start(out=gbc, in_=sgu_lng_d[i, :].partition_broadcast(128)), [SC], ["gbc"])
            D(lambda e: e.dma_start(out=bbc, in_=sgu_lnb_d[i, :].partition_broadcast(128)), [SC], ["bbc"])
            D(lambda e: e.dma_start(out=sguW, in_=wbf["sgu_wT"][i].rearrange("g q p -> q g p")), [SC, "dw"], ["sguW"])
            D(lambda e: e.dma_start(out=sgub[0:1, :], in_=sgu_b_d[i:i + 1, :]), [SC], ["sgub"])
            gw = [load_w(wcol(2048 + 128 * j), 8) for j in R4]
            for tc in range(16):
                b = tc % 4
                tt = tc // 4
                tcs = slice(tc * 128, (tc + 1) * 128)

                def f(e, b=b, tcs=tcs):
                    for j in R4:
                        for k in R8:
                            ins = e.matmul(ps[:, 512 * b + 128 * j:512 * b + 128 * (j + 1)], XB[:, k, tcs],
                                           gw[j][0][:, k, :], start=(k == 0), stop=(k == 7))
                    return ins
                T(f, [g_[1] for g_ in gw] + XBk(R8, [tt]), [PB(b)])
                ta = tmpA[tc % 2]
                tk = ("tmpA", tc % 2)
                A(lambda e, ta=ta, b=b: e.activation(ta, bank(b), AF.Gelu_apprx_tanh), [PB(b), SC], [tk])
                V(lambda e, ta=ta: e.bn_stats(stt[:, 0:6], ta), [tk, SC], ["stt"])
                V(lambda e: e.bn_aggr(stt[:, 6:8], stt[:, 0:6]), ["stt"], ["stt"])
                A(lambda e: e.activation(stt[:, 8:9], stt[:, 7:8], AF.Ln, bias=float(LN_EPS)), ["stt"], ["stt"])
                A(lambda e: e.activation(stt[:, 9:10], stt[:, 8:9], AF.Exp, scale=-0.5), ["stt"], ["stt"])
                V(lambda e: e.scalar_tensor_tensor(stt[:, 10:11], stt[:, 6:7], -1.0, stt[:, 9:10], ALU.mult, ALU.mult),
                  ["stt"], ["stt"])
                A(lambda e, ta=ta: e.activation(ta, ta, AF.Identity, bias=stt[:, 10:11], scale=stt[:, 9:10]),
                  ["stt", tk], [tk])
                V(lambda e, ta=ta: e.tensor_tensor(ta, ta, gbc, ALU.mult), [tk, "gbc"], [tk])
                V(lambda e, ta=ta, tc=tc: e.tensor_tensor(vln[:, tc, :], ta, bbc, ALU.add), [tk, "bbc"], [("vln", tc)])
            for g in R4:
                uw, uk = load_w(wcol(1536 + 128 * g), 8)
                for tt in R4:
                    ts = slice(tt * 512, (tt + 1) * 512)
                    mm(bank(4 + tt), [(uw[:, k, :], XB[:, k, ts]) for k in R8], [uk] + XBk(R8, [tt]), [PB(4 + tt)])
                    A(lambda e, tt=tt, ts=ts: e.activation(U[:, ts], bank(4 + tt), AF.Gelu_apprx_tanh),
                      [PB(4 + tt), SC], [("U", tt)])
                for tt in R4:
                    ts = slice(tt * 512, (tt + 1) * 512)

                    def f(e, g=g, tt=tt):
                        for j in R4:
                            tc = 4 * tt + j
                            o = ps[:, 512 * tt + 128 * j:512 * tt + 128 * (j + 1)]
                            e.matmul(o, vln[:, tc, g * 128:(g + 1) * 128], sguW[:, g, :], start=True, stop=False)
                            ins = e.matmul(o, onesf[0:1, :], sgub[0:1, g * 128:(g + 1) * 128], start=False, stop=True)
                        return ins
                    T(f, [("vln", 4 * tt + j) for j in R4] + ["sguW", "sgub", "onesf"], [PB(tt)])
                    V(lambda e, g=g, tt=tt, ts=ts: e.tensor_tensor(ABT[:, 4 + g, ts], bank(tt), U[:, ts], ALU.mult),
                      [PB(tt), ("U", tt)], [("AB", 4 + g, tt)])
            return na_part(i, wcol)

        def na_part(i, wcol):
            fence()
            if "na" in SKIP:
                G(lambda e: e.memset(ABT[:, 0:4, :], 0.0), [], ABk(R4, R4))
                return w_out_phase("w_out_even", i)
            QT = _carve(scr, 0, BF16, [2048])
            KT = _carve(scr, 4096, BF16, [2048])
            Vaug = _carve(scr, 8192, BF16, [16, 256])
            EBs = [_carve(scr, 16384 + 3840 * j, F32, [15, 64]) for j in range(2)]
            etm = [_carve(scr, 24064 + 2048 * j, F32, [512]) for j in range(2)]
            NPT = 5
            SKEW = int(os.environ.get("KSKEW", "2"))
            pend = []
            PTs = [_carve(scr, 28160 + 1024 * j, BF16, [512]) for j in range(NPT)]
            rcs = [_carve(scr, 33280 + 2048 * j, F32, [512]) for j in range(2)]
            if "nomemset" not in SKIP:
                G(lambda e: e.memset(Vaug[:, :, 64:192], 1.0), [SC], ["Vones"])
            r0 = lambda r: min(max(r - 4, 0), 24)
            VON = [] if "novon" in SKIP else ["Vones"]
            it = 0
            sbctr = 0
            scale = 64 ** -0.5
            for c in ([1, 0, 3, 2] if "corder" in SKIP else R4):
                qw, qk = load_w(wcol(128 * c), 8)
                kw, kk = load_w(wcol(512 + 128 * c), 8)
                vw, vk = load_w(wcol(1024 + 128 * c), 8)
                for tt in R4:
                    ts = slice(tt * 512, (tt + 1) * 512)
                    mm(bank(2 + tt), [(qw[:, k, :], XB[:, k, ts]) for k in R8], [qk] + XBk(R8, [tt]), [PB(2 + tt)])
                    A(lambda e, tt=tt, ts=ts: e.activation(QT[:, ts], bank(2 + tt), AF.Copy), [PB(2 + tt), SC], [("QT", tt)])
                for tt in R4:
                    ts = slice(tt * 512, (tt + 1) * 512)
                    mm(bank(2 + tt), [(kw[:, k, :], XB[:, k, ts]) for k in R8], [kk] + XBk(R8, [tt]), [PB(2 + tt)])
                    V(lambda e, tt=tt, ts=ts: e.tensor_copy(KT[:, ts], bank(2 + tt)), [PB(2 + tt), SC], [("KT", tt)])
                for t4 in R4:
                    if "nov" in SKIP:
                        break
                    b = 2 + t4

                    def f(e, b=b, t4=t4, vw=vw):
                        for j in R4:
                            tc = 4 * t4 + j
                            for k in R8:
                                ins = e.matmul(ps[:, 512 * b + 128 * j:512 * b + 128 * (j + 1)],
                                               XB[:, k, tc * 128:(tc + 1) * 128], vw[:, k, :], start=(k == 0), stop=(k == 7))
                        return ins
                    T(f, [vk] + XBk(R8, [t4]), [PB(b)])
                    src = bank(b).rearrange("p (a b) -> p a b", a=4)
                    A(lambda e, t4=t4, src=src: e.activation(Vaug[:, 4 * t4:4 * t4 + 4, 0:64], src[:, :, 0:64], AF.Copy),
                      [PB(b), "Vones"], [("Va", t4, 0)])
                    A(lambda e, t4=t4, src=src: e.activation(Vaug[:, 4 * t4:4 * t4 + 4, 192:256], src[:, :, 64:128], AF.Copy),
                      [PB(b), "Vones"], [("Va", t4, 1)])
                if "na1" in SKIP:
                    G(lambda e, c=c: e.memset(ABT[:, c, :], 0.0), [], ABk([c], R4))
                    continue
                for hh in range(2):
                    h = 2 * c + hh
                    p0 = 64 * hh
                    EB = EBs[h % 2]
                    ek = ("EB", h % 2)
                    D(lambda e, EB=EB, h=h: e.dma_start(out=EB, in_=ebtab_d[i, h].rearrange("p (a b) -> p a b", a=15)),
                      ["dw", SC], [ek])
                    for qb in R4:
                        ob = qb % 2
                        OB = bank(ob)
                        valid = lambda r, kr: r0(r) <= kr <= r0(r) + 7
                        brows = list(range(8 * qb, 8 * qb + 8))
                        k_lo, k_hi = r0(8 * qb), r0(8 * qb + 7) + 7
                        pairs_m = list(range(k_lo // 2, k_hi // 2 + 1))
                        for ki, m_ in enumerate(pairs_m):
                            rows = [r for r in brows if valid(r, 2 * m_) or valid(r, 2 * m_ + 1)]
                            ra, rb = rows[0], rows[-1] + 1
                            assert rows == list(range(ra, rb))
                            n = (rb - ra) * 64
                            sbk = 2 + sbctr % 6
                            sbctr += 1
                            SBt = ps[:, 512 * sbk:512 * sbk + n]
                            mm(SBt, [(KT[p0:p0 + 64, m_ * 128:(m_ + 1) * 128], QT[p0:p0 + 64, ra * 64:rb * 64])],
                               [("KT", m_ // 4), ("QT", qb)], [PB(sbk)])
                            et = etm[it % 2]
                            etk = ("etm", it % 2)
                            pt = PTs[it % NPT]
                            ptk = ("PT", it % NPT)
                            it += 1
                            A(lambda e, et=et, SBt=SBt, n=n: e.activation(et[:, 0:n], SBt, AF.Exp, scale=scale),
                              [PB(sbk), SC], [etk])
                            t_hi = 2 * m_ - ra + 7
                            nr = rb - ra
                            assert 0 <= t_hi - nr + 1 and t_hi <= 13, (t_hi, nr)
                            ebv = EB[:, t_hi - nr + 1:t_hi + 1, :][:, ::-1, :]
                            V(lambda e, pt=pt, et=et, n=n, ebv=ebv: e.tensor_tensor(
                                pt[:, 0:n].rearrange("p (a b) -> p a b", b=64),
                                et[:, 0:n].rearrange("p (a b) -> p a b", b=64), ebv, ALU.mult),
                              [etk, ek], [ptk])
                            for r in rows:
                                for half in range(2):
                                    if not valid(r, 2 * m_ + half):
                                        a_ = r - ra
                                        G(lambda e, pt=pt, half=half, a_=a_: e.memset(
                                            pt[64 * half:64 * half + 64, a_ * 64:(a_ + 1) * 64], 0.0), [ptk], [ptk])

                            def st2(ob=ob, ra=ra, rb=rb, qb=qb, m_=m_, hh=hh, pt=pt, n=n, ki=ki, nkr=len(pairs_m), ptk=ptk):
                                T(lambda e: e.matmul(
                                    ps[:, 512 * ob + (ra - 8 * qb) * 64:512 * ob + (rb - 8 * qb) * 64],
                                    Vaug[:, m_, 128 * hh:128 * hh + 128], pt[:, 0:n],
                                    start=(ki == 0), stop=(ki == nkr - 1)),
                                  [ptk, ("Va", m_ // 4, hh), "Vones"], [PB(ob)])
                            pend.append(st2)
                            while len(pend) > SKEW:
                                pend.pop(0)()

                        def st3(ob=ob, OB=OB, p0=p0, c=c, qb=qb):
                            rc = rcs[ob]
                            rk = ("rc", ob)
                            dn = 64 - p0
                            A(lambda e: e.activation(rc[p0:p0 + 64, :], OB[dn:dn + 64, :], AF.Ln), [PB(ob), SC], [rk])
                            A(lambda e: e.activation(rc[p0:p0 + 64, :], rc[p0:p0 + 64, :], AF.Exp, scale=-1.0), [rk], [rk])
                            V(lambda e: e.tensor_tensor(
                                ABT[p0:p0 + 64, c, qb * 512:(qb + 1) * 512], OB[p0:p0 + 64, :], rc[p0:p0 + 64, :], ALU.mult),
                              [PB(ob), rk], [("AB", c, qb)])
                        pend.append(st3)
                while pend:
                    pend.pop(0)()
            w_out_phase("w_out_even", i)

        def odd_mixer(i):
            win = wbf["w_in_odd"]
            wcol = lambda c0: win[i, :, c0:c0 + 128].rearrange("(k p) j -> p k j", p=128)
            fence()
            cqn = _carve(scr, 0, BF16, [2, 2048])
            ckvn = _carve(scr, 8192, BF16, [2048])
            kr_t = _carve(scr, 12288, BF16, [2048])
            sq3 = _carve(scr, 16384, BF16, [3, 512])
            r1 = _carve(scr, 19456, F32, [512])
            r2 = _carve(scr, 21504, F32, [512])
            sgs = [_carve(scr, 23552 + 2048 * j, F32, [512]) for j in range(2)]
            cos_s = _carve(scr, 27648, F32, [512])
            sin_s = _carve(scr, 29696, F32, [512])
            t1 = _carve(scr, 31744, F32, [512])
            t2 = _carve(scr, 33792, F32, [512])
            Dpad = _carve(ABr, 0, BF16, [4, 2078])
            DPK = [("AB", c, t) for c in range(5) for t in R4]
            wq0, k0 = load_w(wcol(0), 8)
            wq1, k1 = load_w(wcol(128), 8)
            wkv, k2 = load_w(wcol(256), 8)
            for tt in R4:
                ts = slice(tt * 512, (tt + 1) * 512)
                xk = XBk(R8, [tt])
                for j, (wt, wk) in enumerate(((wq0, k0), (wq1, k1), (wkv, k2))):
                    mm(bank(j), [(wt[:, k, :], XB[:, k, ts]) for k in R8], [wk] + xk, [PB(j)])
                A(lambda e: e.activation(sq3, ps[:, 0:1536].rearrange("p (a b) -> p a b", a=3), AF.Square),
                  [PB(0), PB(1), PB(2), SC], ["sq3"])
                mm(bank(3), [(onesb[:], sq3[:, 0, :]), (onesb[:], sq3[:, 1, :])], ["sq3", "onesb"], [PB(3)])
                mm(bank(4), [(onesb[:], sq3[:, 2, :])], ["sq3", "onesb"], [PB(4)])
                A(lambda e: e.activation(r1, bank(3), AF.Ln, bias=float(RMS_EPS), scale=1.0 / 256), [PB(3), SC], ["r1"])
                A(lambda e: e.activation(r1, r1, AF.Exp, scale=-0.5), ["r1"], ["r1"])
                A(lambda e: e.activation(r2, bank(4), AF.Ln, bias=float(RMS_EPS), scale=1.0 / 128), [PB(4), SC], ["r2"])
                A(lambda e: e.activation(r2, r2, AF.Exp, scale=-0.5), ["r2"], ["r2"])
                for j in range(2):
                    V(lambda e, j=j, ts=ts: e.scalar_tensor_tensor(cqn[:, j, ts], bank(j), pc(("qg", i), 1, j), r1,
                                                                   ALU.mult, ALU.mult),
                      [PB(j), "r1", "par"], [("cqn", tt)])
                V(lambda e, ts=ts: e.scalar_tensor_tensor(ckvn[:, ts], bank(2), pc(("kvg", i), 1, 0), r2, ALU.mult, ALU.mult),
                  [PB(2), "r2", "par"], [("ckvn", tt)])
            wA, kA = load_w(wcol(320), 8)
            wB, kB = load_w(wcol(320), 8, extra=[
                (64, 80, win[i, :, 400:416].rearrange("(k p) j -> p k j", p=128)),
                (80, 96, win[i, :, 384:400].rearrange("(k p) j -> p k j", p=128))])
            for tt in R4:
                ts = slice(tt * 512, (tt + 1) * 512)
                xk = XBk(R8, [tt])
                D(lambda e, ts=ts: e.dma_start(out=cos_s[64:96, :], in_=cost_d[64:96, ts]), [SC], ["cos_s"])
                D(lambda e, ts=ts: e.dma_start(out=sin_s[64:96, :], in_=sint_d[64:96, ts]), [SC], ["sin_s"])
                mm(bank(5), [(wA[:, k, :], XB[:, k, ts]) for k in R8], [kA] + xk, [PB(5)])
                mm(bank(6), [(wB[:, k, :], XB[:, k, ts]) for k in R8], [kB] + xk, [PB(6)])
                V(lambda e: e.tensor_tensor(t1[64:96, :], ps[64:96, 512 * 5:512 * 6], cos_s[64:96, :], ALU.mult),
                  [PB(5), "cos_s", SC], ["t1"])
                V(lambda e: e.tensor_tensor(t2[64:96, :], ps[64:96, 512 * 6:512 * 7], sin_s[64:96, :], ALU.mult),
                  [PB(6), "sin_s", SC], ["t2"])
                G(lambda e, ts=ts: e.tensor_tensor(kr_t[64:96, ts], t1[64:96, :], t2[64:96, :], ALU.add),
                  ["t1", "t2"], [("kr", tt)])
            G(lambda e: e.memset(Dpad[:, :, 0:15], 0.0), DPK, DPK)
            G(lambda e: e.memset(Dpad[:, :, 2063:2078], 0.0), DPK, DPK)
            it = 0
            for cc in R4:
                wa, ka = load_w(wcol(416 + 128 * cc), 8)
                wg, kg = load_w(wcol(928 + 128 * cc), 8)
                for tt in R4:
                    ts = slice(tt * 512, (tt + 1) * 512)
                    xk = XBk(R8, [tt])
                    ba, bg = (it % 2) * 2, (it % 2) * 2 + 1
                    sg = sgs[it % 2]
                    sk = ("sg", it % 2)
                    it += 1
                    mm(bank(ba), [(wa[:, k, :], XB[:, k, ts]) for k in R8], [ka] + xk, [PB(ba)])
                    mm(bank(bg), [(wg[:, k, :], XB[:, k, ts]) for k in R8], [kg] + xk, [PB(bg)])
                    A(lambda e, sg=sg, bg=bg: e.activation(sg, bank(bg), AF.Sigmoid), [PB(bg), SC], [sk])
                    V(lambda e, sg=sg, ba=ba, cc=cc, tt=tt: e.tensor_tensor(
                        Dpad[:, cc, 15 + tt * 512:15 + (tt + 1) * 512], bank(ba), sg, ALU.mult),
                      [PB(ba), sk] + DPK, DPK)
            fence()
            diag = _carve(scr, 16384, BF16, [31, 128])
            sq4 = _carve(scr, 24320, BF16, [4, 512])
            m_t = _carve(scr, 28416, F32, [512])
            v_t = _carve(scr, 30464, F32, [512])
            l_t = _carve(scr, 32512, F32, [512])
            c16 = _carve(scr, 34560, BF16, [4, 512])
            Cv = _carve(XBr, 0, F32, [4, 2048])
            CVK = lambda cc, tt: [("XB", 2 * cc, tt), ("XB", 2 * cc + 1, tt)]
            CVA = [k for cc in R4 for tt in R4 for k in CVK(cc, tt)]
            bctr = 0
            diag2 = _carve(ABr, 20480, BF16, [31, 128])
            D2K = [("AB", c_, t_) for c_ in (5, 6) for t_ in R4]
            for cc in R4:
                cwv = pc(("cw", i), 31, cc * 31)
                dg = diag if cc % 2 == 0 else diag2
                dgk = ["diag"] if cc % 2 == 0 else D2K
                G(lambda e, cwv=cwv, dg=dg: e.tensor_tensor(
                    dg, identb[:].unsqueeze(1).broadcast_to([128, 31, 128]),
                    cwv.unsqueeze(2).broadcast_to([128, 31, 128]), ALU.mult),
                  ["identb", "par", SC], dgk)
                for tt in R4:
                    b = bctr % 4
                    bctr += 1
                    mm(bank(b), [(dg[:, j, :], Dpad[:, cc, tt * 512 + j:tt * 512 + j + 512]) for j in range(31)],
                       dgk + DPK, [PB(b)])
                    A(lambda e, cc=cc, tt=tt, b=b: e.activation(Cv[:, cc, tt * 512:(tt + 1) * 512], bank(b), AF.Identity,
                                                                bias=pc(("cb", i), 1, cc)),
                      [PB(b), "par"], CVA if (cc == 0 and tt == 0) else CVK(cc, tt))
            for tt in R4:
                ts = slice(tt * 512, (tt + 1) * 512)
                bA, bB = 4 + 2 * (tt % 2), 5 + 2 * (tt % 2)
                ck = [k for cc in R4 for k in CVK(cc, tt)]
                A(lambda e, ts=ts: e.activation(sq4, Cv[:, :, ts], AF.Square), ck + [SC], ["sq4"])
                G(lambda e, ts=ts: e.tensor_copy(c16, Cv[:, :, ts]), ck + [SC], ["c16"])
                mm(bank(bA), [(onesb[:], c16[:, c, :]) for c in R4], ["c16", "onesb"], [PB(bA)])
                mm(bank(bB), [(onesb[:], sq4[:, c, :]) for c in R4], ["sq4", "onesb"], [PB(bB)])
                A(lambda e, bA=bA: e.activation(m_t, bank(bA), AF.Identity, scale=1.0 / 512), [PB(bA), SC], ["m_t"])
                V(lambda e: e.tensor_tensor(v_t, m_t, m_t, ALU.mult), ["m_t", SC], ["v_t"])
                V(lambda e, bB=bB: e.scalar_tensor_tensor(v_t, bank(bB), 1.0 / 512, v_t, ALU.mult, ALU.subtract),
                  [PB(bB), "v_t"], ["v_t"])
                A(lambda e: e.activation(l_t, v_t, AF.Ln, bias=float(LN_EPS)), ["v_t", SC], ["l_t"])
                A(lambda e, bA=bA: e.activation(bank(bA), l_t, AF.Exp, scale=-0.5), ["l_t", "m_t"], [PB(bA)])
                V(lambda e, bA=bA, bB=bB: e.scalar_tensor_tensor(bank(bB), m_t, -1.0, bank(bA), ALU.mult, ALU.mult),
                  ["m_t", PB(bA), "v_t"], [PB(bB)])
                V(lambda e, ts=ts, bA=bA: e.tensor_tensor(
                    Cv[:, :, ts], Cv[:, :, ts], bank(bA).unsqueeze(1).broadcast_to([128, 4, 512]), ALU.mult),
                  ck + [PB(bA), "sq4", "c16"], ck)
                V(lambda e, ts=ts, bB=bB: e.tensor_tensor(
                    Cv[:, :, ts], Cv[:, :, ts], bank(bB).unsqueeze(1).broadcast_to([128, 4, 512]), ALU.add),
                  ck + [PB(bB)], ck)
                for cc in R4:
                    A(lambda e, cc=cc, ts=ts: e.activation(ABT[:, 4 + cc, ts], Cv[:, cc, ts], AF.Silu,
                                                           bias=pc(("clb", i), 1, cc), scale=pc(("clg", i), 1, cc)),
                      CVK(cc, tt) + ["par"] + (DPK if (cc == 0) else []), [("AB", 4 + cc, tt)])
            fence()
            QT = _carve(scr, 16384, BF16, [2048])
            KT = _carve(scr, 20480, BF16, [2048])
            VA = [_carve(scr, 24576 + 4096 * j, BF16, [16, 128]) for j in range(2)]
            PT2 = [_carve(scr, 32768 + 2048 * j, BF16, [1024]) for j in range(3)]
            pend = []
            cosF = _carve(XBr, 0, F32, [2048])
            sinF = _carve(XBr, 8192, F32, [2048])
            rcs = [_carve(XBr, 16384 + 2048 * j, F32, [512]) for j in range(2)]
            u1 = _carve(XBr, 20480, F32, [512])
            u2 = _carve(XBr, 22528, F32, [512])
            XBall = XBk(R8, R4)
            D(lambda e: e.dma_start(out=cosF[64:96, :], in_=cost_d[64:96, :]), XBall, XBall)
            D(lambda e: e.dma_start(out=sinF[64:96, :], in_=sint_d[64:96, :]), XBall, XBall)
            TBL = [("XB", 0, 0)]
            G(lambda e: e.memset(VA[0][:, :, 64:128], 1.0), [SC], ["VAones0"])
            G(lambda e: e.memset(VA[1][:, :, 0:64], 1.0), [SC], ["VAones1"])
            scale = 96 ** -0.5
            it = 0
            sbctr = 0
            QTs = [QT, _carve(XBr, 24576, BF16, [2048])]
            KTs = [KT, _carve(XBr, 28672, BF16, [2048])]
            wkv_of = {}

            def proj(h):
                QT, KT = QTs[h % 2], KTs[h % 2]
                hb_ = h % 2
                hh = h % 2
                p0 = 64 * hh
                c = h // 2
                uq = wbf["w_uq"]
                wa, ka = load_w(uq[i, :, 96 * h:96 * h + 96].rearrange("(k p) j -> p k j", p=128), 2, ncol=96)
                wb_, kb = load_w(uq[i, :, 96 * h:96 * h + 96].rearrange("(k p) j -> p k j", p=128), 2, ncol=96, extra=[
                    (64, 80, uq[i, :, 96 * h + 80:96 * h + 96].rearrange("(k p) j -> p k j", p=128)),
                    (80, 96, uq[i, :, 96 * h + 64:96 * h + 80].rearrange("(k p) j -> p k j", p=128))])
                wkv_, kkv = load_w(wbf["w_ukv"][i, :, 128 * h:128 * h + 128].rearrange("(k p) j -> p k j", p=128), 1)
                for tt in R4:
                    ts = slice(tt * 512, (tt + 1) * 512)
                    mm(ps[0:96, 512 * 2:512 * 3], [(wa[:, k, :], cqn[:, k, ts]) for k in range(2)], [ka, ("cqn", tt)], [PB(2)])
                    mm(ps[0:96, 512 * 3:512 * 4], [(wb_[:, k, :], cqn[:, k, ts]) for k in range(2)], [kb, ("cqn", tt)], [PB(3)])
                    mm(ps[0:64, 512 * 4:512 * 5], [(wkv_[:, 0, 0:64], ckvn[:, ts])], [kkv, ("ckvn", tt)], [PB(4)])
                    V(lambda e, ts=ts: e.tensor_copy(QT[0:64, ts], ps[0:64, 1024:1536]), [PB(2), SC] + TBL, [("QTm", hb_, tt, 0)])
                    V(lambda e, ts=ts: e.tensor_tensor(u1[64:96, :], ps[64:96, 1024:1536], cosF[64:96, ts], ALU.mult),
                      [PB(2)] + TBL, ["u1"])
                    V(lambda e, ts=ts: e.tensor_tensor(u2[64:96, :], ps[64:96, 1536:2048], sinF[64:96, ts], ALU.mult),
                      [PB(3)] + TBL, ["u2"])
                    G(lambda e, ts=ts: e.tensor_tensor(QT[64:96, ts], u1[64:96, :], u2[64:96, :], ALU.add),
                      ["u1", "u2", SC] + TBL, [("QTm", hb_, tt, 1)])
                    A(lambda e, ts=ts: e.activation(KT[0:64, ts], ps[0:64, 2048:2560], AF.Copy), [PB(4), SC] + TBL, [("KTm", hb_, tt, 0)])
                    G(lambda e, ts=ts: e.tensor_copy(KT[64:96, ts], kr_t[64:96, ts]), [("kr", tt), SC] + TBL, [("KTm", hb_, tt, 1)])
                va = VA[hh]
                for t8 in range(2):
                    b = 5

                    def f(e, t8=t8, b=b, wkv_=wkv_):
                        for j in range(8):
                            tc = 8 * t8 + j
                            ins = e.matmul(ps[:, 512 * b + 64 * j:512 * b + 64 * (j + 1)], ckvn[:, tc * 128:(tc + 1) * 128],
                                           wkv_[:, 0, 64:128], start=True, stop=True)
                        return ins
                    T(f, [kkv, ("ckvn", 2 * t8), ("ckvn", 2 * t8 + 1)], [PB(b)])
                    src = bank(b).rearrange("p (a b) -> p a b", a=8)
                    V(lambda e, va=va, t8=t8, src=src, hh=hh: e.tensor_copy(
                        va[:, 8 * t8:8 * t8 + 8, 64 * hh:64 * hh + 64], src),
                      [PB(b), SC, "VAones%d" % hh], [("VA", hh, t8)])

            def attn(h):
                nonlocal it, sbctr
                QT, KT = QTs[h % 2], KTs[h % 2]
                hb_ = h % 2
                hh = h % 2
                p0 = 64 * hh
                c = h // 2
                va = VA[hh]
                for tt in R4:
                    ts = slice(tt * 512, (tt + 1) * 512)
                    ob = tt % 2
                    OB = bank(ob)
                    for j2 in range(8):
                        sb0 = 2 + 2 * (sbctr % 3)
                        sbctr += 1
                        for u in range(2):
                            kc = 2 * j2 + u
                            mm(bank(sb0 + u), [(KT[0:96, kc * 128:(kc + 1) * 128], QT[0:96, ts])],
                               [("KTm", hb_, kc // 4, 0), ("KTm", hb_, kc // 4, 1), ("QTm", hb_, tt, 0), ("QTm", hb_, tt, 1)], [PB(sb0 + u)])
                        pt = PT2[it % 3]
                        ptk = ("PT", it % 3)
                        it += 1
                        A(lambda e, pt=pt, sb0=sb0: e.activation(pt, ps[:, 512 * sb0:512 * sb0 + 1024], AF.Exp, scale=scale),
                          [PB(sb0), PB(sb0 + 1), SC], [ptk])

                        def st2(OB=OB, va=va, j2=j2, pt=pt, ptk=ptk, hh=hh, ob=ob):
                            def f(e):
                                for u in range(2):
                                    kc = 2 * j2 + u
                                    ins = e.matmul(OB, va[:, kc, :], pt[:, 512 * u:512 * (u + 1)], start=(kc == 0), stop=(kc == 15))
                                return ins
                            T(f, [ptk, ("VA", hh, j2 // 4), "VAones%d" % hh], [PB(ob)])
                        pend.append(st2)
                        while len(pend) > 2:
                            pend.pop(0)()

                    def st3(ob=ob, OB=OB, p0=p0, c=c, ts=ts, tt=tt):
                        rc = rcs[ob]
                        rk = ("rc", ob)
                        dn = 64 - p0
                        A(lambda e: e.activation(rc[p0:p0 + 64, :], OB[dn:dn + 64, :], AF.Ln), [PB(ob)] + TBL, [rk])
                        A(lambda e: e.activation(rc[p0:p0 + 64, :], rc[p0:p0 + 64, :], AF.Exp, scale=-1.0), [rk], [rk])
                        V(lambda e: e.tensor_tensor(ABT[p0:p0 + 64, c, ts], OB[p0:p0 + 64, :], rc[p0:p0 + 64, :], ALU.mult),
                          [PB(ob), rk], [("AB", c, tt)])
                    pend.append(st3)
                while pend:
                    pend.pop(0)()

            proj(0)
            for h in R8:
                if h + 1 < 8:
                    proj(h + 1)
                attn(h)
            G(lambda e: e.memset(dumt[:, 1:2], 0.0), ["u1", "u2", ("rc", 0), ("rc", 1)] + TBL,
              XBall + [(nm, 1, t_, p_) for nm in ("QTm", "KTm") for t_ in R4 for p_ in range(2)])
            w_out_phase("w_out_odd", i)

        PLAN = os.environ.get("KPLAN", "")
        if PLAN:
            for j, tok in enumerate(PLAN.split(",")):
                S.epoch = 1 + j // 2
                if tok[0] == "L":
                    load_x(int(tok[1]), j == 0)
                else:
                    store_y(int(tok[1]))
            nseq = 0
        for s in range(nseq):
            S.epoch = (s + 1) if 'noepoch' not in SKIP else 0
            load_x(s, s == 0)
            for l in range(nlayers):
                if l % 2 == 0:
                    even_mixer(l // 2)
                else:
                    odd_mixer(l // 2)
                ln_phase("ln1g", "ln1b", l)
                if do_ffn:
                    ffn_phase(l)
                    ln_phase("ln2g", "ln2b", l)
            store_y(s)
        S.emit()
    return nc


_NC_CACHE = {}


def _get_nc(nseq, nlayers=DEPTH, do_ffn=True):
    k = (nseq, nlayers, do_ffn)
    if k not in _NC_CACHE:
        _NC_CACHE[k] = build_nc(nseq, nlayers, do_ffn)
    return _NC_CACHE[k]


def _common_inputs(inp):
    c = _consts()
    d = {n: np.ascontiguousarray(inp[n], dtype=np.float32) for n, _ in WSPECS if n != "sgu_wT"}
    d["sgu_wT"] = np.ascontiguousarray(np.transpose(inp["sgu_w"], (0, 1, 3, 2)), dtype=np.float32)
    rp = np.zeros((2, 8, 15, 127), np.float32)
    rp[..., 48:79] = inp["rpb"]
    d["rpb_pad"] = rp
    d["sgu_ln_g"] = np.ascontiguousarray(inp["sgu_ln_g"], dtype=np.float32)
    d["sgu_ln_b"] = np.ascontiguousarray(inp["sgu_ln_b"], dtype=np.float32)
    d["sgu_b"] = np.ascontiguousarray(np.asarray(inp["sgu_b"], dtype=np.float32).reshape(2, 512))
    d["params"] = _pack_params(inp)
    d.update(c)
    return d


def kernel(**inputs):
    inp = {k: np.asarray(v) for k, v in inputs.items()}
    xs = np.concatenate([inp["x_prompt"], inp["x_sample"]], axis=0).astype(np.float32, copy=False)
    nseq = xs.shape[0] // N_CORES
    nc = _get_nc(nseq)
    common = _common_inputs(inp)
    in_maps = []
    for c in range(N_CORES):
        m = dict(common)
        m["x"] = np.ascontiguousarray(xs[c * nseq:(c + 1) * nseq])
        in_maps.append(m)
    res = run_bass_kernel_spmd(nc, in_maps, core_ids=list(range(N_CORES)))
    ys = np.concatenate([r["y"] for r in res.results], axis=0)
    nb = inp["x_prompt"].shape[0]
    return (np.ascontiguousarray(ys[:nb]), np.ascontiguousarray(ys[nb:]))
```

```python
import contextlib
import numpy as np
import concourse.bass as bass
import concourse.mybir as mybir
from concourse.bass_utils import run_bass_kernel_spmd

F32 = mybir.dt.float32
BF16 = mybir.dt.bfloat16
U8 = mybir.dt.uint8
AF = mybir.ActivationFunctionType
ALU = mybir.AluOpType

S_LEN = 2048
DM = 1024
DEPTH = 4
DFF = 2816
ALPHA = float((2 * DEPTH) ** 0.25)
LN_EPS = 1e-5
RMS_EPS = 1e-6
N_CORES = 8


class _Op:
    __slots__ = ("eng", "fn", "deps", "signal", "count", "is_dma", "lane", "lane_k", "epoch")

    def __init__(self, eng, fn, is_dma):
        self.eng = eng
        self.fn = fn
        self.deps = []
        self.signal = False
        self.count = 0
        self.is_dma = is_dma
        self.lane = 0
        self.lane_k = 0
        self.epoch = 0


class Sched:
    ENG_ATTR = {"pe": "tensor", "act": "scalar", "dve": "vector", "pool": "gpsimd", "sp": "sync"}

    def __init__(self, nc, n_lanes=8, same_eng_sync=True):
        self.nc = nc
        self.ops = {k: [] for k in self.ENG_ATTR}
        self.res_w = {}
        self.res_r = {}
        self.n_lanes = n_lanes
        self.dma_cnt = {k: 0 for k in self.ENG_ATTR}
        self.same_eng_sync = same_eng_sync
        self.epoch = 0

    def _record(self, o, reads, writes):
        o.epoch = self.epoch
        deps = {}
        res_w, res_r = self.res_w, self.res_r
        for r in reads:
            w = res_w.get(r)
            if w is not None:
                deps[id(w)] = w
        for r in writes:
            w = res_w.get(r)
            if w is not None:
                deps[id(w)] = w
            rr = res_r.get(r)
            if rr:
                for x in rr.values():
                    deps[id(x)] = x
        eng = o.eng
        dl = []
        for d in deps.values():
            if d is o:
                continue
            if d.eng == eng and not d.is_dma and not o.is_dma:
                if eng == "pe" or not self.same_eng_sync:
                    continue
            dl.append(d)
        o.deps = dl
        for r in writes:
            res_w[r] = o
            res_r[r] = {}
        for r in reads:
            rr = res_r.get(r)
            if rr is None:
                rr = res_r[r] = {}
            rr[eng if not o.is_dma else (eng, o.lane)] = o
        self.ops[eng].append(o)
        return o

    def op(self, eng, fn, reads=(), writes=()):
        return self._record(_Op(eng, fn, False), reads, writes)

    def dma(self, eng, fn, reads=(), writes=()):
        o = _Op(eng, fn, True)
        i = self.dma_cnt[eng]
        self.dma_cnt[eng] = i + 1
        o.lane = i % self.n_lanes
        o.lane_k = i // self.n_lanes + 1
        return self._record(o, reads, writes)

    def emit(self):
        import os
        TRACE = bool(os.environ.get("KTRACE"))
        nc = self.nc
        for eng, lst in self.ops.items():
            for o in lst:
                for d in o.deps:
                    d.signal = True
        with contextlib.ExitStack() as st:
            sems = {}
            for eng in self.ops:
                for ep in sorted(set(o.epoch for o in self.ops[eng])):
                    sems[(eng, ep)] = st.enter_context(nc.semaphore("s_%s_%d" % (eng, ep)))
            lanes = {}
            for eng in self.ops:
                if self.dma_cnt[eng] > 0:
                    lanes[eng] = [st.enter_context(nc.semaphore("l_%s%d" % (eng, i)))
                                  for i in range(min(self.n_lanes, self.dma_cnt[eng]))]
            for eng, lst in self.ops.items():
                cc = {}
                for o in lst:
                    if not o.is_dma and o.signal:
                        c = cc.get(o.epoch, 0) + 1
                        cc[o.epoch] = c
                        o.count = c
            block = st.enter_context(nc.Block())

            def siginfo(d):
                if d.is_dma:
                    return lanes[d.eng][d.lane], 16 * d.lane_k
                return sems[(d.eng, d.epoch)], d.count

            def body_for(eng):
                lst = self.ops[eng]

                def body(e):
                    waited = {}
                    last_dma = {}
                    for oi, o in enumerate(lst):
                        for d in o.deps:
                            s, v = siginfo(d)
                            k = id(s)
                            if waited.get(k, 0) < v:
                                e.wait_ge(s, v)
                                waited[k] = v
                                if TRACE:
                                    print("  %s op%d waits %s(ep%d lane%d) >= %d" % (eng, oi, d.eng, d.epoch, d.lane if d.is_dma else -1, v))
                        if TRACE:
                            print("%s op%d %s signal=%s count=%d ep=%d lane=%d k=%d" % (eng, oi, "DMA" if o.is_dma else "OP", o.signal, o.count, o.epoch, o.lane, o.lane_k))
                        if o.is_dma:
                            s = lanes[eng][o.lane]
                            if o.lane_k > 1:
                                k = id(s)
                                v = 16 * (o.lane_k - 1)
                                if waited.get(k, 0) < v:
                                    e.wait_ge(s, v)
                                    waited[k] = v
                            ins = o.fn(e)
                            ins.then_inc(s, 16)
                            last_dma[o.lane] = o
                        else:
                            ins = o.fn(e)
                            if o.signal:
                                ins.then_inc(sems[(eng, o.epoch)], 1)
                    for lane, o in last_dma.items():
                        s = lanes[eng][lane]
                        v = 16 * o.lane_k
                        if waited.get(id(s), 0) < v:
                            e.wait_ge(s, v)
                return body

            for eng, attr in self.ENG_ATTR.items():
                if self.ops[eng]:
                    getattr(block, attr)(body_for(eng))


def _carve(t, off, dt, dims):
    n = int(np.prod(dims))
    sz = 2 if dt == BF16 else 4
    ap = t[:, off:off + n * sz].bitcast(dt)
    if len(dims) == 2:
        ap = ap.rearrange("p (a b) -> p a b", a=dims[0])
    elif len(dims) == 3:
        ap = ap.rearrange("p (a b c) -> p a b c", a=dims[0], b=dims[1])
    return ap


def _param_layout():
    off = {}
    n = 0
    for l in range(DEPTH):
        for nm, w in (("ln1g", 8), ("ln1b", 8), ("ln2g", 8), ("ln2b", 8), ("fcw", 132), ("fcb", 44)):
            off[(nm, l)] = n
            n += w
    for i in range(2):
        for nm, w in (("qg", 2), ("kvg", 1), ("cw", 124), ("cb", 4), ("clg", 4), ("clb", 4)):
            off[(nm, i)] = n
            n += w
    return off, n


POFF, NPAR = _param_layout()


def _pack_params(inp):
    P = np.zeros((128, NPAR), np.float32)

    def put(key, arr):
        a = np.ascontiguousarray(arr, dtype=np.float32).reshape(128, -1)
        P[:, POFF[key]:POFF[key] + a.shape[1]] = a

    for l in range(DEPTH):
        put(("ln1g", l), inp["ln1_g"][l].reshape(8, 128).T)
        put(("ln1b", l), inp["ln1_b"][l].reshape(8, 128).T)
        put(("ln2g", l), inp["ln2_g"][l].reshape(8, 128).T)
        put(("ln2b", l), inp["ln2_b"][l].reshape(8, 128).T)
        put(("fcw", l), inp["ffn_conv_w"][l].reshape(3, 44, 128).transpose(2, 1, 0))
        put(("fcb", l), inp["ffn_conv_b"][l].reshape(44, 128).T)
    for i in range(2):
        put(("qg", i), inp["q_norm_g"][i].reshape(2, 128).T)
        put(("kvg", i), inp["kv_norm_g"][i].reshape(1, 128).T)
        put(("cw", i), inp["conv_w"][i].reshape(31, 4, 128).transpose(2, 1, 0))
        put(("cb", i), inp["conv_b"][i].reshape(4, 128).T)
        put(("clg", i), inp["conv_ln_g"][i].reshape(4, 128).T)
        put(("clb", i), inp["conv_ln_b"][i].reshape(4, 128).T)
    return P


def _consts():
    c = {}
    c["ident"] = np.eye(128, dtype=np.float32)
    cq = np.arange(64)
    c0 = np.clip(cq - 8, 0, 48)
    ck = np.arange(64)
    m = ((ck[:, None] >= c0[None, :]) & (ck[:, None] < c0[None, :] + 16)).astype(np.float32)
    c["namask"] = np.concatenate([m, m], axis=0)
    pos = np.arange(S_LEN, dtype=np.float32)
    inv = (10000.0 ** (-np.arange(0, 32, 2, dtype=np.float32) / 32)).astype(np.float32)
    ang = (pos[:, None] * inv[None, :]).astype(np.float32)
    cos = np.cos(ang).astype(np.float32).T
    sin = np.sin(ang).astype(np.float32).T
    ct = np.zeros((128, S_LEN), np.float32)
    sn = np.zeros((128, S_LEN), np.float32)
    ct[64:80] = cos
    ct[80:96] = cos
    sn[64:80] = -sin
    sn[80:96] = sin
    c["cost"] = ct
    c["sint"] = sn
    return c


WSPECS = [
    ("w_in_even", [2, 1024, 2560]), ("w_out_even", [2, 1024, 1024]), ("w_in_odd", [2, 1024, 1440]),
    ("w_uq", [2, 256, 768]), ("w_ukv", [2, 128, 1024]), ("w_out_odd", [2, 1024, 1024]),
    ("ffn_w_up", [4, 1024, 5632]), ("ffn_w_down", [4, 2816, 1024]), ("sgu_wT", [2, 4, 128, 128]),
]


def build_nc(nseq, nlayers=DEPTH, do_ffn=True):
    nc = bass.Bass("TRN2", target_bir_lowering=False)
    din = lambda n, s, d=F32: nc.dram_tensor(n, list(s), d, kind="ExternalInput").ap()
    x_d = din("x", [nseq, S_LEN, DM])
    y_d = nc.dram_tensor("y", [nseq, S_LEN, DM], F32, kind="ExternalOutput").ap()
    wsrc = {n: din(n, s) for n, s in WSPECS}
    wbf = {n: nc.dram_tensor(n + "_bf", list(s), BF16, kind="Internal").ap() for n, s in WSPECS}
    rpbpad_d = din("rpb_pad", [2, 8, 15, 127])
    sgu_lng_d = din("sgu_ln_g", [2, 512])
    sgu_lnb_d = din("sgu_ln_b", [2, 512])
    sgu_b_d = din("sgu_b", [2, 512])
    params_d = din("params", [128, NPAR])
    ident_d = din("ident", [128, 128])
    namask_d = din("namask", [128, 64])
    cost_d = din("cost", [128, S_LEN])
    sint_d = din("sint", [128, S_LEN])
    ebtab_d = nc.dram_tensor("ebtab", [2, 8, 128, 960], F32, kind="Internal").ap()
    dummy_d = nc.dram_tensor("dummy_scr", [1, 64], F32, kind="Internal").ap()

    import os
    DBG = os.environ.get("KDBG", "")
    if DBG:
        dbg_d = nc.dram_tensor("dbg", [128, 16384], BF16, kind="ExternalOutput").ap()
    S = Sched(nc, n_lanes=int(os.environ.get('KLANES', '8')))
    NSLOT = 6
    SLOTW = 1408
    SCRB = 40960

    with contextlib.ExitStack() as st:
        sb = lambda n, s, d: st.enter_context(nc.sbuf_tensor(n, s, d))
        XT = sb("XT", [128, 8, S_LEN], F32)
        XBr = sb("XBr", [128, 32768], U8)
        ABr = sb("ABr", [128, 32768], U8)
        wsl = sb("wsl", [128, NSLOT, SLOTW], BF16)
        scr = sb("scr", [128, SCRB], U8)
        par = sb("par", [128, NPAR], F32)
        identf = sb("identf", [128, 128], F32)
        identb = sb("identb", [128, 128], BF16)
        onesb = sb("onesb", [128, 128], BF16)
        onesf = sb("onesf", [1, 128], F32)
        dumt = sb("dumt", [128, 2], F32)
        ps = st.enter_context(nc.psum_tensor("ps", [128, 4096], F32))

        XB = _carve(XBr, 0, BF16, [8, S_LEN])
        ABT = _carve(ABr, 0, BF16, [8, S_LEN])
        bank = lambda b: ps[:, 512 * b:512 * (b + 1)]
        PB = lambda b: ("ps", b)
        SC = "SCR"

        def pc(key, w=1, j=0):
            o = POFF[key] + j
            return par[:, o:o + w]

        A = lambda fn, r=(), w=(): S.op("act", fn, list(r) + [SC], w)
        V = lambda fn, r=(), w=(): S.op("dve", fn, list(r) + [SC], w)
        G = lambda fn, r=(), w=(): S.op("pool", fn, list(r) + [SC], w)
        T = lambda fn, r=(), w=(): S.op("pe", fn, list(r) + [SC], w)
        D = lambda fn, r=(), w=(), nosc=False: S.dma("sp", fn, list(r) + ([] if nosc else [SC]), w)

        def fence():
            S.op("pool", lambda e: e.memset(dumt[:, 0:1], 0.0), [], [SC])

        def mm(out, pairs, r, w):
            def f(e):
                n = len(pairs)
                for i, (l, rh) in enumerate(pairs):
                    ins = e.matmul(out, l, rh, start=(i == 0), stop=(i == n - 1))
                return ins
            T(f, r, w)

        wctr = [0]

        def load_w(src_ap, nk, ncol=128, extra=None):
            slot = wctr[0] % NSLOT
            wctr[0] += 1
            dst = wsl[:, slot, 0:nk * ncol].rearrange("p (k j) -> p k j", j=ncol)
            key = ("w", slot)
            D(lambda e: e.dma_start(out=dst, in_=src_ap), ["dw"], [key], nosc=True)
            if extra:
                for (c0, c1, sap) in extra:
                    D(lambda e, c0=c0, c1=c1, sap=sap: e.dma_start(out=dst[:, :, c0:c1], in_=sap), ["dw"], [key], nosc=True)
            return dst, key

        D(lambda e: e.dma_start(out=par[:], in_=params_d), [], ["par"])
        D(lambda e: e.dma_start(out=identf[:], in_=ident_d), [], ["identf"])
        V(lambda e: e.tensor_copy(identb[:], identf[:]), ["identf"], ["identb"])
        G(lambda e: e.memset(onesb[:], 1.0), [], ["onesb"])
        G(lambda e: e.memset(onesf[:], 1.0), [], ["onesf"])

        st32 = [XT[:, 2 * i:2 * i + 2, :].rearrange("p a b -> p (a b)") for i in range(4)]
        st16 = [_carve(XBr, 8192 * i, BF16, [4096]) for i in range(4)]
        pst = []
        ci = 0
        import os
        SKIP = os.environ.get("KSKIP", "").split(",")
        for name, shp in WSPECS:
            if "conv" in SKIP:
                break
            n = int(np.prod(shp))
            per = n // 128
            letters = " ".join("abcd"[:len(shp)])
            sv = wsrc[name].rearrange("%s -> (%s)" % (letters, letters)).rearrange("(p f) -> p f", p=128)
            dv = wbf[name].rearrange("%s -> (%s)" % (letters, letters)).rearrange("(p f) -> p f", p=128)
            for c0 in range(0, per, 4096):
                w = min(4096, per - c0)
                sl = ci % 4
                D(lambda e, sl=sl, w=w, c0=c0, sv=sv: e.dma_start(out=st32[sl][:, 0:w], in_=sv[:, c0:c0 + w]),
                  [], [("st32", sl)])
                eng = ("act", "dve")[ci % 2]
                if eng == "act":
                    A(lambda e, sl=sl, w=w: e.activation(st16[sl][:, 0:w], st32[sl][:, 0:w], AF.Copy),
                      [("st32", sl)], [("st16", sl)])
                else:
                    S.op(eng, lambda e, sl=sl, w=w: e.tensor_copy(st16[sl][:, 0:w], st32[sl][:, 0:w]),
                         [("st32", sl)], [("st16", sl)])
                k = ("pst", ci)
                D(lambda e, sl=sl, w=w, c0=c0, dv=dv: e.dma_start(out=dv[:, c0:c0 + w], in_=st16[sl][:, 0:w]),
                  [("st16", sl)], [k])
                pst.append(k)
                ci += 1
        hk = _carve(scr, 0, F32, [15, 64])
        eb = _carve(scr, 4096, F32, [15, 64])
        mk = _carve(scr, 8192, F32, [64])
        D(lambda e: e.dma_start(out=mk, in_=namask_d), [], ["mk"])
        for i in range(2):
            if "eb" in SKIP:
                break
            for h in range(8):
                base = ((i * 8 + h) * 15) * 127
                src = bass.AP(tensor=rpbpad_d.tensor, offset=base, ap=[[1, 64], [127, 15], [1, 64]])
                src2 = bass.AP(tensor=rpbpad_d.tensor, offset=base + 127, ap=[[1, 64], [127, 14], [1, 64]])
                D(lambda e, src=src: e.dma_start(out=hk[0:64], in_=src), [], ["hk0"])
                G(lambda e: e.memset(hk[64:128, 14:15, :], 0.0), [], ["hk1"])
                D(lambda e, src2=src2: e.dma_start(out=hk[64:128, 0:14, :], in_=src2), [], ["hk1"])
                A(lambda e: e.activation(hk, hk, AF.Exp), ["hk0", "hk1"], ["hk0", "hk1"])
                V(lambda e: e.tensor_tensor(eb, hk[:, :, ::-1], mk.unsqueeze(1).broadcast_to([128, 15, 64]), ALU.mult),
                  ["hk0", "hk1", "mk"], ["eb"])
                k = ("pst", ci)
                ci += 1
                D(lambda e, i=i, h=h: e.dma_start(out=ebtab_d[i, h].rearrange("p (a b) -> p a b", a=15), in_=eb),
                  ["eb"], [k])
                pst.append(k)
        D(lambda e: e.dma_start(out=dummy_d, in_=ident_d[0:1, 0:64]), pst, ["dw"])
        stage_keys = [("st32", i) for i in range(4)] + [("st16", i) for i in range(4)]

        XTk = lambda cs, tts: [("XT", c, t) for c in cs for t in tts]
        XBk = lambda cs, tts: [("XB", c, t) for c in cs for t in tts]
        ABk = lambda cs, tts: [("AB", c, t) for c in cs for t in tts]
        R8 = range(8)
        R4 = range(4)

        def load_x(s, first):
            fence()
            for tc in range(16):
                sl = tc % 2
                xtok = _carve(scr, 4096 * sl, F32, [1024])
                tt = tc // 4
                D(lambda e, xtok=xtok, tc=tc: e.dma_start(out=xtok, in_=x_d[s, tc * 128:(tc + 1) * 128, :]),
                  [], [("tok", sl, 0), ("tok", sl, 1)] + (["hk0", "hk1", "eb", "mk"] if first else []))
                for half in range(2):
                    b = (tc * 2 + half) % 8

                    def f(e, xtok=xtok, half=half, b=b):
                        for j in range(4):
                            ins = e.transpose(ps[:, 512 * b + 128 * j:512 * b + 128 * (j + 1)],
                                              xtok[:, (4 * half + j) * 128:(4 * half + j + 1) * 128], identf[:])
                        return ins
                    T(f, [("tok", sl, 0), ("tok", sl, 1), "identf"], [PB(b)])
                    cs = range(4 * half, 4 * half + 4)
                    src = bank(b).rearrange("p (a b) -> p a b", a=4)
                    extra = stage_keys if first else []
                    A(lambda e, half=half, tc=tc, src=src: e.activation(
                        XT[:, 4 * half:4 * half + 4, tc * 128:(tc + 1) * 128], src, AF.Copy),
                      [PB(b)], XTk(cs, [tt]) + extra)
                    V(lambda e, half=half, tc=tc: e.tensor_copy(
                        XB[:, 4 * half:4 * half + 4, tc * 128:(tc + 1) * 128],
                        XT[:, 4 * half:4 * half + 4, tc * 128:(tc + 1) * 128]),
                      XTk(cs, [tt]), XBk(cs, [tt]) + extra)

        def store_y(s):
            fence()
            for tc in range(16):
                sl = tc % 2
                ytok = _carve(scr, 4096 * sl, F32, [1024])
                tt = tc // 4
                for half in range(2):
                    b = (tc * 2 + half) % 8

                    def f(e, half=half, b=b, tc=tc):
                        for j in range(4):
                            ins = e.transpose(ps[:, 512 * b + 128 * j:512 * b + 128 * (j + 1)],
                                              XT[:, 4 * half + j, tc * 128:(tc + 1) * 128], identf[:])
                        return ins
                    T(f, XTk(range(4 * half, 4 * half + 4), [tt]) + ["identf"], [PB(b)])
                    if half == 0:
                        A(lambda e, ytok=ytok, b=b: e.activation(ytok[:, 0:512], bank(b), AF.Copy),
                          [PB(b), SC], [("tok", sl, 0)])
                    else:
                        V(lambda e, ytok=ytok, b=b: e.tensor_copy(ytok[:, 512:1024], bank(b)),
                          [PB(b), SC], [("tok", sl, 1)])
                D(lambda e, ytok=ytok, tc=tc: e.dma_start(out=y_d[s, tc * 128:(tc + 1) * 128, :], in_=ytok),
                  [("tok", sl, 0), ("tok", sl, 1)], [("tok", sl, 0), ("tok", sl, 1)])

        def ln_stats(tt, src_fn, nch, scale_n, eps, bA, bB, sq, x16, m_t, v_t, l_t, rkeys, x16keys):
            pass

        def ln_phase(gkey, bkey, l):
            fence()
            if "ln" in SKIP:
                return
            sqs = [_carve(scr, 8192 * j, BF16, [8, 512]) for j in range(2)]
            m_ts = [_carve(scr, 16384 + 2048 * j, F32, [512]) for j in range(2)]
            v_ts = [_carve(scr, 20480 + 2048 * j, F32, [512]) for j in range(2)]
            l_ts = [_carve(scr, 24576 + 2048 * j, F32, [512]) for j in range(2)]

            def s1(tt):
                ts = slice(tt * 512, (tt + 1) * 512)
                bA, bB = 2 * tt, 2 * tt + 1
                sq = sqs[tt % 2]
                sqk = ("sq", tt % 2)
                xk = XTk(R8, [tt])
                A(lambda e: e.activation(sq, XT[:, :, ts], AF.Square), xk, [sqk])
                V(lambda e: e.tensor_copy(XB[:, :, ts], XT[:, :, ts]), xk, XBk(R8, [tt]))
                mm(bank(bA), [(onesb[:], XB[:, c, ts]) for c in R8], XBk(R8, [tt]) + ["onesb"], [PB(bA)])
                mm(bank(bB), [(onesb[:], sq[:, c, :]) for c in R8], [sqk, "onesb"], [PB(bB)])

            def s2(tt):
                bA, bB = 2 * tt, 2 * tt + 1
                m_t, v_t, l_t = m_ts[tt % 2], v_ts[tt % 2], l_ts[tt % 2]
                mk_, vk_, lk_ = ("m_t", tt % 2), ("v_t", tt % 2), ("l_t", tt % 2)
                A(lambda e: e.activation(m_t, bank(bA), AF.Identity, scale=1.0 / DM), [PB(bA)], [mk_])
                A(lambda e: e.activation(v_t, bank(bA), AF.Square, scale=1.0 / DM), [PB(bA)], [vk_])
                V(lambda e: e.scalar_tensor_tensor(v_t, bank(bB), 1.0 / DM, v_t, ALU.mult, ALU.subtract),
                  [PB(bB), vk_], [vk_])
                A(lambda e: e.activation(l_t, v_t, AF.Ln, bias=float(LN_EPS)), [vk_], [lk_])
                A(lambda e: e.activation(bank(bA), l_t, AF.Exp, scale=-0.5), [lk_, mk_], [PB(bA)])
                V(lambda e: e.scalar_tensor_tensor(bank(bB), m_t, -1.0, bank(bA), ALU.mult, ALU.mult),
                  [mk_, PB(bA), vk_], [PB(bB)])

            def s3(tt):
                ts = slice(tt * 512, (tt + 1) * 512)
                bA, bB = 2 * tt, 2 * tt + 1
                xk = XTk(R8, [tt])
                V(lambda e: e.tensor_tensor(
                    XT[:, :, ts], XT[:, :, ts], bank(bA).unsqueeze(1).broadcast_to([128, 8, 512]), ALU.mult),
                  xk + [PB(bA)], xk)
                V(lambda e: e.tensor_tensor(
                    XT[:, :, ts], XT[:, :, ts], bank(bB).unsqueeze(1).broadcast_to([128, 8, 512]), ALU.add),
                  xk + [PB(bB)], xk)

            def s4(tt):
                ts = slice(tt * 512, (tt + 1) * 512)
                for c in R8:
                    gs = pc((gkey, l), 1, c)
                    bs = pc((bkey, l), 1, c)
                    A(lambda e, c=c, gs=gs, bs=bs: e.activation(XT[:, c, ts], XT[:, c, ts], AF.Identity, bias=bs, scale=gs),
                      [("XT", c, tt), "par"], [("XT", c, tt)])
                V(lambda e: e.tensor_copy(XB[:, :, ts], XT[:, :, ts]), XTk(R8, [tt]), XBk(R8, [tt]))

            for st_fn, tt_ in ((s1, 0), (s1, 1), (s2, 0), (s1, 2), (s2, 1), (s3, 0), (s1, 3), (s2, 2), (s3, 1), (s4, 0),
                               (s2, 3), (s3, 2), (s4, 1), (s3, 3), (s4, 2), (s4, 3)):
                st_fn(tt_)

        def w_out_phase(wname, i):
            fence()
            if DBG:
                D(lambda e: e.dma_start(out=dbg_d, in_=_carve(ABr, 0, BF16, [16384])), ABk(R8, R4), [])
            if "wout" in SKIP:
                return
            bctr = 0
            for oc in R8:
                wt, wk = load_w(wbf[wname][i, :, oc * 128:(oc + 1) * 128].rearrange("(k p) j -> p k j", p=128), 8)
                for tt in R4:
                    ts = slice(tt * 512, (tt + 1) * 512)
                    b = bctr % 8
                    bctr += 1
                    mm(bank(b), [(wt[:, k, :], ABT[:, k, ts]) for k in R8], [wk] + ABk(R8, [tt]), [PB(b)])
                    V(lambda e, oc=oc, ts=ts, b=b: e.scalar_tensor_tensor(
                        XT[:, oc, ts], XT[:, oc, ts], ALPHA, bank(b), ALU.mult, ALU.add),
                      [PB(b), ("XT", oc, tt)], [("XT", oc, tt)])

        def ffn_phase(l):
            fence()
            if "ffn" in SKIP:
                return
            groups = [(0, 6), (6, 12), (12, 17), (17, 22)]
            Abuf = ABT
            for gi, (c0, c1) in enumerate(groups):
                for ci_ in range(c0, c1):
                    a = ci_ - c0
                    for half_gv in range(2):
                        col0 = ci_ * 128 + (DFF if half_gv else 0)
                        ch = ci_ + (22 if half_gv else 0)
                        wt, wk = load_w(wbf["ffn_w_up"][l, :, col0:col0 + 128].rearrange("(k p) j -> p k j", p=128), 8)
                        b0 = 4 * half_gv
                        hb = [PB(b0 + j) for j in R4]
                        for tt in R4:
                            ts = slice(tt * 512, (tt + 1) * 512)
                            mm(bank(b0 + tt), [(wt[:, k, :], XB[:, k, ts]) for k in R8],
                               [wk] + XBk(R8, [tt]), [PB(b0 + tt)])
                        H = ps[:, 2048 * half_gv:2048 * (half_gv + 1)]
                        w0 = pc(("fcw", l), 1, ch * 3 + 0)
                        w1 = pc(("fcw", l), 1, ch * 3 + 1)
                        w2 = pc(("fcw", l), 1, ch * 3 + 2)
                        bb = pc(("fcb", l), 1, ch)
                        bufs = [_carve(scr, (8192 if half_gv else 0) + 4096 * hf, F32, [1024]) for hf in range(2)]
                        bks = [("ffb", half_gv, hf) for hf in range(2)]
                        for hf in range(2):
                            lo = 1024 * hf
                            A(lambda e, buf=bufs[hf], lo=lo, H=H, w1=w1, bb=bb: e.activation(
                                buf, H[:, lo:lo + 1024], AF.Identity, bias=bb, scale=w1),
                              [PB(b0 + 2 * hf), PB(b0 + 2 * hf + 1), "par", SC], [bks[hf]])
                        hb01 = [PB(b0), PB(b0 + 1)]
                        for hf in range(2):
                            lo = 1024 * hf
                            buf = bufs[hf]
                            bk = bks[hf]
                            if hf == 0:
                                V(lambda e, buf=buf, H=H, w0=w0: e.scalar_tensor_tensor(
                                    buf[:, 1:1024], H[:, 0:1023], w0, buf[:, 1:1024], ALU.mult, ALU.add),
                                  hb01 + [bks[0], "par"], [bk])
                                V(lambda e, buf=buf, H=H, w2=w2: e.scalar_tensor_tensor(
                                    buf[:, 0:1023], H[:, 1:1024], w2, buf[:, 0:1023], ALU.mult, ALU.add),
                                  hb01 + [bk, "par"], [bk])
                                V(lambda e, buf=buf, H=H, w2=w2: e.scalar_tensor_tensor(
                                    buf[:, 1023:1024], H[:, 1024:1025], w2, buf[:, 1023:1024], ALU.mult, ALU.add),
                                  hb + [bk, bks[1], "par"], [bk])
                            else:
                                V(lambda e, buf=buf, H=H, w0=w0: e.scalar_tensor_tensor(
                                    buf[:, 0:1024], H[:, 1023:2047], w0, buf[:, 0:1024], ALU.mult, ALU.add),
                                  hb + [bks[0], bks[1], "par"], [bk])
                                V(lambda e, buf=buf, H=H, w2=w2: e.scalar_tensor_tensor(
                                    buf[:, 0:1023], H[:, 1025:2048], w2, buf[:, 0:1023], ALU.mult, ALU.add),
                                  hb + [bk, "par"], [bk])
                            if half_gv == 0:
                                A(lambda e, buf=buf: e.activation(buf, buf, AF.Silu), [bk], [bk])
                            else:
                                gbuf = _carve(scr, 4096 * hf, F32, [1024])
                                G(lambda e, buf=buf, gbuf=gbuf, a=a, lo=lo: e.tensor_tensor(
                                    Abuf[:, a, lo:lo + 1024], gbuf, buf, ALU.mult),
                                  [bk, ("ffb", 0, hf)], ABk([a], [2 * hf, 2 * hf + 1]))
                nk = c1 - c0
                bctr = 0
                wts = {}
                for kpass in range(2):
                    ks = list(range(nk - 1)) if kpass == 0 else [nk - 1]
                    for oc in R8:
                        if kpass == 0 or oc >= 4:
                            wt, wk = load_w(wbf["ffn_w_down"][l, c0 * 128:c1 * 128, oc * 128:(oc + 1) * 128]
                                            .rearrange("(k p) j -> p k j", p=128), nk)
                            wts[oc] = (wt, wk)
                        else:
                            wt, wk = load_w(wbf["ffn_w_down"][l, (c1 - 1) * 128:c1 * 128, oc * 128:(oc + 1) * 128]
                                            .rearrange("(k p) j -> p k j", p=128), 1)
                            wt = wt
                            wts[oc] = (None, None)
                        for tt in R4:
                            ts = slice(tt * 512, (tt + 1) * 512)
                            b = bctr % 8
                            bctr += 1
                            if kpass == 1 and oc < 4:
                                pairs = [(wt[:, 0, :], Abuf[:, nk - 1, ts])]
                            elif kpass == 1:
                                pairs = [(wt[:, nk - 1, :], Abuf[:, nk - 1, ts])]
                            else:
                                pairs = [(wt[:, k, :], Abuf[:, k, ts]) for k in ks]
                            mm(bank(b), pairs, [wk] + ABk(ks, [tt]), [PB(b)])
                            sc_ = ALPHA if (gi == 0 and kpass == 0) else 1.0
                            V(lambda e, oc=oc, ts=ts, b=b, sc_=sc_: e.scalar_tensor_tensor(
                                XT[:, oc, ts], XT[:, oc, ts], sc_, bank(b), ALU.mult, ALU.add),
                              [PB(b), ("XT", oc, tt)], [("XT", oc, tt)])

        def even_mixer(i):
            win = wbf["w_in_even"]
            wcol = lambda c0: win[i, :, c0:c0 + 128].rearrange("(k p) j -> p k j", p=128)
            fence()
            if "sgu" in SKIP:
                G(lambda e: e.memset(ABT[:, 4:8, :], 0.0), [], ABk(range(4, 8), R4))
                return na_part(i, wcol)
            vln = _carve(scr, 0, BF16, [16, 512])
            U = _carve(scr, 16384, F32, [2048])
            tmpA = [_carve(scr, 24576 + 2048 * j, F32, [512]) for j in range(2)]
            gbc = _carve(scr, 28672, F32, [512])
            bbc = _carve(scr, 30720, F32, [512])
            sguW = _carve(scr, 32768, BF16, [4, 128])
            stt = _carve(scr, 33792, F32, [16])
            sgub = _carve(scr, 34048, F32, [512])
            D(lambda e: e.dma_start(out=gbc, in_=sgu_lng_d[i, :].partition_broadcast(128)), [SC], ["gbc"])
            D(lambda e: e.dma_start(out=bbc, in_=sgu_lnb_d[i, :].partition_broadcast(128)), [SC], ["bbc"])
            D(lambda e: e.dma_start(out=sguW, in_=wbf["sgu_wT"][i].rearrange("g q p -> q g p")), [SC, "dw"], ["sguW"])
            D(lambda e: e.dma_start(out=sgub[0:1, :], in_=sgu_b_d[i:i + 1, :]), [SC], ["sgub"])
            gw = [load_w(wcol(2048 + 128 * j), 8) for j in R4]
            stt2 = _carve(scr, 33792, F32, [64])

            def g_s1(tc):
                b = tc % 4
                tt = tc // 4
                tcs = slice(tc * 128, (tc + 1) * 128)

                def f(e):
                    for j in R4:
                        for k in R8:
                            ins = e.matmul(ps[:, 512 * b + 128 * j:512 * b + 128 * (j + 1)], XB[:, k, tcs],
                                           gw[j][0][:, k, :], start=(k == 0), stop=(k == 7))
                    return ins
                T(f, [g_[1] for g_ in gw] + XBk(R8, [tt]), [PB(b)])
                ta = tmpA[tc % 2]
                tk = ("tmpA", tc % 2)
                st_ = stt2[:, 16 * (tc % 2):16 * (tc % 2) + 16]
                sk = ("stt", tc % 2)
                A(lambda e: e.activation(ta, bank(b), AF.Gelu_apprx_tanh), [PB(b), SC], [tk])
                V(lambda e: e.bn_stats(st_[:, 0:6], ta), [tk, SC], [sk])
                V(lambda e: e.bn_aggr(st_[:, 6:8], st_[:, 0:6]), [sk], [sk])

            def g_s2(tc, which):
                st_ = stt2[:, 16 * (tc % 2):16 * (tc % 2) + 16]
                sk = ("stt", tc % 2)
                if which == 0:
                    A(lambda e: e.activation(st_[:, 8:9], st_[:, 7:8], AF.Ln, bias=float(LN_EPS)), [sk], [sk])
                else:
                    A(lambda e: e.activation(st_[:, 9:10], st_[:, 8:9], AF.Exp, scale=-0.5), [sk], [sk])
                    V(lambda e: e.scalar_tensor_tensor(st_[:, 10:11], st_[:, 6:7], -1.0, st_[:, 9:10], ALU.mult, ALU.mult),
                      [sk], [sk])

            def g_s3(tc):
                ta = tmpA[tc % 2]
                tk = ("tmpA", tc % 2)
                st_ = stt2[:, 16 * (tc % 2):16 * (tc % 2) + 16]
                sk = ("stt", tc % 2)
                A(lambda e: e.activation(ta, ta, AF.Identity, bias=st_[:, 10:11], scale=st_[:, 9:10]), [sk, tk], [tk])
                V(lambda e: e.tensor_tensor(ta, ta, gbc, ALU.mult), [tk, "gbc"], [tk])
                V(lambda e: e.tensor_tensor(vln[:, tc, :], ta, bbc, ALU.add), [tk, "bbc"], [("vln", tc)])

            for tp in range(8):
                t0_, t1_ = 2 * tp, 2 * tp + 1
                g_s1(t0_)
                g_s1(t1_)
                g_s2(t0_, 0)
                g_s2(t1_, 0)
                g_s2(t0_, 1)
                g_s2(t1_, 1)
                g_s3(t0_)
                g_s3(t1_)
            for g in R4:
                uw, uk = load_w(wcol(1536 + 128 * g), 8)
                for tt in R4:
                    ts = slice(tt * 512, (tt + 1) * 512)
                    mm(bank(4 + tt), [(uw[:, k, :], XB[:, k, ts]) for k in R8], [uk] + XBk(R8, [tt]), [PB(4 + tt)])
                    A(lambda e, tt=tt, ts=ts: e.activation(U[:, ts], bank(4 + tt), AF.Gelu_apprx_tanh),
                      [PB(4 + tt), SC], [("U", tt)])
                for tt in R4:
                    ts = slice(tt * 512, (tt + 1) * 512)

                    def f(e, g=g, tt=tt):
                        for j in R4:
                            tc = 4 * tt + j
                            o = ps[:, 512 * tt + 128 * j:512 * tt + 128 * (j + 1)]
                            e.matmul(o, vln[:, tc, g * 128:(g + 1) * 128], sguW[:, g, :], start=True, stop=False)
                            ins = e.matmul(o, onesf[0:1, :], sgub[0:1, g * 128:(g + 1) * 128], start=False, stop=True)
                        return ins
                    T(f, [("vln", 4 * tt + j) for j in R4] + ["sguW", "sgub", "onesf"], [PB(tt)])
                    V(lambda e, g=g, tt=tt, ts=ts: e.tensor_tensor(ABT[:, 4 + g, ts], bank(tt), U[:, ts], ALU.mult),
                      [PB(tt), ("U", tt)], [("AB", 4 + g, tt)])
            return na_part(i, wcol)

        def na_part(i, wcol):
            fence()
            if "na" in SKIP:
                G(lambda e: e.memset(ABT[:, 0:4, :], 0.0), [], ABk(R4, R4))
                return w_out_phase("w_out_even", i)
            QT = _carve(scr, 0, BF16, [2048])
            KT = _carve(scr, 4096, BF16, [2048])
            Vaug = _carve(scr, 8192, BF16, [16, 256])
            EBs = [_carve(scr, 16384 + 3840 * j, F32, [15, 64]) for j in range(2)]
            etm = [_carve(scr, 24064 + 2048 * j, F32, [512]) for j in range(2)]
            NPT = 5
            SKEW = int(os.environ.get("KSKEW", "2"))
            pend = []
            PTs = [_carve(scr, 28160 + 1024 * j, BF16, [512]) for j in range(NPT)]
            rcs = [_carve(scr, 33280 + 2048 * j, F32, [512]) for j in range(2)]
            if "nomemset" not in SKIP:
                G(lambda e: e.memset(Vaug[:, :, 64:192], 1.0), [SC], ["Vones"])
            r0 = lambda r: min(max(r - 4, 0), 24)
            VON = [] if "novon" in SKIP else ["Vones"]
            it = 0
            sbctr = 0
            scale = 64 ** -0.5
            for c in ([1, 0, 3, 2] if "corder" in SKIP else R4):
                qw, qk = load_w(wcol(128 * c), 8)
                kw, kk = load_w(wcol(512 + 128 * c), 8)
                vw, vk = load_w(wcol(1024 + 128 * c), 8)
                for tt in R4:
                    ts = slice(tt * 512, (tt + 1) * 512)
                    mm(bank(2 + tt), [(qw[:, k, :], XB[:, k, ts]) for k in R8], [qk] + XBk(R8, [tt]), [PB(2 + tt)])
                    A(lambda e, tt=tt, ts=ts: e.activation(QT[:, ts], bank(2 + tt), AF.Copy), [PB(2 + tt), SC], [("QT", tt)])
                for tt in R4:
                    ts = slice(tt * 512, (tt + 1) * 512)
                    mm(bank(2 + tt), [(kw[:, k, :], XB[:, k, ts]) for k in R8], [kk] + XBk(R8, [tt]), [PB(2 + tt)])
                    V(lambda e, tt=tt, ts=ts: e.tensor_copy(KT[:, ts], bank(2 + tt)), [PB(2 + tt), SC], [("KT", tt)])
                for t4 in R4:
                    if "nov" in SKIP:
                        break
                    b = 2 + t4

                    def f(e, b=b, t4=t4, vw=vw):
                        for j in R4:
                            tc = 4 * t4 + j
                            for k in R8:
                                ins = e.matmul(ps[:, 512 * b + 128 * j:512 * b + 128 * (j + 1)],
                                               XB[:, k, tc * 128:(tc + 1) * 128], vw[:, k, :], start=(k == 0), stop=(k == 7))
                        return ins
                    T(f, [vk] + XBk(R8, [t4]), [PB(b)])
                    src = bank(b).rearrange("p (a b) -> p a b", a=4)
                    A(lambda e, t4=t4, src=src: e.activation(Vaug[:, 4 * t4:4 * t4 + 4, 0:64], src[:, :, 0:64], AF.Copy),
                      [PB(b), "Vones"], [("Va", t4, 0)])
                    A(lambda e, t4=t4, src=src: e.activation(Vaug[:, 4 * t4:4 * t4 + 4, 192:256], src[:, :, 64:128], AF.Copy),
                      [PB(b), "Vones"], [("Va", t4, 1)])
                if "na1" in SKIP:
                    G(lambda e, c=c: e.memset(ABT[:, c, :], 0.0), [], ABk([c], R4))
                    continue
                for hh in range(2):
                    h = 2 * c + hh
                    p0 = 64 * hh
                    EB = EBs[h % 2]
                    ek = ("EB", h % 2)
                    D(lambda e, EB=EB, h=h: e.dma_start(out=EB, in_=ebtab_d[i, h].rearrange("p (a b) -> p a b", a=15)),
                      ["dw", SC], [ek])
                    for qb in R4:
                        ob = qb % 2
                        OB = bank(ob)
                        valid = lambda r, kr: r0(r) <= kr <= r0(r) + 7
                        brows = list(range(8 * qb, 8 * qb + 8))
                        k_lo, k_hi = r0(8 * qb), r0(8 * qb + 7) + 7
                        pairs_m = list(range(k_lo // 2, k_hi // 2 + 1))
                        for ki, m_ in enumerate(pairs_m):
                            rows = [r for r in brows if valid(r, 2 * m_) or valid(r, 2 * m_ + 1)]
                            ra, rb = rows[0], rows[-1] + 1
                            assert rows == list(range(ra, rb))
                            n = (rb - ra) * 64
                            sbk = 2 + sbctr % 6
                            sbctr += 1
                            SBt = ps[:, 512 * sbk:512 * sbk + n]
                            mm(SBt, [(KT[p0:p0 + 64, m_ * 128:(m_ + 1) * 128], QT[p0:p0 + 64, ra * 64:rb * 64])],
                               [("KT", m_ // 4), ("QT", qb)], [PB(sbk)])
                            et = etm[it % 2]
                            etk = ("etm", it % 2)
                            pt = PTs[it % NPT]
                            ptk = ("PT", it % NPT)
                            it += 1
                            A(lambda e, et=et, SBt=SBt, n=n: e.activation(et[:, 0:n], SBt, AF.Exp, scale=scale),
                              [PB(sbk), SC], [etk])
                            t_hi = 2 * m_ - ra + 7
                            nr = rb - ra
                            assert 0 <= t_hi - nr + 1 and t_hi <= 13, (t_hi, nr)
                            ebv = EB[:, t_hi - nr + 1:t_hi + 1, :][:, ::-1, :]
                            V(lambda e, pt=pt, et=et, n=n, ebv=ebv: e.tensor_tensor(
                                pt[:, 0:n].rearrange("p (a b) -> p a b", b=64),
                                et[:, 0:n].rearrange("p (a b) -> p a b", b=64), ebv, ALU.mult),
                              [etk, ek], [ptk])
                            for r in rows:
                                for half in range(2):
                                    if not valid(r, 2 * m_ + half):
                                        a_ = r - ra
                                        G(lambda e, pt=pt, half=half, a_=a_: e.memset(
                                            pt[64 * half:64 * half + 64, a_ * 64:(a_ + 1) * 64], 0.0), [ptk], [ptk])

                            def st2(ob=ob, ra=ra, rb=rb, qb=qb, m_=m_, hh=hh, pt=pt, n=n, ki=ki, nkr=len(pairs_m), ptk=ptk):
                                T(lambda e: e.matmul(
                                    ps[:, 512 * ob + (ra - 8 * qb) * 64:512 * ob + (rb - 8 * qb) * 64],
                                    Vaug[:, m_, 128 * hh:128 * hh + 128], pt[:, 0:n],
                                    start=(ki == 0), stop=(ki == nkr - 1)),
                                  [ptk, ("Va", m_ // 4, hh), "Vones"], [PB(ob)])
                            pend.append(st2)
                            while len(pend) > SKEW:
                                pend.pop(0)()

                        def st3(ob=ob, OB=OB, p0=p0, c=c, qb=qb):
                            rc = rcs[ob]
                            rk = ("rc", ob)
                            dn = 64 - p0
                            A(lambda e: e.activation(rc[p0:p0 + 64, :], OB[dn:dn + 64, :], AF.Ln), [PB(ob), SC], [rk])
                            A(lambda e: e.activation(rc[p0:p0 + 64, :], rc[p0:p0 + 64, :], AF.Exp, scale=-1.0), [rk], [rk])
                            V(lambda e: e.tensor_tensor(
                                ABT[p0:p0 + 64, c, qb * 512:(qb + 1) * 512], OB[p0:p0 + 64, :], rc[p0:p0 + 64, :], ALU.mult),
                              [PB(ob), rk], [("AB", c, qb)])
                        pend.append(st3)
                while pend:
                    pend.pop(0)()
            w_out_phase("w_out_even", i)

        def odd_mixer(i):
            win = wbf["w_in_odd"]
            wcol = lambda c0: win[i, :, c0:c0 + 128].rearrange("(k p) j -> p k j", p=128)
            fence()
            cqn = _carve(scr, 0, BF16, [2, 2048])
            ckvn = _carve(scr, 8192, BF16, [2048])
            kr_t = _carve(scr, 12288, BF16, [2048])
            sq3 = _carve(scr, 16384, BF16, [3, 512])
            r1 = _carve(scr, 19456, F32, [512])
            r2 = _carve(scr, 21504, F32, [512])
            sgs = [_carve(scr, 23552 + 2048 * j, F32, [512]) for j in range(2)]
            cos_s = _carve(scr, 27648, F32, [512])
            sin_s = _carve(scr, 29696, F32, [512])
            t1 = _carve(scr, 31744, F32, [512])
            t2 = _carve(scr, 33792, F32, [512])
            Dpad = _carve(ABr, 0, BF16, [4, 2078])
            DPK = [("AB", c, t) for c in range(5) for t in R4]
            wq0, k0 = load_w(wcol(0), 8)
            wq1, k1 = load_w(wcol(128), 8)
            wkv, k2 = load_w(wcol(256), 8)
            for tt in R4:
                ts = slice(tt * 512, (tt + 1) * 512)
                xk = XBk(R8, [tt])
                for j, (wt, wk) in enumerate(((wq0, k0), (wq1, k1), (wkv, k2))):
                    mm(bank(j), [(wt[:, k, :], XB[:, k, ts]) for k in R8], [wk] + xk, [PB(j)])
                A(lambda e: e.activation(sq3, ps[:, 0:1536].rearrange("p (a b) -> p a b", a=3), AF.Square),
                  [PB(0), PB(1), PB(2), SC], ["sq3"])
                mm(bank(3), [(onesb[:], sq3[:, 0, :]), (onesb[:], sq3[:, 1, :])], ["sq3", "onesb"], [PB(3)])
                mm(bank(4), [(onesb[:], sq3[:, 2, :])], ["sq3", "onesb"], [PB(4)])
                A(lambda e: e.activation(r1, bank(3), AF.Ln, bias=float(RMS_EPS), scale=1.0 / 256), [PB(3), SC], ["r1"])
                A(lambda e: e.activation(r1, r1, AF.Exp, scale=-0.5), ["r1"], ["r1"])
                A(lambda e: e.activation(r2, bank(4), AF.Ln, bias=float(RMS_EPS), scale=1.0 / 128), [PB(4), SC], ["r2"])
                A(lambda e: e.activation(r2, r2, AF.Exp, scale=-0.5), ["r2"], ["r2"])
                for j in range(2):
                    V(lambda e, j=j, ts=ts: e.scalar_tensor_tensor(cqn[:, j, ts], bank(j), pc(("qg", i), 1, j), r1,
                                                                   ALU.mult, ALU.mult),
                      [PB(j), "r1", "par"], [("cqn", tt)])
                V(lambda e, ts=ts: e.scalar_tensor_tensor(ckvn[:, ts], bank(2), pc(("kvg", i), 1, 0), r2, ALU.mult, ALU.mult),
                  [PB(2), "r2", "par"], [("ckvn", tt)])
            wA, kA = load_w(wcol(320), 8)
            wB, kB = load_w(wcol(320), 8, extra=[
                (64, 80, win[i, :, 400:416].rearrange("(k p) j -> p k j", p=128)),
                (80, 96, win[i, :, 384:400].rearrange("(k p) j -> p k j", p=128))])
            for tt in R4:
                ts = slice(tt * 512, (tt + 1) * 512)
                xk = XBk(R8, [tt])
                D(lambda e, ts=ts: e.dma_start(out=cos_s[64:96, :], in_=cost_d[64:96, ts]), [SC], ["cos_s"])
                D(lambda e, ts=ts: e.dma_start(out=sin_s[64:96, :], in_=sint_d[64:96, ts]), [SC], ["sin_s"])
                mm(bank(5), [(wA[:, k, :], XB[:, k, ts]) for k in R8], [kA] + xk, [PB(5)])
                mm(bank(6), [(wB[:, k, :], XB[:, k, ts]) for k in R8], [kB] + xk, [PB(6)])
                V(lambda e: e.tensor_tensor(t1[64:96, :], ps[64:96, 512 * 5:512 * 6], cos_s[64:96, :], ALU.mult),
                  [PB(5), "cos_s", SC], ["t1"])
                V(lambda e: e.tensor_tensor(t2[64:96, :], ps[64:96, 512 * 6:512 * 7], sin_s[64:96, :], ALU.mult),
                  [PB(6), "sin_s", SC], ["t2"])
                G(lambda e, ts=ts: e.tensor_tensor(kr_t[64:96, ts], t1[64:96, :], t2[64:96, :], ALU.add),
                  ["t1", "t2"], [("kr", tt)])
            G(lambda e: e.memset(Dpad[:, :, 0:15], 0.0), DPK, DPK)
            G(lambda e: e.memset(Dpad[:, :, 2063:2078], 0.0), DPK, DPK)
            it = 0
            for cc in R4:
                wa, ka = load_w(wcol(416 + 128 * cc), 8)
                wg, kg = load_w(wcol(928 + 128 * cc), 8)
                for tt in R4:
                    ts = slice(tt * 512, (tt + 1) * 512)
                    xk = XBk(R8, [tt])
                    ba, bg = (it % 2) * 2, (it % 2) * 2 + 1
                    sg = sgs[it % 2]
                    sk = ("sg", it % 2)
                    it += 1
                    mm(bank(ba), [(wa[:, k, :], XB[:, k, ts]) for k in R8], [ka] + xk, [PB(ba)])
                    mm(bank(bg), [(wg[:, k, :], XB[:, k, ts]) for k in R8], [kg] + xk, [PB(bg)])
                    A(lambda e, sg=sg, bg=bg: e.activation(sg, bank(bg), AF.Sigmoid), [PB(bg), SC], [sk])
                    V(lambda e, sg=sg, ba=ba, cc=cc, tt=tt: e.tensor_tensor(
                        Dpad[:, cc, 15 + tt * 512:15 + (tt + 1) * 512], bank(ba), sg, ALU.mult),
                      [PB(ba), sk] + DPK, DPK)
            fence()
            diag = _carve(scr, 16384, BF16, [31, 128])
            sq4 = _carve(scr, 24320, BF16, [4, 512])
            m_t = _carve(scr, 28416, F32, [512])
            v_t = _carve(scr, 30464, F32, [512])
            l_t = _carve(scr, 32512, F32, [512])
            c16 = _carve(scr, 34560, BF16, [4, 512])
            Cv = _carve(XBr, 0, F32, [4, 2048])
            CVK = lambda cc, tt: [("XB", 2 * cc, tt), ("XB", 2 * cc + 1, tt)]
            CVA = [k for cc in R4 for tt in R4 for k in CVK(cc, tt)]
            bctr = 0
            diag2 = _carve(ABr, 20480, BF16, [31, 128])
            D2K = [("AB", c_, t_) for c_ in (5, 6) for t_ in R4]
            for cc in R4:
                cwv = pc(("cw", i), 31, cc * 31)
                dg = diag if cc % 2 == 0 else diag2
                dgk = ["diag"] if cc % 2 == 0 else D2K
                G(lambda e, cwv=cwv, dg=dg: e.tensor_tensor(
                    dg, identb[:].unsqueeze(1).broadcast_to([128, 31, 128]),
                    cwv.unsqueeze(2).broadcast_to([128, 31, 128]), ALU.mult),
                  ["identb", "par", SC], dgk)
                for tt in R4:
                    b = bctr % 4
                    bctr += 1
                    mm(bank(b), [(dg[:, j, :], Dpad[:, cc, tt * 512 + j:tt * 512 + j + 512]) for j in range(31)],
                       dgk + DPK, [PB(b)])
                    A(lambda e, cc=cc, tt=tt, b=b: e.activation(Cv[:, cc, tt * 512:(tt + 1) * 512], bank(b), AF.Identity,
                                                                bias=pc(("cb", i), 1, cc)),
                      [PB(b), "par"], CVA if (cc == 0 and tt == 0) else CVK(cc, tt))
            for tt in R4:
                ts = slice(tt * 512, (tt + 1) * 512)
                bA, bB = 4 + 2 * (tt % 2), 5 + 2 * (tt % 2)
                ck = [k for cc in R4 for k in CVK(cc, tt)]
                A(lambda e, ts=ts: e.activation(sq4, Cv[:, :, ts], AF.Square), ck + [SC], ["sq4"])
                G(lambda e, ts=ts: e.tensor_copy(c16, Cv[:, :, ts]), ck + [SC], ["c16"])
                mm(bank(bA), [(onesb[:], c16[:, c, :]) for c in R4], ["c16", "onesb"], [PB(bA)])
                mm(bank(bB), [(onesb[:], sq4[:, c, :]) for c in R4], ["sq4", "onesb"], [PB(bB)])
                A(lambda e, bA=bA: e.activation(m_t, bank(bA), AF.Identity, scale=1.0 / 512), [PB(bA), SC], ["m_t"])
                V(lambda e: e.tensor_tensor(v_t, m_t, m_t, ALU.mult), ["m_t", SC], ["v_t"])
                V(lambda e, bB=bB: e.scalar_tensor_tensor(v_t, bank(bB), 1.0 / 512, v_t, ALU.mult, ALU.subtract),
                  [PB(bB), "v_t"], ["v_t"])
                A(lambda e: e.activation(l_t, v_t, AF.Ln, bias=float(LN_EPS)), ["v_t", SC], ["l_t"])
                A(lambda e, bA=bA: e.activation(bank(bA), l_t, AF.Exp, scale=-0.5), ["l_t", "m_t"], [PB(bA)])
                V(lambda e, bA=bA, bB=bB: e.scalar_tensor_tensor(bank(bB), m_t, -1.0, bank(bA), ALU.mult, ALU.mult),
                  ["m_t", PB(bA), "v_t"], [PB(bB)])
                V(lambda e, ts=ts, bA=bA: e.tensor_tensor(
                    Cv[:, :, ts], Cv[:, :, ts], bank(bA).unsqueeze(1).broadcast_to([128, 4, 512]), ALU.mult),
                  ck + [PB(bA), "sq4", "c16"], ck)
                V(lambda e, ts=ts, bB=bB: e.tensor_tensor(
                    Cv[:, :, ts], Cv[:, :, ts], bank(bB).unsqueeze(1).broadcast_to([128, 4, 512]), ALU.add),
                  ck + [PB(bB)], ck)
                for cc in R4:
                    A(lambda e, cc=cc, ts=ts: e.activation(ABT[:, 4 + cc, ts], Cv[:, cc, ts], AF.Silu,
                                                           bias=pc(("clb", i), 1, cc), scale=pc(("clg", i), 1, cc)),
                      CVK(cc, tt) + ["par"] + (DPK if (cc == 0) else []), [("AB", 4 + cc, tt)])
            fence()
            QT = _carve(scr, 16384, BF16, [2048])
            KT = _carve(scr, 20480, BF16, [2048])
            VA = [_carve(scr, 24576 + 4096 * j, BF16, [16, 128]) for j in range(2)]
            PT2 = [_carve(scr, 32768 + 2048 * j, BF16, [1024]) for j in range(3)]
            pend = []
            cosF = _carve(XBr, 0, F32, [2048])
            sinF = _carve(XBr, 8192, F32, [2048])
            rcs = [_carve(XBr, 16384 + 2048 * j, F32, [512]) for j in range(2)]
            u1 = _carve(XBr, 20480, F32, [512])
            u2 = _carve(XBr, 22528, F32, [512])
            XBall = XBk(R8, R4)
            D(lambda e: e.dma_start(out=cosF[64:96, :], in_=cost_d[64:96, :]), XBall, XBall)
            D(lambda e: e.dma_start(out=sinF[64:96, :], in_=sint_d[64:96, :]), XBall, XBall)
            TBL = [("XB", 0, 0)]
            G(lambda e: e.memset(VA[0][:, :, 64:128], 1.0), [SC], ["VAones0"])
            G(lambda e: e.memset(VA[1][:, :, 0:64], 1.0), [SC], ["VAones1"])
            scale = 96 ** -0.5
            it = 0
            sbctr = 0
            QTs = [QT, _carve(XBr, 24576, BF16, [2048])]
            KTs = [KT, _carve(XBr, 28672, BF16, [2048])]
            wkv_of = {}

            def proj(h):
                QT, KT = QTs[h % 2], KTs[h % 2]
                hb_ = h % 2
                hh = h % 2
                p0 = 64 * hh
                c = h // 2
                uq = wbf["w_uq"]
                wa, ka = load_w(uq[i, :, 96 * h:96 * h + 96].rearrange("(k p) j -> p k j", p=128), 2, ncol=96)
                wb_, kb = load_w(uq[i, :, 96 * h:96 * h + 96].rearrange("(k p) j -> p k j", p=128), 2, ncol=96, extra=[
                    (64, 80, uq[i, :, 96 * h + 80:96 * h + 96].rearrange("(k p) j -> p k j", p=128)),
                    (80, 96, uq[i, :, 96 * h + 64:96 * h + 80].rearrange("(k p) j -> p k j", p=128))])
                wkv_, kkv = load_w(wbf["w_ukv"][i, :, 128 * h:128 * h + 128].rearrange("(k p) j -> p k j", p=128), 1)
                for tt in R4:
                    ts = slice(tt * 512, (tt + 1) * 512)
                    mm(ps[0:96, 512 * 2:512 * 3], [(wa[:, k, :], cqn[:, k, ts]) for k in range(2)], [ka, ("cqn", tt)], [PB(2)])
                    mm(ps[0:96, 512 * 3:512 * 4], [(wb_[:, k, :], cqn[:, k, ts]) for k in range(2)], [kb, ("cqn", tt)], [PB(3)])
                    mm(ps[0:64, 512 * 4:512 * 5], [(wkv_[:, 0, 0:64], ckvn[:, ts])], [kkv, ("ckvn", tt)], [PB(4)])
                    V(lambda e, ts=ts: e.tensor_copy(QT[0:64, ts], ps[0:64, 1024:1536]), [PB(2), SC] + TBL, [("QTm", hb_, tt, 0)])
                    V(lambda e, ts=ts: e.tensor_tensor(u1[64:96, :], ps[64:96, 1024:1536], cosF[64:96, ts], ALU.mult),
                      [PB(2)] + TBL, ["u1"])
                    V(lambda e, ts=ts: e.tensor_tensor(u2[64:96, :], ps[64:96, 1536:2048], sinF[64:96, ts], ALU.mult),
                      [PB(3)] + TBL, ["u2"])
                    G(lambda e, ts=ts: e.tensor_tensor(QT[64:96, ts], u1[64:96, :], u2[64:96, :], ALU.add),
                      ["u1", "u2", SC] + TBL, [("QTm", hb_, tt, 1)])
                    V(lambda e, ts=ts: e.tensor_copy(KT[0:64, ts], ps[0:64, 2048:2560]), [PB(4), SC] + TBL, [("KTm", hb_, tt, 0)])
                    G(lambda e, ts=ts: e.tensor_copy(KT[64:96, ts], kr_t[64:96, ts]), [("kr", tt), SC] + TBL, [("KTm", hb_, tt, 1)])
                va = VA[hh]
                for t8 in range(2):
                    b = 5

                    def f(e, t8=t8, b=b, wkv_=wkv_):
                        for j in range(8):
                            tc = 8 * t8 + j
                            ins = e.matmul(ps[:, 512 * b + 64 * j:512 * b + 64 * (j + 1)], ckvn[:, tc * 128:(tc + 1) * 128],
                                           wkv_[:, 0, 64:128], start=True, stop=True)
                        return ins
                    T(f, [kkv, ("ckvn", 2 * t8), ("ckvn", 2 * t8 + 1)], [PB(b)])
                    src = bank(b).rearrange("p (a b) -> p a b", a=8)
                    V(lambda e, va=va, t8=t8, src=src, hh=hh: e.tensor_copy(
                        va[:, 8 * t8:8 * t8 + 8, 64 * hh:64 * hh + 64], src),
                      [PB(b), SC, "VAones%d" % hh], [("VA", hh, t8)])

            def attn(h):
                nonlocal it, sbctr
                QT, KT = QTs[h % 2], KTs[h % 2]
                hb_ = h % 2
                hh = h % 2
                p0 = 64 * hh
                c = h // 2
                va = VA[hh]
                for tt in R4:
                    ts = slice(tt * 512, (tt + 1) * 512)
                    ob = tt % 2
                    OB = bank(ob)
                    for j2 in range(8):
                        sb0 = 2 + 2 * (sbctr % 3)
                        sbctr += 1
                        for u in range(2):
                            kc = 2 * j2 + u
                            mm(bank(sb0 + u), [(KT[0:96, kc * 128:(kc + 1) * 128], QT[0:96, ts])],
                               [("KTm", hb_, kc // 4, 0), ("KTm", hb_, kc // 4, 1), ("QTm", hb_, tt, 0), ("QTm", hb_, tt, 1)], [PB(sb0 + u)])
                        pt = PT2[it % 3]
                        ptk = ("PT", it % 3)
                        it += 1
                        A(lambda e, pt=pt, sb0=sb0: e.activation(pt, ps[:, 512 * sb0:512 * sb0 + 1024], AF.Exp, scale=scale),
                          [PB(sb0), PB(sb0 + 1), SC], [ptk])

                        def st2(OB=OB, va=va, j2=j2, pt=pt, ptk=ptk, hh=hh, ob=ob):
                            def f(e):
                                for u in range(2):
                                    kc = 2 * j2 + u
                                    ins = e.matmul(OB, va[:, kc, :], pt[:, 512 * u:512 * (u + 1)], start=(kc == 0), stop=(kc == 15))
                                return ins
                            T(f, [ptk, ("VA", hh, j2 // 4), "VAones%d" % hh], [PB(ob)])
                        pend.append(st2)
                        while len(pend) > 2:
                            pend.pop(0)()

                    def st3(ob=ob, OB=OB, p0=p0, c=c, ts=ts, tt=tt):
                        rc = rcs[ob]
                        rk = ("rc", ob)
                        dn = 64 - p0
                        A(lambda e: e.activation(rc[p0:p0 + 64, :], OB[dn:dn + 64, :], AF.Ln), [PB(ob)] + TBL, [rk])
                        A(lambda e: e.activation(rc[p0:p0 + 64, :], rc[p0:p0 + 64, :], AF.Exp, scale=-1.0), [rk], [rk])
                        V(lambda e: e.tensor_tensor(ABT[p0:p0 + 64, c, ts], OB[p0:p0 + 64, :], rc[p0:p0 + 64, :], ALU.mult),
                          [PB(ob), rk], [("AB", c, tt)])
                    pend.append(st3)
                while pend:
                    pend.pop(0)()

            proj(0)
            for h in R8:
                if h + 1 < 8:
                    proj(h + 1)
                attn(h)
            G(lambda e: e.memset(dumt[:, 1:2], 0.0), ["u1", "u2", ("rc", 0), ("rc", 1)] + TBL,
              XBall + [(nm, 1, t_, p_) for nm in ("QTm", "KTm") for t_ in R4 for p_ in range(2)])
            w_out_phase("w_out_odd", i)

        PLAN = os.environ.get("KPLAN", "")
        if PLAN:
            for j, tok in enumerate(PLAN.split(",")):
                S.epoch = 1 + j // 2
                if tok[0] == "L":
                    load_x(int(tok[1]), j == 0)
                else:
                    store_y(int(tok[1]))
            nseq = 0
        for s in range(nseq):
            S.epoch = (s + 1) if 'noepoch' not in SKIP else 0
            load_x(s, s == 0)
            for l in range(nlayers):
                if l % 2 == 0:
                    even_mixer(l // 2)
                else:
                    odd_mixer(l // 2)
                ln_phase("ln1g", "ln1b", l)
                if do_ffn:
                    ffn_phase(l)
                    ln_phase("ln2g", "ln2b", l)
            store_y(s)
        S.emit()
    return nc


_NC_CACHE = {}


def _get_nc(nseq, nlayers=DEPTH, do_ffn=True):
    k = (nseq, nlayers, do_ffn)
    if k not in _NC_CACHE:
        _NC_CACHE[k] = build_nc(nseq, nlayers, do_ffn)
    return _NC_CACHE[k]


def _common_inputs(inp):
    c = _consts()
    d = {n: np.ascontiguousarray(inp[n], dtype=np.float32) for n, _ in WSPECS if n != "sgu_wT"}
    d["sgu_wT"] = np.ascontiguousarray(np.transpose(inp["sgu_w"], (0, 1, 3, 2)), dtype=np.float32)
    rp = np.zeros((2, 8, 15, 127), np.float32)
    rp[..., 48:79] = inp["rpb"]
    d["rpb_pad"] = rp
    d["sgu_ln_g"] = np.ascontiguousarray(inp["sgu_ln_g"], dtype=np.float32)
    d["sgu_ln_b"] = np.ascontiguousarray(inp["sgu_ln_b"], dtype=np.float32)
    d["sgu_b"] = np.ascontiguousarray(np.asarray(inp["sgu_b"], dtype=np.float32).reshape(2, 512))
    d["params"] = _pack_params(inp)
    d.update(c)
    return d


def kernel(**inputs):
    inp = {k: np.asarray(v) for k, v in inputs.items()}
    xs = np.concatenate([inp["x_prompt"], inp["x_sample"]], axis=0).astype(np.float32, copy=False)
    nseq = xs.shape[0] // N_CORES
    nc = _get_nc(nseq)
    common = _common_inputs(inp)
    in_maps = []
    for c in range(N_CORES):
        m = dict(common)
        m["x"] = np.ascontiguousarray(xs[c * nseq:(c + 1) * nseq])
        in_maps.append(m)
    res = run_bass_kernel_spmd(nc, in_maps, core_ids=list(range(N_CORES)))
    ys = np.concatenate([r["y"] for r in res.results], axis=0)
    nb = inp["x_prompt"].shape[0]
    return (np.ascontiguousarray(ys[:nb]), np.ascontiguousarray(ys[nb:]))
```

```python
import contextlib
import numpy as np
import concourse.bass as bass
import concourse.mybir as mybir
from concourse.bass_utils import run_bass_kernel_spmd

F32 = mybir.dt.float32
BF16 = mybir.dt.bfloat16
U8 = mybir.dt.uint8
AF = mybir.ActivationFunctionType
ALU = mybir.AluOpType

S_LEN = 2048
DM = 1024
DEPTH = 4
DFF = 2816
ALPHA = float((2 * DEPTH) ** 0.25)
LN_EPS = 1e-5
RMS_EPS = 1e-6
N_CORES = 8


class _Op:
    __slots__ = ("eng", "fn", "deps", "signal", "count", "is_dma", "lane", "lane_k", "epoch")

    def __init__(self, eng, fn, is_dma):
        self.eng = eng
        self.fn = fn
        self.deps = []
        self.signal = False
        self.count = 0
        self.is_dma = is_dma
        self.lane = 0
        self.lane_k = 0
        self.epoch = 0


class Sched:
    ENG_ATTR = {"pe": "tensor", "act": "scalar", "dve": "vector", "pool": "gpsimd", "sp": "sync"}

    def __init__(self, nc, n_lanes=8, same_eng_sync=True):
        self.nc = nc
        self.ops = {k: [] for k in self.ENG_ATTR}
        self.res_w = {}
        self.res_r = {}
        self.n_lanes = n_lanes
        self.dma_cnt = {k: 0 for k in self.ENG_ATTR}
        self.same_eng_sync = same_eng_sync
        self.epoch = 0

    def _record(self, o, reads, writes):
        o.epoch = self.epoch
        deps = {}
        res_w, res_r = self.res_w, self.res_r
        for r in reads:
            w = res_w.get(r)
            if w is not None:
                deps[id(w)] = w
        for r in writes:
            w = res_w.get(r)
            if w is not None:
                deps[id(w)] = w
            rr = res_r.get(r)
            if rr:
                for x in rr.values():
                    deps[id(x)] = x
        eng = o.eng
        dl = []
        for d in deps.values():
            if d is o:
                continue
            if d.eng == eng and not d.is_dma and not o.is_dma:
                if eng == "pe" or not self.same_eng_sync:
                    continue
            dl.append(d)
        o.deps = dl
        for r in writes:
            res_w[r] = o
            res_r[r] = {}
        for r in reads:
            rr = res_r.get(r)
            if rr is None:
                rr = res_r[r] = {}
            rr[eng if not o.is_dma else (eng, o.lane)] = o
        self.ops[eng].append(o)
        return o

    def op(self, eng, fn, reads=(), writes=()):
        return self._record(_Op(eng, fn, False), reads, writes)

    def dma(self, eng, fn, reads=(), writes=()):
        o = _Op(eng, fn, True)
        i = self.dma_cnt[eng]
        self.dma_cnt[eng] = i + 1
        o.lane = i % self.n_lanes
        o.lane_k = i // self.n_lanes + 1
        return self._record(o, reads, writes)

    def emit(self):
        import os
        TRACE = bool(os.environ.get("KTRACE"))
        nc = self.nc
        for eng, lst in self.ops.items():
            for o in lst:
                for d in o.deps:
                    d.signal = True
        with contextlib.ExitStack() as st:
            sems = {}
            for eng in self.ops:
                for ep in sorted(set(o.epoch for o in self.ops[eng])):
                    sems[(eng, ep)] = st.enter_context(nc.semaphore("s_%s_%d" % (eng, ep)))
            lanes = {}
            for eng in self.ops:
                if self.dma_cnt[eng] > 0:
                    lanes[eng] = [st.enter_context(nc.semaphore("l_%s%d" % (eng, i)))
                                  for i in range(min(self.n_lanes, self.dma_cnt[eng]))]
            for eng, lst in self.ops.items():
                cc = {}
                for o in lst:
                    if not o.is_dma and o.signal:
                        c = cc.get(o.epoch, 0) + 1
                        cc[o.epoch] = c
                        o.count = c
            block = st.enter_context(nc.Block())

            def siginfo(d):
                if d.is_dma:
                    return lanes[d.eng][d.lane], 16 * d.lane_k
                return sems[(d.eng, d.epoch)], d.count

            def body_for(eng):
                lst = self.ops[eng]

                def body(e):
                    waited = {}
                    last_dma = {}
                    for oi, o in enumerate(lst):
                        for d in o.deps:
                            s, v = siginfo(d)
                            k = id(s)
                            if waited.get(k, 0) < v:
                                e.wait_ge(s, v)
                                waited[k] = v
                                if TRACE:
                                    print("  %s op%d waits %s(ep%d lane%d) >= %d" % (eng, oi, d.eng, d.epoch, d.lane if d.is_dma else -1, v))
                        if TRACE:
                            print("%s op%d %s signal=%s count=%d ep=%d lane=%d k=%d" % (eng, oi, "DMA" if o.is_dma else "OP", o.signal, o.count, o.epoch, o.lane, o.lane_k))
                        if o.is_dma:
                            s = lanes[eng][o.lane]
                            if o.lane_k > 1:
                                k = id(s)
                                v = 16 * (o.lane_k - 1)
                                if waited.get(k, 0) < v:
                                    e.wait_ge(s, v)
                                    waited[k] = v
                            ins = o.fn(e)
                            ins.then_inc(s, 16)
                            last_dma[o.lane] = o
                        else:
                            ins = o.fn(e)
                            if o.signal:
                                ins.then_inc(sems[(eng, o.epoch)], 1)
                    for lane, o in last_dma.items():
                        s = lanes[eng][lane]
                        v = 16 * o.lane_k
                        if waited.get(id(s), 0) < v:
                            e.wait_ge(s, v)
                return body

            for eng, attr in self.ENG_ATTR.items():
                if self.ops[eng]:
                    getattr(block, attr)(body_for(eng))


def _carve(t, off, dt, dims):
    n = int(np.prod(dims))
    sz = 2 if dt == BF16 else 4
    ap = t[:, off:off + n * sz].bitcast(dt)
    if len(dims) == 2:
        ap = ap.rearrange("p (a b) -> p a b", a=dims[0])
    elif len(dims) == 3:
        ap = ap.rearrange("p (a b c) -> p a b c", a=dims[0], b=dims[1])
    return ap


def _param_layout():
    off = {}
    n = 0
    for l in range(DEPTH):
        for nm, w in (("ln1g", 8), ("ln1b", 8), ("ln2g", 8), ("ln2b", 8), ("fcw", 132), ("fcb", 44)):
            off[(nm, l)] = n
            n += w
    for i in range(2):
        for nm, w in (("qg", 2), ("kvg", 1), ("cw", 124), ("cb", 4), ("clg", 4), ("clb", 4)):
            off[(nm, i)] = n
            n += w
    return off, n


POFF, NPAR = _param_layout()


def _pack_params(inp):
    P = np.zeros((128, NPAR), np.float32)

    def put(key, arr):
        a = np.ascontiguousarray(arr, dtype=np.float32).reshape(128, -1)
        P[:, POFF[key]:POFF[key] + a.shape[1]] = a

    for l in range(DEPTH):
        put(("ln1g", l), inp["ln1_g"][l].reshape(8, 128).T)
        put(("ln1b", l), inp["ln1_b"][l].reshape(8, 128).T)
        put(("ln2g", l), inp["ln2_g"][l].reshape(8, 128).T)
        put(("ln2b", l), inp["ln2_b"][l].reshape(8, 128).T)
        put(("fcw", l), inp["ffn_conv_w"][l].reshape(3, 44, 128).transpose(2, 1, 0))
        put(("fcb", l), inp["ffn_conv_b"][l].reshape(44, 128).T)
    for i in range(2):
        put(("qg", i), inp["q_norm_g"][i].reshape(2, 128).T)
        put(("kvg", i), inp["kv_norm_g"][i].reshape(1, 128).T)
        put(("cw", i), inp["conv_w"][i].reshape(31, 4, 128).transpose(2, 1, 0))
        put(("cb", i), inp["conv_b"][i].reshape(4, 128).T)
        put(("clg", i), inp["conv_ln_g"][i].reshape(4, 128).T)
        put(("clb", i), inp["conv_ln_b"][i].reshape(4, 128).T)
    return P


def _consts():
    c = {}
    c["ident"] = np.eye(128, dtype=np.float32)
    cq = np.arange(64)
    c0 = np.clip(cq - 8, 0, 48)
    ck = np.arange(64)
    m = ((ck[:, None] >= c0[None, :]) & (ck[:, None] < c0[None, :] + 16)).astype(np.float32)
    c["namask"] = np.concatenate([m, m], axis=0)
    pos = np.arange(S_LEN, dtype=np.float32)
    inv = (10000.0 ** (-np.arange(0, 32, 2, dtype=np.float32) / 32)).astype(np.float32)
    ang = (pos[:, None] * inv[None, :]).astype(np.float32)
    cos = np.cos(ang).astype(np.float32).T
    sin = np.sin(ang).astype(np.float32).T
    ct = np.zeros((128, S_LEN), np.float32)
    sn = np.zeros((128, S_LEN), np.float32)
    ct[64:80] = cos
    ct[80:96] = cos
    sn[64:80] = -sin
    sn[80:96] = sin
    c["cost"] = ct
    c["sint"] = sn
    return c


WSPECS = [
    ("w_in_even", [2, 1024, 2560]), ("w_out_even", [2, 1024, 1024]), ("w_in_odd", [2, 1024, 1440]),
    ("w_uq", [2, 256, 768]), ("w_ukv", [2, 128, 1024]), ("w_out_odd", [2, 1024, 1024]),
    ("ffn_w_up", [4, 1024, 5632]), ("ffn_w_down", [4, 2816, 1024]), ("sgu_wT", [2, 4, 128, 128]),
]


def build_nc(nseq, nlayers=DEPTH, do_ffn=True):
    nc = bass.Bass("TRN2", target_bir_lowering=False)
    din = lambda n, s, d=F32: nc.dram_tensor(n, list(s), d, kind="ExternalInput").ap()
    x_d = din("x", [nseq, S_LEN, DM])
    y_d = nc.dram_tensor("y", [nseq, S_LEN, DM], F32, kind="ExternalOutput").ap()
    wsrc = {n: din(n, s) for n, s in WSPECS}
    wbf = {n: nc.dram_tensor(n + "_bf", list(s), BF16, kind="Internal").ap() for n, s in WSPECS}
    rpbpad_d = din("rpb_pad", [2, 8, 15, 127])
    sgu_lng_d = din("sgu_ln_g", [2, 512])
    sgu_lnb_d = din("sgu_ln_b", [2, 512])
    sgu_b_d = din("sgu_b", [2, 512])
    params_d = din("params", [128, NPAR])
    ident_d = din("ident", [128, 128])
    namask_d = din("namask", [128, 64])
    cost_d = din("cost", [128, S_LEN])
    sint_d = din("sint", [128, S_LEN])
    ebtab_d = nc.dram_tensor("ebtab", [2, 8, 128, 960], F32, kind="Internal").ap()
    dummy_d = nc.dram_tensor("dummy_scr", [1, 64], F32, kind="Internal").ap()

    import os
    DBG = os.environ.get("KDBG", "")
    if DBG:
        dbg_d = nc.dram_tensor("dbg", [128, 16384], BF16, kind="ExternalOutput").ap()
    S = Sched(nc, n_lanes=int(os.environ.get('KLANES', '8')))
    NSLOT = 6
    SLOTW = 1408
    SCRB = 40960

    with contextlib.ExitStack() as st:
        sb = lambda n, s, d: st.enter_context(nc.sbuf_tensor(n, s, d))
        XT = sb("XT", [128, 8, S_LEN], F32)
        XBr = sb("XBr", [128, 32768], U8)
        ABr = sb("ABr", [128, 32768], U8)
        wsl = sb("wsl", [128, NSLOT, SLOTW], BF16)
        scr = sb("scr", [128, SCRB], U8)
        par = sb("par", [128, NPAR], F32)
        identf = sb("identf", [128, 128], F32)
        identb = sb("identb", [128, 128], BF16)
        onesb = sb("onesb", [128, 128], BF16)
        onesf = sb("onesf", [1, 128], F32)
        dumt = sb("dumt", [128, 2], F32)
        ps = st.enter_context(nc.psum_tensor("ps", [128, 4096], F32))

        XB = _carve(XBr, 0, BF16, [8, S_LEN])
        ABT = _carve(ABr, 0, BF16, [8, S_LEN])
        bank = lambda b: ps[:, 512 * b:512 * (b + 1)]
        PB = lambda b: ("ps", b)
        SC = "SCR"

        def pc(key, w=1, j=0):
            o = POFF[key] + j
            return par[:, o:o + w]

        A = lambda fn, r=(), w=(): S.op("act", fn, list(r) + [SC], w)
        V = lambda fn, r=(), w=(): S.op("dve", fn, list(r) + [SC], w)
        G = lambda fn, r=(), w=(): S.op("pool", fn, list(r) + [SC], w)
        T = lambda fn, r=(), w=(): S.op("pe", fn, list(r) + [SC], w)
        D = lambda fn, r=(), w=(), nosc=False: S.dma("sp", fn, list(r) + ([] if nosc else [SC]), w)

        def fence():
            S.op("pool", lambda e: e.memset(dumt[:, 0:1], 0.0), [], [SC])

        def mm(out, pairs, r, w):
            def f(e):
                n = len(pairs)
                for i, (l, rh) in enumerate(pairs):
                    ins = e.matmul(out, l, rh, start=(i == 0), stop=(i == n - 1))
                return ins
            T(f, r, w)

        wctr = [0]

        def load_w(src_ap, nk, ncol=128, extra=None):
            slot = wctr[0] % NSLOT
            wctr[0] += 1
            dst = wsl[:, slot, 0:nk * ncol].rearrange("p (k j) -> p k j", j=ncol)
            key = ("w", slot)
            D(lambda e: e.dma_start(out=dst, in_=src_ap), ["dw"], [key], nosc=True)
            if extra:
                for (c0, c1, sap) in extra:
                    D(lambda e, c0=c0, c1=c1, sap=sap: e.dma_start(out=dst[:, :, c0:c1], in_=sap), ["dw"], [key], nosc=True)
            return dst, key

        D(lambda e: e.dma_start(out=par[:], in_=params_d), [], ["par"])
        D(lambda e: e.dma_start(out=identf[:], in_=ident_d), [], ["identf"])
        V(lambda e: e.tensor_copy(identb[:], identf[:]), ["identf"], ["identb"])
        G(lambda e: e.memset(onesb[:], 1.0), [], ["onesb"])
        G(lambda e: e.memset(onesf[:], 1.0), [], ["onesf"])

        st32 = [XT[:, 2 * i:2 * i + 2, :].rearrange("p a b -> p (a b)") for i in range(4)]
        st16 = [_carve(XBr, 8192 * i, BF16, [4096]) for i in range(4)]
        pst = []
        ci = 0
        import os
        SKIP = os.environ.get("KSKIP", "").split(",")
        for name, shp in WSPECS:
            if "conv" in SKIP:
                break
            n = int(np.prod(shp))
            per = n // 128
            letters = " ".join("abcd"[:len(shp)])
            sv = wsrc[name].rearrange("%s -> (%s)" % (letters, letters)).rearrange("(p f) -> p f", p=128)
            dv = wbf[name].rearrange("%s -> (%s)" % (letters, letters)).rearrange("(p f) -> p f", p=128)
            for c0 in range(0, per, 4096):
                w = min(4096, per - c0)
                sl = ci % 4
                D(lambda e, sl=sl, w=w, c0=c0, sv=sv: e.dma_start(out=st32[sl][:, 0:w], in_=sv[:, c0:c0 + w]),
                  [], [("st32", sl)])
                eng = ("act", "dve")[ci % 2]
                if eng == "act":
                    A(lambda e, sl=sl, w=w: e.activation(st16[sl][:, 0:w], st32[sl][:, 0:w], AF.Copy),
                      [("st32", sl)], [("st16", sl)])
                else:
                    S.op(eng, lambda e, sl=sl, w=w: e.tensor_copy(st16[sl][:, 0:w], st32[sl][:, 0:w]),
                         [("st32", sl)], [("st16", sl)])
                k = ("pst", ci)
                D(lambda e, sl=sl, w=w, c0=c0, dv=dv: e.dma_start(out=dv[:, c0:c0 + w], in_=st16[sl][:, 0:w]),
                  [("st16", sl)], [k])
                pst.append(k)
                ci += 1
        hk = _carve(scr, 0, F32, [15, 64])
        eb = _carve(scr, 4096, F32, [15, 64])
        mk = _carve(scr, 8192, F32, [64])
        D(lambda e: e.dma_start(out=mk, in_=namask_d), [], ["mk"])
        for i in range(2):
            if "eb" in SKIP:
                break
            for h in range(8):
                base = ((i * 8 + h) * 15) * 127
                src = bass.AP(tensor=rpbpad_d.tensor, offset=base, ap=[[1, 64], [127, 15], [1, 64]])
                src2 = bass.AP(tensor=rpbpad_d.tensor, offset=base + 127, ap=[[1, 64], [127, 14], [1, 64]])
                D(lambda e, src=src: e.dma_start(out=hk[0:64], in_=src), [], ["hk0"])
                G(lambda e: e.memset(hk[64:128, 14:15, :], 0.0), [], ["hk1"])
                D(lambda e, src2=src2: e.dma_start(out=hk[64:128, 0:14, :], in_=src2), [], ["hk1"])
                A(lambda e: e.activation(hk, hk, AF.Exp), ["hk0", "hk1"], ["hk0", "hk1"])
                V(lambda e: e.tensor_tensor(eb, hk[:, :, ::-1], mk.unsqueeze(1).broadcast_to([128, 15, 64]), ALU.mult),
                  ["hk0", "hk1", "mk"], ["eb"])
                k = ("pst", ci)
                ci += 1
                D(lambda e, i=i, h=h: e.dma_start(out=ebtab_d[i, h].rearrange("p (a b) -> p a b", a=15), in_=eb),
                  ["eb"], [k])
                pst.append(k)
        D(lambda e: e.dma_start(out=dummy_d, in_=ident_d[0:1, 0:64]), pst, ["dw"])
        stage_keys = [("st32", i) for i in range(4)] + [("st16", i) for i in range(4)]

        XTk = lambda cs, tts: [("XT", c, t) for c in cs for t in tts]
        XBk = lambda cs, tts: [("XB", c, t) for c in cs for t in tts]
        ABk = lambda cs, tts: [("AB", c, t) for c in cs for t in tts]
        R8 = range(8)
        R4 = range(4)

        def load_x(s, first):
            fence()
            for tc in range(16):
                sl = tc % 2
                xtok = _carve(scr, 4096 * sl, F32, [1024])
                tt = tc // 4
                D(lambda e, xtok=xtok, tc=tc: e.dma_start(out=xtok, in_=x_d[s, tc * 128:(tc + 1) * 128, :]),
                  [], [("tok", sl, 0), ("tok", sl, 1)] + (["hk0", "hk1", "eb", "mk"] if first else []))
                for half in range(2):
                    b = (tc * 2 + half) % 8

                    def f(e, xtok=xtok, half=half, b=b):
                        for j in range(4):
                            ins = e.transpose(ps[:, 512 * b + 128 * j:512 * b + 128 * (j + 1)],
                                              xtok[:, (4 * half + j) * 128:(4 * half + j + 1) * 128], identf[:])
                        return ins
                    T(f, [("tok", sl, 0), ("tok", sl, 1), "identf"], [PB(b)])
                    cs = range(4 * half, 4 * half + 4)
                    src = bank(b).rearrange("p (a b) -> p a b", a=4)
                    extra = stage_keys if first else []
                    A(lambda e, half=half, tc=tc, src=src: e.activation(
                        XT[:, 4 * half:4 * half + 4, tc * 128:(tc + 1) * 128], src, AF.Copy),
                      [PB(b)], XTk(cs, [tt]) + extra)
                    V(lambda e, half=half, tc=tc: e.tensor_copy(
                        XB[:, 4 * half:4 * half + 4, tc * 128:(tc + 1) * 128],
                        XT[:, 4 * half:4 * half + 4, tc * 128:(tc + 1) * 128]),
                      XTk(cs, [tt]), XBk(cs, [tt]) + extra)

        def store_y(s):
            fence()
            for tc in range(16):
                sl = tc % 2
                ytok = _carve(scr, 4096 * sl, F32, [1024])
                tt = tc // 4
                for half in range(2):
                    b = (tc * 2 + half) % 8

                    def f(e, half=half, b=b, tc=tc):
                        for j in range(4):
                            ins = e.transpose(ps[:, 512 * b + 128 * j:512 * b + 128 * (j + 1)],
                                              XT[:, 4 * half + j, tc * 128:(tc + 1) * 128], identf[:])
                        return ins
                    T(f, XTk(range(4 * half, 4 * half + 4), [tt]) + ["identf"], [PB(b)])
                    if half == 0:
                        A(lambda e, ytok=ytok, b=b: e.activation(ytok[:, 0:512], bank(b), AF.Copy),
                          [PB(b), SC], [("tok", sl, 0)])
                    else:
                        V(lambda e, ytok=ytok, b=b: e.tensor_copy(ytok[:, 512:1024], bank(b)),
                          [PB(b), SC], [("tok", sl, 1)])
                D(lambda e, ytok=ytok, tc=tc: e.dma_start(out=y_d[s, tc * 128:(tc + 1) * 128, :], in_=ytok),
                  [("tok", sl, 0), ("tok", sl, 1)], [("tok", sl, 0), ("tok", sl, 1)])

        def ln_stats(tt, src_fn, nch, scale_n, eps, bA, bB, sq, x16, m_t, v_t, l_t, rkeys, x16keys):
            pass

        def ln_phase(gkey, bkey, l):
            fence()
            if "ln" in SKIP:
                return
            sqs = [_carve(scr, 8192 * j, BF16, [8, 512]) for j in range(2)]
            m_ts = [_carve(scr, 16384 + 2048 * j, F32, [512]) for j in range(2)]
            v_ts = [_carve(scr, 20480 + 2048 * j, F32, [512]) for j in range(2)]
            l_ts = [_carve(scr, 24576 + 2048 * j, F32, [512]) for j in range(2)]

            def s1(tt):
                ts = slice(tt * 512, (tt + 1) * 512)
                bA, bB = 2 * tt, 2 * tt + 1
                sq = sqs[tt % 2]
                sqk = ("sq", tt % 2)
                xk = XTk(R8, [tt])
                A(lambda e: e.activation(sq, XT[:, :, ts], AF.Square), xk, [sqk])
                V(lambda e: e.tensor_copy(XB[:, :, ts], XT[:, :, ts]), xk, XBk(R8, [tt]))
                mm(bank(bA), [(onesb[:], XB[:, c, ts]) for c in R8], XBk(R8, [tt]) + ["onesb"], [PB(bA)])
                mm(bank(bB), [(onesb[:], sq[:, c, :]) for c in R8], [sqk, "onesb"], [PB(bB)])

            def s2(tt):
                bA, bB = 2 * tt, 2 * tt + 1
                m_t, v_t, l_t = m_ts[tt % 2], v_ts[tt % 2], l_ts[tt % 2]
                mk_, vk_, lk_ = ("m_t", tt % 2), ("v_t", tt % 2), ("l_t", tt % 2)
                A(lambda e: e.activation(m_t, bank(bA), AF.Identity, scale=1.0 / DM), [PB(bA)], [mk_])
                A(lambda e: e.activation(v_t, bank(bA), AF.Square, scale=1.0 / DM), [PB(bA)], [vk_])
                V(lambda e: e.scalar_tensor_tensor(v_t, bank(bB), 1.0 / DM, v_t, ALU.mult, ALU.subtract),
                  [PB(bB), vk_], [vk_])
                A(lambda e: e.activation(l_t, v_t, AF.Ln, bias=float(LN_EPS)), [vk_], [lk_])
                A(lambda e: e.activation(bank(bA), l_t, AF.Exp, scale=-0.5), [lk_, mk_], [PB(bA)])
                V(lambda e: e.scalar_tensor_tensor(bank(bB), m_t, -1.0, bank(bA), ALU.mult, ALU.mult),
                  [mk_, PB(bA), vk_], [PB(bB)])

            def s3(tt):
                ts = slice(tt * 512, (tt + 1) * 512)
                bA, bB = 2 * tt, 2 * tt + 1
                xk = XTk(R8, [tt])
                V(lambda e: e.tensor_tensor(
                    XT[:, :, ts], XT[:, :, ts], bank(bA).unsqueeze(1).broadcast_to([128, 8, 512]), ALU.mult),
                  xk + [PB(bA)], xk)
                V(lambda e: e.tensor_tensor(
                    XT[:, :, ts], XT[:, :, ts], bank(bB).unsqueeze(1).broadcast_to([128, 8, 512]), ALU.add),
                  xk + [PB(bB)], xk)

            def s4(tt):
                ts = slice(tt * 512, (tt + 1) * 512)
                for c in R8:
                    gs = pc((gkey, l), 1, c)
                    bs = pc((bkey, l), 1, c)
                    A(lambda e, c=c, gs=gs, bs=bs: e.activation(XT[:, c, ts], XT[:, c, ts], AF.Identity, bias=bs, scale=gs),
                      [("XT", c, tt), "par"], [("XT", c, tt)])
                V(lambda e: e.tensor_copy(XB[:, :, ts], XT[:, :, ts]), XTk(R8, [tt]), XBk(R8, [tt]))

            for st_fn, tt_ in ((s1, 0), (s1, 1), (s2, 0), (s1, 2), (s2, 1), (s3, 0), (s1, 3), (s2, 2), (s3, 1), (s4, 0),
                               (s2, 3), (s3, 2), (s4, 1), (s3, 3), (s4, 2), (s4, 3)):
                st_fn(tt_)

        def w_out_phase(wname, i):
            if DBG:
                D(lambda e: e.dma_start(out=dbg_d, in_=_carve(ABr, 0, BF16, [16384])), ABk(R8, R4), [])
            if "wout" in SKIP:
                return
            bctr = 0
            for oc in R8:
                wt, wk = load_w(wbf[wname][i, :, oc * 128:(oc + 1) * 128].rearrange("(k p) j -> p k j", p=128), 8)
                for tt in R4:
                    ts = slice(tt * 512, (tt + 1) * 512)
                    b = bctr % 8
                    bctr += 1
                    mm(bank(b), [(wt[:, k, :], ABT[:, k, ts]) for k in R8], [wk] + ABk(R8, [tt]), [PB(b)])
                    V(lambda e, oc=oc, ts=ts, b=b: e.scalar_tensor_tensor(
                        XT[:, oc, ts], XT[:, oc, ts], ALPHA, bank(b), ALU.mult, ALU.add),
                      [PB(b), ("XT", oc, tt)], [("XT", oc, tt)])

        def ffn_phase(l):
            fence()
            if "ffn" in SKIP:
                return
            groups = [(0, 8), (8, 15), (15, 22)]
            Abuf = ABT
            for gi, (c0, c1) in enumerate(groups):
                for ci_ in range(c0, c1):
                    a = ci_ - c0
                    for half_gv in range(2):
                        col0 = ci_ * 128 + (DFF if half_gv else 0)
                        ch = ci_ + (22 if half_gv else 0)
                        wt, wk = load_w(wbf["ffn_w_up"][l, :, col0:col0 + 128].rearrange("(k p) j -> p k j", p=128), 8)
                        b0 = 4 * half_gv
                        hb = [PB(b0 + j) for j in R4]
                        for tt in R4:
                            ts = slice(tt * 512, (tt + 1) * 512)
                            mm(bank(b0 + tt), [(wt[:, k, :], XB[:, k, ts]) for k in R8],
                               [wk] + XBk(R8, [tt]), [PB(b0 + tt)])
                        H = ps[:, 2048 * half_gv:2048 * (half_gv + 1)]
                        w0 = pc(("fcw", l), 1, ch * 3 + 0)
                        w1 = pc(("fcw", l), 1, ch * 3 + 1)
                        w2 = pc(("fcw", l), 1, ch * 3 + 2)
                        bb = pc(("fcb", l), 1, ch)
                        bufs = [_carve(scr, (8192 if half_gv else 0) + 4096 * hf, F32, [1024]) for hf in range(2)]
                        bks = [("ffb", half_gv, hf) for hf in range(2)]
                        for hf in range(2):
                            lo = 1024 * hf
                            A(lambda e, buf=bufs[hf], lo=lo, H=H, w1=w1, bb=bb: e.activation(
                                buf, H[:, lo:lo + 1024], AF.Identity, bias=bb, scale=w1),
                              [PB(b0 + 2 * hf), PB(b0 + 2 * hf + 1), "par", SC], [bks[hf]])
                        hb01 = [PB(b0), PB(b0 + 1)]
                        for hf in range(2):
                            lo = 1024 * hf
                            buf = bufs[hf]
                            bk = bks[hf]
                            if hf == 0:
                                V(lambda e, buf=buf, H=H, w0=w0: e.scalar_tensor_tensor(
                                    buf[:, 1:1024], H[:, 0:1023], w0, buf[:, 1:1024], ALU.mult, ALU.add),
                                  hb01 + [bks[0], "par"], [bk])
                                V(lambda e, buf=buf, H=H, w2=w2: e.scalar_tensor_tensor(
                                    buf[:, 0:1023], H[:, 1:1024], w2, buf[:, 0:1023], ALU.mult, ALU.add),
                                  hb01 + [bk, "par"], [bk])
                                V(lambda e, buf=buf, H=H, w2=w2: e.scalar_tensor_tensor(
                                    buf[:, 1023:1024], H[:, 1024:1025], w2, buf[:, 1023:1024], ALU.mult, ALU.add),
                                  hb + [bk, bks[1], "par"], [bk])
                            else:
                                V(lambda e, buf=buf, H=H, w0=w0: e.scalar_tensor_tensor(
                                    buf[:, 0:1024], H[:, 1023:2047], w0, buf[:, 0:1024], ALU.mult, ALU.add),
                                  hb + [bks[0], bks[1], "par"], [bk])
                                V(lambda e, buf=buf, H=H, w2=w2: e.scalar_tensor_tensor(
                                    buf[:, 0:1023], H[:, 1025:2048], w2, buf[:, 0:1023], ALU.mult, ALU.add),
                                  hb + [bk, "par"], [bk])
                            if half_gv == 0:
                                A(lambda e, buf=buf: e.activation(buf, buf, AF.Silu), [bk], [bk])
                            else:
                                gbuf = _carve(scr, 4096 * hf, F32, [1024])
                                G(lambda e, buf=buf, gbuf=gbuf, a=a, lo=lo: e.tensor_tensor(
                                    Abuf[:, a, lo:lo + 1024], gbuf, buf, ALU.mult),
                                  [bk, ("ffb", 0, hf)], ABk([a], [2 * hf, 2 * hf + 1]))
                nk = c1 - c0
                bctr = 0
                wts = {}
                for kpass in range(2):
                    ks = list(range(nk - 1)) if kpass == 0 else [nk - 1]
                    for oc in R8:
                        if kpass == 0 or oc >= 4:
                            wt, wk = load_w(wbf["ffn_w_down"][l, c0 * 128:c1 * 128, oc * 128:(oc + 1) * 128]
                                            .rearrange("(k p) j -> p k j", p=128), nk)
                            wts[oc] = (wt, wk)
                        else:
                            wt, wk = load_w(wbf["ffn_w_down"][l, (c1 - 1) * 128:c1 * 128, oc * 128:(oc + 1) * 128]
                                            .rearrange("(k p) j -> p k j", p=128), 1)
                            wt = wt
                            wts[oc] = (None, None)
                        for tt in R4:
                            ts = slice(tt * 512, (tt + 1) * 512)
                            b = bctr % 8
                            bctr += 1
                            if kpass == 1 and oc < 4:
                                pairs = [(wt[:, 0, :], Abuf[:, nk - 1, ts])]
                            elif kpass == 1:
                                pairs = [(wt[:, nk - 1, :], Abuf[:, nk - 1, ts])]
                            else:
                                pairs = [(wt[:, k, :], Abuf[:, k, ts]) for k in ks]
                            mm(bank(b), pairs, [wk] + ABk(ks, [tt]), [PB(b)])
                            sc_ = ALPHA if (gi == 0 and kpass == 0) else 1.0
                            V(lambda e, oc=oc, ts=ts, b=b, sc_=sc_: e.scalar_tensor_tensor(
                                XT[:, oc, ts], XT[:, oc, ts], sc_, bank(b), ALU.mult, ALU.add),
                              [PB(b), ("XT", oc, tt)], [("XT", oc, tt)])

        def even_mixer(i):
            win = wbf["w_in_even"]
            wcol = lambda c0: win[i, :, c0:c0 + 128].rearrange("(k p) j -> p k j", p=128)
            fence()
            if "sgu" in SKIP:
                G(lambda e: e.memset(ABT[:, 4:8, :], 0.0), [], ABk(range(4, 8), R4))
                return na_part(i, wcol)
            vln = _carve(scr, 0, BF16, [16, 512])
            U = _carve(scr, 16384, F32, [2048])
            tmpA = [_carve(scr, 24576 + 2048 * j, F32, [512]) for j in range(2)]
            gbc = _carve(scr, 28672, F32, [512])
            bbc = _carve(scr, 30720, F32, [512])
            sguW = _carve(scr, 32768, BF16, [4, 128])
            stt = _carve(scr, 33792, F32, [16])
            sgub = _carve(scr, 34048, F32, [512])
            D(lambda e: e.dma_start(out=gbc, in_=sgu_lng_d[i, :].partition_broadcast(128)), [SC], ["gbc"])
            D(lambda e: e.dma_start(out=bbc, in_=sgu_lnb_d[i, :].partition_broadcast(128)), [SC], ["bbc"])
            D(lambda e: e.dma_start(out=sguW, in_=wbf["sgu_wT"][i].rearrange("g q p -> q g p")), [SC, "dw"], ["sguW"])
            D(lambda e: e.dma_start(out=sgub[0:1, :], in_=sgu_b_d[i:i + 1, :]), [SC], ["sgub"])
            gw = [load_w(wcol(2048 + 128 * j), 8) for j in R4]
            stt2 = _carve(scr, 33792, F32, [64])

            def g_s1(tc):
                b = tc % 4
                tt = tc // 4
                tcs = slice(tc * 128, (tc + 1) * 128)

                def f(e):
                    for j in R4:
                        for k in R8:
                            ins = e.matmul(ps[:, 512 * b + 128 * j:512 * b + 128 * (j + 1)], XB[:, k, tcs],
                                           gw[j][0][:, k, :], start=(k == 0), stop=(k == 7))
                    return ins
                T(f, [g_[1] for g_ in gw] + XBk(R8, [tt]), [PB(b)])
                ta = tmpA[tc % 2]
                tk = ("tmpA", tc % 2)
                st_ = stt2[:, 16 * (tc % 2):16 * (tc % 2) + 16]
                sk = ("stt", tc % 2)
                A(lambda e: e.activation(ta, bank(b), AF.Gelu_apprx_tanh), [PB(b), SC], [tk])
                V(lambda e: e.bn_stats(st_[:, 0:6], ta), [tk, SC], [sk])
                V(lambda e: e.bn_aggr(st_[:, 6:8], st_[:, 0:6]), [sk], [sk])

            def g_s2(tc, which):
                st_ = stt2[:, 16 * (tc % 2):16 * (tc % 2) + 16]
                sk = ("stt", tc % 2)
                if which == 0:
                    A(lambda e: e.activation(st_[:, 8:9], st_[:, 7:8], AF.Ln, bias=float(LN_EPS)), [sk], [sk])
                else:
                    A(lambda e: e.activation(st_[:, 9:10], st_[:, 8:9], AF.Exp, scale=-0.5), [sk], [sk])
                    V(lambda e: e.scalar_tensor_tensor(st_[:, 10:11], st_[:, 6:7], -1.0, st_[:, 9:10], ALU.mult, ALU.mult),
                      [sk], [sk])

            def g_s3(tc):
                ta = tmpA[tc % 2]
                tk = ("tmpA", tc % 2)
                st_ = stt2[:, 16 * (tc % 2):16 * (tc % 2) + 16]
                sk = ("stt", tc % 2)
                A(lambda e: e.activation(ta, ta, AF.Identity, bias=st_[:, 10:11], scale=st_[:, 9:10]), [sk, tk], [tk])
                V(lambda e: e.tensor_tensor(ta, ta, gbc, ALU.mult), [tk, "gbc"], [tk])
                V(lambda e: e.tensor_tensor(vln[:, tc, :], ta, bbc, ALU.add), [tk, "bbc"], [("vln", tc)])

            for tp in range(8):
                t0_, t1_ = 2 * tp, 2 * tp + 1
                g_s1(t0_)
                g_s1(t1_)
                g_s2(t0_, 0)
                g_s2(t1_, 0)
                g_s2(t0_, 1)
                g_s2(t1_, 1)
                g_s3(t0_)
                g_s3(t1_)
            for g in R4:
                uw, uk = load_w(wcol(1536 + 128 * g), 8)
                for tt in R4:
                    ts = slice(tt * 512, (tt + 1) * 512)
                    mm(bank(4 + tt), [(uw[:, k, :], XB[:, k, ts]) for k in R8], [uk] + XBk(R8, [tt]), [PB(4 + tt)])
                    A(lambda e, tt=tt, ts=ts: e.activation(U[:, ts], bank(4 + tt), AF.Gelu_apprx_tanh),
                      [PB(4 + tt), SC], [("U", tt)])
                for tt in R4:
                    ts = slice(tt * 512, (tt + 1) * 512)

                    def f(e, g=g, tt=tt):
                        for j in R4:
                            tc = 4 * tt + j
                            o = ps[:, 512 * tt + 128 * j:512 * tt + 128 * (j + 1)]
                            e.matmul(o, vln[:, tc, g * 128:(g + 1) * 128], sguW[:, g, :], start=True, stop=False)
                            ins = e.matmul(o, onesf[0:1, :], sgub[0:1, g * 128:(g + 1) * 128], start=False, stop=True)
                        return ins
                    T(f, [("vln", 4 * tt + j) for j in R4] + ["sguW", "sgub", "onesf"], [PB(tt)])
                    V(lambda e, g=g, tt=tt, ts=ts: e.tensor_tensor(ABT[:, 4 + g, ts], bank(tt), U[:, ts], ALU.mult),
                      [PB(tt), ("U", tt)], [("AB", 4 + g, tt)])
            return na_part(i, wcol)

        def na_part(i, wcol):
            fence()
            if "na" in SKIP:
                G(lambda e: e.memset(ABT[:, 0:4, :], 0.0), [], ABk(R4, R4))
                return w_out_phase("w_out_even", i)
            QT = _carve(scr, 0, BF16, [2048])
            KT = _carve(scr, 4096, BF16, [2048])
            Vaug = _carve(scr, 8192, BF16, [16, 256])
            EBs = [_carve(scr, 16384 + 3840 * j, F32, [15, 64]) for j in range(2)]
            etm = [_carve(scr, 24064 + 2048 * j, F32, [512]) for j in range(2)]
            NPT = 5
            SKEW = int(os.environ.get("KSKEW", "2"))
            pend = []
            PTs = [_carve(scr, 28160 + 1024 * j, BF16, [512]) for j in range(NPT)]
            rcs = [_carve(scr, 33280 + 2048 * j, F32, [512]) for j in range(2)]
            if "nomemset" not in SKIP:
                G(lambda e: e.memset(Vaug[:, :, 64:192], 1.0), [SC], ["Vones"])
            r0 = lambda r: min(max(r - 4, 0), 24)
            VON = [] if "novon" in SKIP else ["Vones"]
            it = 0
            sbctr = 0
            scale = 64 ** -0.5
            for c in ([1, 0, 3, 2] if "corder" in SKIP else R4):
                qw, qk = load_w(wcol(128 * c), 8)
                kw, kk = load_w(wcol(512 + 128 * c), 8)
                vw, vk = load_w(wcol(1024 + 128 * c), 8)
                for tt in R4:
                    ts = slice(tt * 512, (tt + 1) * 512)
                    mm(bank(2 + tt), [(qw[:, k, :], XB[:, k, ts]) for k in R8], [qk] + XBk(R8, [tt]), [PB(2 + tt)])
                    A(lambda e, tt=tt, ts=ts: e.activation(QT[:, ts], bank(2 + tt), AF.Copy), [PB(2 + tt), SC], [("QT", tt)])
                for tt in R4:
                    ts = slice(tt * 512, (tt + 1) * 512)
                    mm(bank(2 + tt), [(kw[:, k, :], XB[:, k, ts]) for k in R8], [kk] + XBk(R8, [tt]), [PB(2 + tt)])
                    V(lambda e, tt=tt, ts=ts: e.tensor_copy(KT[:, ts], bank(2 + tt)), [PB(2 + tt), SC], [("KT", tt)])
                for t4 in R4:
                    if "nov" in SKIP:
                        break
                    b = 2 + t4

                    def f(e, b=b, t4=t4, vw=vw):
                        for j in R4:
                            tc = 4 * t4 + j
                            for k in R8:
                                ins = e.matmul(ps[:, 512 * b + 128 * j:512 * b + 128 * (j + 1)],
                                               XB[:, k, tc * 128:(tc + 1) * 128], vw[:, k, :], start=(k == 0), stop=(k == 7))
                        return ins
                    T(f, [vk] + XBk(R8, [t4]), [PB(b)])
                    src = bank(b).rearrange("p (a b) -> p a b", a=4)
                    A(lambda e, t4=t4, src=src: e.activation(Vaug[:, 4 * t4:4 * t4 + 4, 0:64], src[:, :, 0:64], AF.Copy),
                      [PB(b), "Vones"], [("Va", t4, 0)])
                    A(lambda e, t4=t4, src=src: e.activation(Vaug[:, 4 * t4:4 * t4 + 4, 192:256], src[:, :, 64:128], AF.Copy),
                      [PB(b), "Vones"], [("Va", t4, 1)])
                if "na1" in SKIP:
                    G(lambda e, c=c: e.memset(ABT[:, c, :], 0.0), [], ABk([c], R4))
                    continue
                for hh in range(2):
                    h = 2 * c + hh
                    p0 = 64 * hh
                    EB = EBs[h % 2]
                    ek = ("EB", h % 2)
                    D(lambda e, EB=EB, h=h: e.dma_start(out=EB, in_=ebtab_d[i, h].rearrange("p (a b) -> p a b", a=15)),
                      ["dw", SC], [ek])
                    for qb in R4:
                        ob = qb % 2
                        OB = bank(ob)
                        valid = lambda r, kr: r0(r) <= kr <= r0(r) + 7
                        brows = list(range(8 * qb, 8 * qb + 8))
                        k_lo, k_hi = r0(8 * qb), r0(8 * qb + 7) + 7
                        pairs_m = list(range(k_lo // 2, k_hi // 2 + 1))
                        for ki, m_ in enumerate(pairs_m):
                            rows = [r for r in brows if valid(r, 2 * m_) or valid(r, 2 * m_ + 1)]
                            ra, rb = rows[0], rows[-1] + 1
                            assert rows == list(range(ra, rb))
                            n = (rb - ra) * 64
                            sbk = 2 + sbctr % 6
                            sbctr += 1
                            SBt = ps[:, 512 * sbk:512 * sbk + n]
                            mm(SBt, [(KT[p0:p0 + 64, m_ * 128:(m_ + 1) * 128], QT[p0:p0 + 64, ra * 64:rb * 64])],
                               [("KT", m_ // 4), ("QT", qb)], [PB(sbk)])
                            et = etm[it % 2]
                            etk = ("etm", it % 2)
                            pt = PTs[it % NPT]
                            ptk = ("PT", it % NPT)
                            it += 1
                            A(lambda e, et=et, SBt=SBt, n=n: e.activation(et[:, 0:n], SBt, AF.Exp, scale=scale),
                              [PB(sbk), SC], [etk])
                            t_hi = 2 * m_ - ra + 7
                            nr = rb - ra
                            assert 0 <= t_hi - nr + 1 and t_hi <= 13, (t_hi, nr)
                            ebv = EB[:, t_hi - nr + 1:t_hi + 1, :][:, ::-1, :]
                            V(lambda e, pt=pt, et=et, n=n, ebv=ebv: e.tensor_tensor(
                                pt[:, 0:n].rearrange("p (a b) -> p a b", b=64),
                                et[:, 0:n].rearrange("p (a b) -> p a b", b=64), ebv, ALU.mult),
                              [etk, ek], [ptk])
                            for r in rows:
                                for half in range(2):
                                    if not valid(r, 2 * m_ + half):
                                        a_ = r - ra
                                        G(lambda e, pt=pt, half=half, a_=a_: e.memset(
                                            pt[64 * half:64 * half + 64, a_ * 64:(a_ + 1) * 64], 0.0), [ptk], [ptk])

                            def st2(ob=ob, ra=ra, rb=rb, qb=qb, m_=m_, hh=hh, pt=pt, n=n, ki=ki, nkr=len(pairs_m), ptk=ptk):
                                T(lambda e: e.matmul(
                                    ps[:, 512 * ob + (ra - 8 * qb) * 64:512 * ob + (rb - 8 * qb) * 64],
                                    Vaug[:, m_, 128 * hh:128 * hh + 128], pt[:, 0:n],
                                    start=(ki == 0), stop=(ki == nkr - 1)),
                                  [ptk, ("Va", m_ // 4, hh), "Vones"], [PB(ob)])
                            pend.append(st2)
                            while len(pend) > SKEW:
                                pend.pop(0)()

                        def st3(ob=ob, OB=OB, p0=p0, c=c, qb=qb):
                            rc = rcs[ob]
                            rk = ("rc", ob)
                            dn = 64 - p0
                            A(lambda e: e.activation(rc[p0:p0 + 64, :], OB[dn:dn + 64, :], AF.Ln), [PB(ob), SC], [rk])
                            A(lambda e: e.activation(rc[p0:p0 + 64, :], rc[p0:p0 + 64, :], AF.Exp, scale=-1.0), [rk], [rk])
                            V(lambda e: e.tensor_tensor(
                                ABT[p0:p0 + 64, c, qb * 512:(qb + 1) * 512], OB[p0:p0 + 64, :], rc[p0:p0 + 64, :], ALU.mult),
                              [PB(ob), rk], [("AB", c, qb)])
                        pend.append(st3)
                while pend:
                    pend.pop(0)()
            w_out_phase("w_out_even", i)

        def odd_mixer(i):
            win = wbf["w_in_odd"]
            wcol = lambda c0: win[i, :, c0:c0 + 128].rearrange("(k p) j -> p k j", p=128)
            fence()
            cqn = _carve(scr, 0, BF16, [2, 2048])
            ckvn = _carve(scr, 8192, BF16, [2048])
            kr_t = _carve(scr, 12288, BF16, [2048])
            sq3 = _carve(scr, 16384, BF16, [3, 512])
            r1 = _carve(scr, 19456, F32, [512])
            r2 = _carve(scr, 21504, F32, [512])
            sgs = [_carve(scr, 23552 + 2048 * j, F32, [512]) for j in range(2)]
            cos_s = _carve(scr, 27648, F32, [512])
            sin_s = _carve(scr, 29696, F32, [512])
            t1 = _carve(scr, 31744, F32, [512])
            t2 = _carve(scr, 33792, F32, [512])
            Dpad = _carve(ABr, 0, BF16, [4, 2078])
            DPK = [("AB", c, t) for c in range(5) for t in R4]
            wq0, k0 = load_w(wcol(0), 8)
            wq1, k1 = load_w(wcol(128), 8)
            wkv, k2 = load_w(wcol(256), 8)
            for tt in R4:
                ts = slice(tt * 512, (tt + 1) * 512)
                xk = XBk(R8, [tt])
                for j, (wt, wk) in enumerate(((wq0, k0), (wq1, k1), (wkv, k2))):
                    mm(bank(j), [(wt[:, k, :], XB[:, k, ts]) for k in R8], [wk] + xk, [PB(j)])
                A(lambda e: e.activation(sq3, ps[:, 0:1536].rearrange("p (a b) -> p a b", a=3), AF.Square),
                  [PB(0), PB(1), PB(2), SC], ["sq3"])
                mm(bank(3), [(onesb[:], sq3[:, 0, :]), (onesb[:], sq3[:, 1, :])], ["sq3", "onesb"], [PB(3)])
                mm(bank(4), [(onesb[:], sq3[:, 2, :])], ["sq3", "onesb"], [PB(4)])
                A(lambda e: e.activation(r1, bank(3), AF.Ln, bias=float(RMS_EPS), scale=1.0 / 256), [PB(3), SC], ["r1"])
                A(lambda e: e.activation(r1, r1, AF.Exp, scale=-0.5), ["r1"], ["r1"])
                A(lambda e: e.activation(r2, bank(4), AF.Ln, bias=float(RMS_EPS), scale=1.0 / 128), [PB(4), SC], ["r2"])
                A(lambda e: e.activation(r2, r2, AF.Exp, scale=-0.5), ["r2"], ["r2"])
                for j in range(2):
                    V(lambda e, j=j, ts=ts: e.scalar_tensor_tensor(cqn[:, j, ts], bank(j), pc(("qg", i), 1, j), r1,
                                                                   ALU.mult, ALU.mult),
                      [PB(j), "r1", "par"], [("cqn", tt)])
                V(lambda e, ts=ts: e.scalar_tensor_tensor(ckvn[:, ts], bank(2), pc(("kvg", i), 1, 0), r2, ALU.mult, ALU.mult),
                  [PB(2), "r2", "par"], [("ckvn", tt)])
            wA, kA = load_w(wcol(320), 8)
            wB, kB = load_w(wcol(320), 8, extra=[
                (64, 80, win[i, :, 400:416].rearrange("(k p) j -> p k j", p=128)),
                (80, 96, win[i, :, 384:400].rearrange("(k p) j -> p k j", p=128))])
            for tt in R4:
                ts = slice(tt * 512, (tt + 1) * 512)
                xk = XBk(R8, [tt])
                D(lambda e, ts=ts: e.dma_start(out=cos_s[64:96, :], in_=cost_d[64:96, ts]), [SC], ["cos_s"])
                D(lambda e, ts=ts: e.dma_start(out=sin_s[64:96, :], in_=sint_d[64:96, ts]), [SC], ["sin_s"])
                mm(bank(5), [(wA[:, k, :], XB[:, k, ts]) for k in R8], [kA] + xk, [PB(5)])
                mm(bank(6), [(wB[:, k, :], XB[:, k, ts]) for k in R8], [kB] + xk, [PB(6)])
                V(lambda e: e.tensor_tensor(t1[64:96, :], ps[64:96, 512 * 5:512 * 6], cos_s[64:96, :], ALU.mult),
                  [PB(5), "cos_s", SC], ["t1"])
                V(lambda e: e.tensor_tensor(t2[64:96, :], ps[64:96, 512 * 6:512 * 7], sin_s[64:96, :], ALU.mult),
                  [PB(6), "sin_s", SC], ["t2"])
                G(lambda e, ts=ts: e.tensor_tensor(kr_t[64:96, ts], t1[64:96, :], t2[64:96, :], ALU.add),
                  ["t1", "t2"], [("kr", tt)])
            G(lambda e: e.memset(Dpad[:, :, 0:15], 0.0), DPK, DPK)
            G(lambda e: e.memset(Dpad[:, :, 2063:2078], 0.0), DPK, DPK)
            it = 0
            for cc in R4:
                wa, ka = load_w(wcol(416 + 128 * cc), 8)
                wg, kg = load_w(wcol(928 + 128 * cc), 8)
                for tt in R4:
                    ts = slice(tt * 512, (tt + 1) * 512)
                    xk = XBk(R8, [tt])
                    ba, bg = (it % 2) * 2, (it % 2) * 2 + 1
                    sg = sgs[it % 2]
                    sk = ("sg", it % 2)
                    it += 1
                    mm(bank(ba), [(wa[:, k, :], XB[:, k, ts]) for k in R8], [ka] + xk, [PB(ba)])
                    mm(bank(bg), [(wg[:, k, :], XB[:, k, ts]) for k in R8], [kg] + xk, [PB(bg)])
                    A(lambda e, sg=sg, bg=bg: e.activation(sg, bank(bg), AF.Sigmoid), [PB(bg), SC], [sk])
                    V(lambda e, sg=sg, ba=ba, cc=cc, tt=tt: e.tensor_tensor(
                        Dpad[:, cc, 15 + tt * 512:15 + (tt + 1) * 512], bank(ba), sg, ALU.mult),
                      [PB(ba), sk] + DPK, DPK)
            fence()
            diag = _carve(scr, 16384, BF16, [31, 128])
            sq4 = _carve(scr, 24320, BF16, [4, 512])
            m_t = _carve(scr, 28416, F32, [512])
            v_t = _carve(scr, 30464, F32, [512])
            l_t = _carve(scr, 32512, F32, [512])
            c16 = _carve(scr, 34560, BF16, [4, 512])
            Cv = _carve(XBr, 0, F32, [4, 2048])
            CVK = lambda cc, tt: [("XB", 2 * cc, tt), ("XB", 2 * cc + 1, tt)]
            CVA = [k for cc in R4 for tt in R4 for k in CVK(cc, tt)]
            bctr = 0
            diag2 = _carve(ABr, 20480, BF16, [31, 128])
            D2K = [("AB", c_, t_) for c_ in (5, 6) for t_ in R4]
            for cc in R4:
                cwv = pc(("cw", i), 31, cc * 31)
                dg = diag if cc % 2 == 0 else diag2
                dgk = ["diag"] if cc % 2 == 0 else D2K
                G(lambda e, cwv=cwv, dg=dg: e.tensor_tensor(
                    dg, identb[:].unsqueeze(1).broadcast_to([128, 31, 128]),
                    cwv.unsqueeze(2).broadcast_to([128, 31, 128]), ALU.mult),
                  ["identb", "par", SC], dgk)
                for tt in R4:
                    b = bctr % 4
                    bctr += 1
                    mm(bank(b), [(dg[:, j, :], Dpad[:, cc, tt * 512 + j:tt * 512 + j + 512]) for j in range(31)],
                       dgk + DPK, [PB(b)])
                    A(lambda e, cc=cc, tt=tt, b=b: e.activation(Cv[:, cc, tt * 512:(tt + 1) * 512], bank(b), AF.Identity,
                                                                bias=pc(("cb", i), 1, cc)),
                      [PB(b), "par"], CVA if (cc == 0 and tt == 0) else CVK(cc, tt))
            for tt in R4:
                ts = slice(tt * 512, (tt + 1) * 512)
                bA, bB = 4 + 2 * (tt % 2), 5 + 2 * (tt % 2)
                ck = [k for cc in R4 for k in CVK(cc, tt)]
                A(lambda e, ts=ts: e.activation(sq4, Cv[:, :, ts], AF.Square), ck + [SC], ["sq4"])
                G(lambda e, ts=ts: e.tensor_copy(c16, Cv[:, :, ts]), ck + [SC], ["c16"])
                mm(bank(bA), [(onesb[:], c16[:, c, :]) for c in R4], ["c16", "onesb"], [PB(bA)])
                mm(bank(bB), [(onesb[:], sq4[:, c, :]) for c in R4], ["sq4", "onesb"], [PB(bB)])
                A(lambda e, bA=bA: e.activation(m_t, bank(bA), AF.Identity, scale=1.0 / 512), [PB(bA), SC], ["m_t"])
                V(lambda e: e.tensor_tensor(v_t, m_t, m_t, ALU.mult), ["m_t", SC], ["v_t"])
                V(lambda e, bB=bB: e.scalar_tensor_tensor(v_t, bank(bB), 1.0 / 512, v_t, ALU.mult, ALU.subtract),
                  [PB(bB), "v_t"], ["v_t"])
                A(lambda e: e.activation(l_t, v_t, AF.Ln, bias=float(LN_EPS)), ["v_t", SC], ["l_t"])
                A(lambda e, bA=bA: e.activation(bank(bA), l_t, AF.Exp, scale=-0.5), ["l_t", "m_t"], [PB(bA)])
                V(lambda e, bA=bA, bB=bB: e.scalar_tensor_tensor(bank(bB), m_t, -1.0, bank(bA), ALU.mult, ALU.mult),
                  ["m_t", PB(bA), "v_t"], [PB(bB)])
                V(lambda e, ts=ts, bA=bA: e.tensor_tensor(
                    Cv[:, :, ts], Cv[:, :, ts], bank(bA).unsqueeze(1).broadcast_to([128, 4, 512]), ALU.mult),
                  ck + [PB(bA), "sq4", "c16"], ck)
                V(lambda e, ts=ts, bB=bB: e.tensor_tensor(
                    Cv[:, :, ts], Cv[:, :, ts], bank(bB).unsqueeze(1).broadcast_to([128, 4, 512]), ALU.add),
                  ck + [PB(bB)], ck)
                for cc in R4:
                    A(lambda e, cc=cc, ts=ts: e.activation(ABT[:, 4 + cc, ts], Cv[:, cc, ts], AF.Silu,
                                                           bias=pc(("clb", i), 1, cc), scale=pc(("clg", i), 1, cc)),
                      CVK(cc, tt) + ["par"] + (DPK if (cc == 0) else []), [("AB", 4 + cc, tt)])
            fence()
            QT = _carve(scr, 16384, BF16, [2048])
            KT = _carve(scr, 20480, BF16, [2048])
            VA = [_carve(scr, 24576 + 4096 * j, BF16, [16, 128]) for j in range(2)]
            PT2 = [_carve(scr, 32768 + 2048 * j, BF16, [1024]) for j in range(3)]
            pend = []
            cosF = _carve(XBr, 0, F32, [2048])
            sinF = _carve(XBr, 8192, F32, [2048])
            rcs = [_carve(XBr, 16384 + 2048 * j, F32, [512]) for j in range(2)]
            u1 = _carve(XBr, 20480, F32, [512])
            u2 = _carve(XBr, 22528, F32, [512])
            XBall = XBk(R8, R4)
            D(lambda e: e.dma_start(out=cosF[64:96, :], in_=cost_d[64:96, :]), XBall, XBall)
            D(lambda e: e.dma_start(out=sinF[64:96, :], in_=sint_d[64:96, :]), XBall, XBall)
            TBL = [("XB", 0, 0)]
            G(lambda e: e.memset(VA[0][:, :, 64:128], 1.0), [SC], ["VAones0"])
            G(lambda e: e.memset(VA[1][:, :, 0:64], 1.0), [SC], ["VAones1"])
            scale = 96 ** -0.5
            it = 0
            sbctr = 0
            QTs = [QT, _carve(XBr, 24576, BF16, [2048])]
            KTs = [KT, _carve(XBr, 28672, BF16, [2048])]
            wkv_of = {}

            def proj(h):
                QT, KT = QTs[h % 2], KTs[h % 2]
                hb_ = h % 2
                hh = h % 2
                p0 = 64 * hh
                c = h // 2
                uq = wbf["w_uq"]
                wa, ka = load_w(uq[i, :, 96 * h:96 * h + 96].rearrange("(k p) j -> p k j", p=128), 2, ncol=96)
                wb_, kb = load_w(uq[i, :, 96 * h:96 * h + 96].rearrange("(k p) j -> p k j", p=128), 2, ncol=96, extra=[
                    (64, 80, uq[i, :, 96 * h + 80:96 * h + 96].rearrange("(k p) j -> p k j", p=128)),
                    (80, 96, uq[i, :, 96 * h + 64:96 * h + 80].rearrange("(k p) j -> p k j", p=128))])
                wkv_, kkv = load_w(wbf["w_ukv"][i, :, 128 * h:128 * h + 128].rearrange("(k p) j -> p k j", p=128), 1)
                for tt in R4:
                    ts = slice(tt * 512, (tt + 1) * 512)
                    mm(ps[0:96, 512 * 2:512 * 3], [(wa[:, k, :], cqn[:, k, ts]) for k in range(2)], [ka, ("cqn", tt)], [PB(2)])
                    mm(ps[0:96, 512 * 3:512 * 4], [(wb_[:, k, :], cqn[:, k, ts]) for k in range(2)], [kb, ("cqn", tt)], [PB(3)])
                    mm(ps[0:64, 512 * 4:512 * 5], [(wkv_[:, 0, 0:64], ckvn[:, ts])], [kkv, ("ckvn", tt)], [PB(4)])
                    V(lambda e, ts=ts: e.tensor_copy(QT[0:64, ts], ps[0:64, 1024:1536]), [PB(2), SC] + TBL, [("QTm", hb_, tt, 0)])
                    V(lambda e, ts=ts: e.tensor_tensor(u1[64:96, :], ps[64:96, 1024:1536], cosF[64:96, ts], ALU.mult),
                      [PB(2)] + TBL, ["u1"])
                    V(lambda e, ts=ts: e.tensor_tensor(u2[64:96, :], ps[64:96, 1536:2048], sinF[64:96, ts], ALU.mult),
                      [PB(3)] + TBL, ["u2"])
                    G(lambda e, ts=ts: e.tensor_tensor(QT[64:96, ts], u1[64:96, :], u2[64:96, :], ALU.add),
                      ["u1", "u2", SC] + TBL, [("QTm", hb_, tt, 1)])
                    V(lambda e, ts=ts: e.tensor_copy(KT[0:64, ts], ps[0:64, 2048:2560]), [PB(4), SC] + TBL, [("KTm", hb_, tt, 0)])
                    G(lambda e, ts=ts: e.tensor_copy(KT[64:96, ts], kr_t[64:96, ts]), [("kr", tt), SC] + TBL, [("KTm", hb_, tt, 1)])
                va = VA[hh]
                for t8 in range(2):
                    b = 5

                    def f(e, t8=t8, b=b, wkv_=wkv_):
                        for j in range(8):
                            tc = 8 * t8 + j
                            ins = e.matmul(ps[:, 512 * b + 64 * j:512 * b + 64 * (j + 1)], ckvn[:, tc * 128:(tc + 1) * 128],
                                           wkv_[:, 0, 64:128], start=True, stop=True)
                        return ins
                    T(f, [kkv, ("ckvn", 2 * t8), ("ckvn", 2 * t8 + 1)], [PB(b)])
                    src = bank(b).rearrange("p (a b) -> p a b", a=8)
                    V(lambda e, va=va, t8=t8, src=src, hh=hh: e.tensor_copy(
                        va[:, 8 * t8:8 * t8 + 8, 64 * hh:64 * hh + 64], src),
                      [PB(b), SC, "VAones%d" % hh], [("VA", hh, t8)])

            def attn(h):
                nonlocal it, sbctr
                QT, KT = QTs[h % 2], KTs[h % 2]
                hb_ = h % 2
                hh = h % 2
                p0 = 64 * hh
                c = h // 2
                va = VA[hh]
                for tt in R4:
                    ts = slice(tt * 512, (tt + 1) * 512)
                    ob = tt % 2
                    OB = bank(ob)
                    for j2 in range(8):
                        sb0 = 2 + 2 * (sbctr % 3)
                        sbctr += 1
                        for u in range(2):
                            kc = 2 * j2 + u
                            mm(bank(sb0 + u), [(KT[0:96, kc * 128:(kc + 1) * 128], QT[0:96, ts])],
                               [("KTm", hb_, kc // 4, 0), ("KTm", hb_, kc // 4, 1), ("QTm", hb_, tt, 0), ("QTm", hb_, tt, 1)], [PB(sb0 + u)])
                        pt = PT2[it % 3]
                        ptk = ("PT", it % 3)
                        it += 1
                        A(lambda e, pt=pt, sb0=sb0: e.activation(pt, ps[:, 512 * sb0:512 * sb0 + 1024], AF.Exp, scale=scale),
                          [PB(sb0), PB(sb0 + 1), SC], [ptk])

                        def st2(OB=OB, va=va, j2=j2, pt=pt, ptk=ptk, hh=hh, ob=ob):
                            def f(e):
                                for u in range(2):
                                    kc = 2 * j2 + u
                                    ins = e.matmul(OB, va[:, kc, :], pt[:, 512 * u:512 * (u + 1)], start=(kc == 0), stop=(kc == 15))
                                return ins
                            T(f, [ptk, ("VA", hh, j2 // 4), "VAones%d" % hh], [PB(ob)])
                        pend.append(st2)
                        while len(pend) > 2:
                            pend.pop(0)()

                    def st3(ob=ob, OB=OB, p0=p0, c=c, ts=ts, tt=tt):
                        rc = rcs[ob]
                        rk = ("rc", ob)
                        dn = 64 - p0
                        A(lambda e: e.activation(rc[p0:p0 + 64, :], OB[dn:dn + 64, :], AF.Ln), [PB(ob)] + TBL, [rk])
                        A(lambda e: e.activation(rc[p0:p0 + 64, :], rc[p0:p0 + 64, :], AF.Exp, scale=-1.0), [rk], [rk])
                        V(lambda e: e.tensor_tensor(ABT[p0:p0 + 64, c, ts], OB[p0:p0 + 64, :], rc[p0:p0 + 64, :], ALU.mult),
                          [PB(ob), rk], [("AB", c, tt)])
                    pend.append(st3)
                while pend:
                    pend.pop(0)()

            proj(0)
            for h in R8:
                if h + 1 < 8:
                    proj(h + 1)
                attn(h)
            G(lambda e: e.memset(dumt[:, 1:2], 0.0), ["u1", "u2", ("rc", 0), ("rc", 1)] + TBL,
              XBall + [(nm, 1, t_, p_) for nm in ("QTm", "KTm") for t_ in R4 for p_ in range(2)])
            w_out_phase("w_out_odd", i)

        PLAN = os.environ.get("KPLAN", "")
        if PLAN:
            for j, tok in enumerate(PLAN.split(",")):
                S.epoch = 1 + j // 2
                if tok[0] == "L":
                    load_x(int(tok[1]), j == 0)
                else:
                    store_y(int(tok[1]))
            nseq = 0
        for s in range(nseq):
            S.epoch = (s + 1) if 'noepoch' not in SKIP else 0
            load_x(s, s == 0)
            for l in range(nlayers):
                if l % 2 == 0:
                    even_mixer(l // 2)
                else:
                    odd_mixer(l // 2)
                ln_phase("ln1g", "ln1b", l)
                if do_ffn:
                    ffn_phase(l)
                    ln_phase("ln2g", "ln2b", l)
            store_y(s)
        S.emit()
    return nc


_NC_CACHE = {}


def _get_nc(nseq, nlayers=DEPTH, do_ffn=True):
    k = (nseq, nlayers, do_ffn)
    if k not in _NC_CACHE:
        _NC_CACHE[k] = build_nc(nseq, nlayers, do_ffn)
    return _NC_CACHE[k]


def _common_inputs(inp):
    c = _consts()
    d = {n: np.ascontiguousarray(inp[n], dtype=np.float32) for n, _ in WSPECS if n != "sgu_wT"}
    d["sgu_wT"] = np.ascontiguousarray(np.transpose(inp["sgu_w"], (0, 1, 3, 2)), dtype=np.float32)
    rp = np.zeros((2, 8, 15, 127), np.float32)
    rp[..., 48:79] = inp["rpb"]
    d["rpb_pad"] = rp
    d["sgu_ln_g"] = np.ascontiguousarray(inp["sgu_ln_g"], dtype=np.float32)
    d["sgu_ln_b"] = np.ascontiguousarray(inp["sgu_ln_b"], dtype=np.float32)
    d["sgu_b"] = np.ascontiguousarray(np.asarray(inp["sgu_b"], dtype=np.float32).reshape(2, 512))
    d["params"] = _pack_params(inp)
    d.update(c)
    return d


def kernel(**inputs):
    inp = {k: np.asarray(v) for k, v in inputs.items()}
    xs = np.concatenate([inp["x_prompt"], inp["x_sample"]], axis=0).astype(np.float32, copy=False)
    nseq = xs.shape[0] // N_CORES
    nc = _get_nc(nseq)
    common = _common_inputs(inp)
    in_maps = []
    for c in range(N_CORES):
        m = dict(common)
        m["x"] = np.ascontiguousarray(xs[c * nseq:(c + 1) * nseq])
        in_maps.append(m)
    res = run_bass_kernel_spmd(nc, in_maps, core_ids=list(range(N_CORES)))
    ys = np.concatenate([r["y"] for r in res.results], axis=0)
    nb = inp["x_prompt"].shape[0]
    return (np.ascontiguousarray(ys[:nb]), np.ascontiguousarray(ys[nb:]))
```
